# Optimizing a Trainium2 kernel written in Bass

```python
import math
import jax, jax.numpy as jnp
from jax import lax
import numpy as np

D_MODEL = 1024
BATCH = 4
SEQ = 4096
DEPTH = 4
DEC_BATCH = 16
DEC_SEQ = 2048
PAST_LEN = 128

D_A = D_MODEL // 2
CONV_A_WIDTH = 31
SSD_HEADS = 16
SSD_HEAD_DIM = 64
D_SSD = SSD_HEADS * SSD_HEAD_DIM
SSD_GROUPS = 2
SSD_STATE = 128
SSD_CONV = 5
CHUNK = 128
XBC_DIM = D_SSD + 2 * SSD_GROUPS * SSD_STATE
FOURIER_GROUPS = 4
FOURIER_GROUP_DIM = 128
D_C = FOURIER_GROUPS * FOURIER_GROUP_DIM
N_BRANCH = 3
D_FF = -(-8 * D_MODEL // (3 * 256)) * 256
IN_SIZES = (2 * D_A, D_SSD, XBC_DIM, 2 * SSD_HEADS, D_C, N_BRANCH * D_MODEL)
IN_COLS = sum(IN_SIZES)
EPS = 1e-6

kernel_name = "hybrid_conv_ssd_fnet_encoder"


def _split(t, sizes):
    offs = []
    s = 0
    for n in sizes[:-1]:
        s += n
        offs.append(s)
    return jnp.split(t, offs, axis=-1)


def _rmsnorm(x, g, out_dtype=None):
    x32 = x.astype(jnp.float32)
    y = x32 * lax.rsqrt(jnp.mean(x32 * x32, axis=-1, keepdims=True) + EPS) * g.astype(jnp.float32)
    return y.astype(out_dtype if out_dtype is not None else x.dtype)


def _layernorm(x, g, b):
    x32 = x.astype(jnp.float32)
    mu = jnp.mean(x32, axis=-1, keepdims=True)
    xc = x32 - mu
    y = xc * lax.rsqrt(jnp.mean(xc * xc, axis=-1, keepdims=True) + EPS)
    return (y * g.astype(jnp.float32) + b.astype(jnp.float32)).astype(x.dtype)


def _depthwise_conv(u, w, b):
    k = w.shape[0]
    out = lax.conv_general_dilated(
        u, w[:, None, :].astype(u.dtype), window_strides=(1,),
        padding=[(k // 2, k // 2)], dimension_numbers=("NWC", "WIO", "NWC"),
        feature_group_count=u.shape[-1])
    return out + b.astype(u.dtype)


def _ssd_chunked(xh, dt, a_coef, bm, cm):
    b, l, _, _ = xh.shape
    nc = l // CHUNK
    kk = SSD_HEADS // SSD_GROUPS
    f32 = jnp.float32
    x = (xh.astype(f32) * dt[..., None]).reshape(b, nc, CHUNK, SSD_GROUPS, kk, SSD_HEAD_DIM)
    a = (dt * a_coef.astype(f32)).reshape(b, nc, CHUNK, SSD_GROUPS, kk)
    a = jnp.transpose(a, (0, 1, 3, 4, 2))
    bc = bm.astype(f32).reshape(b, nc, CHUNK, SSD_GROUPS, SSD_STATE)
    cc = cm.astype(f32).reshape(b, nc, CHUNK, SSD_GROUPS, SSD_STATE)
    a_cs = jnp.cumsum(a, axis=-1)
    seg = a_cs[..., :, None] - a_cs[..., None, :]
    causal = jnp.tril(jnp.ones((CHUNK, CHUNK), dtype=bool))
    lmat = jnp.exp(jnp.where(causal, seg, -jnp.inf))
    scores = jnp.einsum("bclgn,bcsgn->bcgls", cc, bc)
    y_diag = jnp.einsum("bcgls,bcgkls,bcsgkp->bclgkp", scores, lmat, x)
    decay_to_end = jnp.exp(a_cs[..., -1:] - a_cs)
    states = jnp.einsum("bcsgn,bcgks,bcsgkp->bcgkpn", bc, decay_to_end, x)
    chunk_decay = jnp.exp(a_cs[..., -1])

    def step(h, inp):
        s, d = inp
        return h * d[..., None, None] + s, h

    h0 = jnp.zeros((b, SSD_GROUPS, kk, SSD_HEAD_DIM, SSD_STATE), f32)
    _, h_in = lax.scan(step, h0, (jnp.moveaxis(states, 1, 0), jnp.moveaxis(chunk_decay, 1, 0)))
    h_in = jnp.moveaxis(h_in, 0, 1)
    y_off = jnp.einsum("bclgn,bcgkpn,bcgkl->bclgkp", cc, h_in, jnp.exp(a_cs))
    return (y_diag + y_off).reshape(b, l, SSD_HEADS, SSD_HEAD_DIM)


def _fourier_mix(u):
    b, l, _ = u.shape
    ug = u.astype(jnp.float32).reshape(b, l, FOURIER_GROUPS, FOURIER_GROUP_DIM)
    f = jnp.fft.fft2(ug, axes=(1, 3), norm="ortho").real
    return f.reshape(b, l, D_C).astype(u.dtype)


def _mixer(h, w_in, conv_a_w, conv_a_b, ln_a_g, ln_a_b, w_a_out,
           conv_s_w, conv_s_b, dt_bias_f, dt_bias_b, a_log_f, a_log_b, d_skip, g_ssd, w_b_out,
           w_c_out, w_out):
    b, l, _ = h.shape
    proj = h @ w_in
    a_in, z, xbc, dt_raw, u_c, gate_raw = _split(proj, IN_SIZES)

    a_val, a_gate = _split(a_in, (D_A, D_A))
    a = a_val * jax.nn.sigmoid(a_gate)
    a = _depthwise_conv(a, conv_a_w, conv_a_b)
    a = jax.nn.silu(_layernorm(a, ln_a_g, ln_a_b))
    o_a = a @ w_a_out

    xbc = jax.nn.silu(_depthwise_conv(xbc, conv_s_w, conv_s_b))
    xs, bm, cm = _split(xbc, (D_SSD, SSD_GROUPS * SSD_STATE, SSD_GROUPS * SSD_STATE))
    xh = xs.reshape(b, l, SSD_HEADS, SSD_HEAD_DIM)
    bm = bm.reshape(b, l, SSD_GROUPS, SSD_STATE)
    cm = cm.reshape(b, l, SSD_GROUPS, SSD_STATE)
    dt_f_raw, dt_b_raw = _split(dt_raw.astype(jnp.float32), (SSD_HEADS, SSD_HEADS))
    dt_f = jax.nn.softplus(dt_f_raw + dt_bias_f.astype(jnp.float32))
    dt_b = jax.nn.softplus(dt_b_raw + dt_bias_b.astype(jnp.float32))
    a_f = -jnp.exp(a_log_f.astype(jnp.float32))
    a_b = -jnp.exp(a_log_b.astype(jnp.float32))
    y_f = _ssd_chunked(xh, dt_f, a_f, bm, cm)
    flip = lambda t: jnp.flip(t, axis=1)
    y_b = flip(_ssd_chunked(flip(xh), flip(dt_b), a_b, flip(bm), flip(cm)))
    y = y_f + y_b + d_skip.astype(jnp.float32)[:, None] * xh.astype(jnp.float32)
    y = y.reshape(b, l, D_SSD) * jax.nn.silu(z.astype(jnp.float32))
    y = _rmsnorm(y, g_ssd, out_dtype=h.dtype)
    o_b = y @ w_b_out

    o_c = _fourier_mix(u_c) @ w_c_out

    g_a, g_b, g_c = _split(jax.nn.sigmoid(gate_raw), (D_MODEL, D_MODEL, D_MODEL))
    merged = g_a * o_a + g_b * o_b + g_c * o_c
    return merged @ w_out


def _layer(x, c, w_ada, b_ada, g_pre_mix, g_post_mix, g_pre_ffn, g_post_ffn,
           w_in, conv_a_w, conv_a_b, ln_a_g, ln_a_b, w_a_out,
           conv_s_w, conv_s_b, dt_bias_f, dt_bias_b, a_log_f, a_log_b, d_skip, g_ssd, w_b_out,
           w_c_out, w_out, w_ffn_in, w_ffn_out):
    mod = jax.nn.silu(c) @ w_ada + b_ada
    sh1, sc1, gt1, sh2, sc2, gt2 = [m[:, None, :] for m in _split(mod, (D_MODEL,) * 6)]
    h = _rmsnorm(x, g_pre_mix) * (1 + sc1) + sh1
    m = _mixer(h, w_in, conv_a_w, conv_a_b, ln_a_g, ln_a_b, w_a_out,
               conv_s_w, conv_s_b, dt_bias_f, dt_bias_b, a_log_f, a_log_b, d_skip, g_ssd, w_b_out,
               w_c_out, w_out)
    x = x + gt1 * _rmsnorm(m, g_post_mix)
    h = _rmsnorm(x, g_pre_ffn) * (1 + sc2) + sh2
    gu = h @ w_ffn_in
    f = (jax.nn.silu(gu[..., :D_FF]) * gu[..., D_FF:]) @ w_ffn_out
    x = x + gt2 * _rmsnorm(f, g_post_ffn)
    return x


def setup_inputs(seed: int = 0) -> dict:
    key = jax.random.key(seed)
    ks = jax.random.split(key, 32)
    f32 = jnp.float32
    nrm = lambda k, shape, scale: jax.random.normal(k, shape, f32) * scale
    gain = lambda k, shape: 1.0 + 0.02 * jax.random.normal(k, shape, f32)
    dt0 = jnp.exp(jax.random.uniform(ks[16], (2, DEPTH, SSD_HEADS), f32, math.log(1e-3), math.log(1e-1)))
    dt_bias = dt0 + jnp.log(-jnp.expm1(-dt0))
    a_log = jnp.log(jax.random.uniform(ks[17], (2, DEPTH, SSD_HEADS), f32, 1.0, 16.0))
    return {
        "x_prompt": jax.random.normal(ks[0], (BATCH, SEQ, D_MODEL), f32),
        "x_sample": jax.random.normal(ks[1], (DEC_BATCH, DEC_SEQ, D_MODEL), f32),
        "c_prompt": jax.random.normal(ks[2], (BATCH, D_MODEL), f32),
        "c_sample": jax.random.normal(ks[3], (DEC_BATCH, D_MODEL), f32),
        "w_ada": nrm(ks[4], (DEPTH, D_MODEL, 6 * D_MODEL), 0.5 * D_MODEL ** -0.5),
        "b_ada": nrm(ks[5], (DEPTH, 6 * D_MODEL), 0.02),
        "g_pre_mix": gain(ks[6], (DEPTH, D_MODEL)),
        "g_post_mix": gain(ks[7], (DEPTH, D_MODEL)),
        "g_pre_ffn": gain(ks[8], (DEPTH, D_MODEL)),
        "g_post_ffn": gain(ks[9], (DEPTH, D_MODEL)),
        "w_in": nrm(ks[10], (DEPTH, D_MODEL, IN_COLS), D_MODEL ** -0.5),
        "conv_a_w": nrm(ks[11], (DEPTH, CONV_A_WIDTH, D_A), CONV_A_WIDTH ** -0.5),
        "conv_a_b": nrm(ks[12], (DEPTH, D_A), 0.02),
        "ln_a_g": gain(ks[13], (DEPTH, D_A)),
        "ln_a_b": nrm(ks[14], (DEPTH, D_A), 0.02),
        "w_a_out": nrm(ks[15], (DEPTH, D_A, D_MODEL), D_A ** -0.5),
        "conv_s_w": nrm(ks[18], (DEPTH, SSD_CONV, XBC_DIM), SSD_CONV ** -0.5),
        "conv_s_b": nrm(ks[19], (DEPTH, XBC_DIM), 0.02),
        "dt_bias_f": dt_bias[0],
        "dt_bias_b": dt_bias[1],
        "a_log_f": a_log[0],
        "a_log_b": a_log[1],
        "d_skip": gain(ks[20], (DEPTH, SSD_HEADS)),
        "g_ssd": gain(ks[21], (DEPTH, D_SSD)),
        "w_b_out": nrm(ks[22], (DEPTH, D_SSD, D_MODEL), D_SSD ** -0.5),
        "w_c_out": nrm(ks[23], (DEPTH, D_C, D_MODEL), D_C ** -0.5),
        "w_out": nrm(ks[24], (DEPTH, D_MODEL, D_MODEL), D_MODEL ** -0.5),
        "w_ffn_in": nrm(ks[25], (DEPTH, D_MODEL, 2 * D_FF), D_MODEL ** -0.5),
        "w_ffn_out": nrm(ks[26], (DEPTH, D_FF, D_MODEL), D_FF ** -0.5),
    }


def reference(x_prompt, x_sample, c_prompt, c_sample, w_ada, b_ada, g_pre_mix, g_post_mix,
              g_pre_ffn, g_post_ffn, w_in, conv_a_w, conv_a_b, ln_a_g, ln_a_b, w_a_out,
              conv_s_w, conv_s_b, dt_bias_f, dt_bias_b, a_log_f, a_log_b, d_skip, g_ssd, w_b_out,
              w_c_out, w_out, w_ffn_in, w_ffn_out):
    y_prompt = x_prompt
    y_sample = x_sample
    for i in range(DEPTH):
        lp = (w_ada[i], b_ada[i], g_pre_mix[i], g_post_mix[i], g_pre_ffn[i], g_post_ffn[i],
              w_in[i], conv_a_w[i], conv_a_b[i], ln_a_g[i], ln_a_b[i], w_a_out[i],
              conv_s_w[i], conv_s_b[i], dt_bias_f[i], dt_bias_b[i], a_log_f[i], a_log_b[i],
              d_skip[i], g_ssd[i], w_b_out[i], w_c_out[i], w_out[i], w_ffn_in[i], w_ffn_out[i])
        y_prompt = _layer(y_prompt, c_prompt, *lp)
        y_sample = _layer(y_sample, c_sample, *lp)
    return (y_prompt, y_sample)
```

```python
import contextlib
import numpy as np
import ml_dtypes
import concourse.bass as bass
import concourse.mybir as mybir
from concourse.bass_utils import run_bass_kernel_spmd

F32 = mybir.dt.float32
BF16 = mybir.dt.bfloat16
AF = mybir.ActivationFunctionType
ALU = mybir.AluOpType

ENGINES = ("tensor", "scalar", "vector", "gpsimd", "sync")
COMPUTE = ("tensor", "scalar", "vector", "gpsimd")

D = 1024
T = 6144
SEG = 2048
NT = 12
NSUB = 48
DEPTH = 4
D_FF = 2816
IN_COLS = 7200
EPS = 1e-6
O_AV, O_AG, O_Z, O_XBC, O_DT, O_UC, O_GATE = 0, 512, 1024, 2048, 3584, 3616, 4128


class Res:
    __slots__ = ("name", "last_writer", "readers", "sem", "issued")

    def __init__(self, name):
        self.name = name
        self.last_writer = None
        self.readers = []
        self.sem = None
        self.issued = 0


class Op:
    __slots__ = ("eng", "fn", "reads", "writes", "dma", "semres", "deps", "signal", "token", "waits", "bar")

    def __init__(self, eng, fn, reads, writes, dma, semres, bar=False):
        self.eng = eng
        self.fn = fn
        self.reads = reads
        self.writes = writes
        self.dma = dma
        self.semres = semres
        self.deps = ()
        self.signal = False
        self.token = None
        self.waits = ()
        self.bar = bar


class _Rec:
    __slots__ = ("call",)

    def __init__(self):
        self.call = None

    def __getattr__(self, name):
        def f(*a, **k):
            self.call = (name, a, k)
            return None
        return f


class Prog:
    def __init__(self, nc):
        self.nc = nc
        self.ops = []
        self.stack = contextlib.ExitStack()
        self.all_res = []

    def sb(self, name, shape, dtype):
        return self.stack.enter_context(self.nc.sbuf_tensor(name, list(shape), dtype))

    def ps(self, name, shape, dtype=F32):
        return self.stack.enter_context(self.nc.psum_tensor(name, list(shape), dtype))

    def res(self, name=None):
        r = Res(name or f"r{len(self.all_res)}")
        self.all_res.append(r)
        return r

    def op(self, eng, fn, reads=(), writes=()):
        rec = _Rec()
        fn(rec)
        assert rec.call is not None
        self.ops.append(Op(eng, rec.call, tuple(reads), tuple(writes), False, None))

    def dma(self, eng, fn, reads=(), writes=(), semres=None):
        assert semres is not None
        rec = _Rec()
        fn(rec)
        assert rec.call is not None
        self.ops.append(Op(eng, rec.call, tuple(reads), tuple(writes), True, semres))

    def barrier(self):
        for e in ENGINES:
            self.ops.append(Op(e, None, (), (), False, None, bar=True))

    def finalize(self):
        nc = self.nc
        ops = self.ops
        last_on = {e: None for e in COMPUTE}
        i = 0
        n = len(ops)
        while i < n:
            o = ops[i]
            if o.bar:
                for e in COMPUTE:
                    if last_on[e] is not None:
                        ops[last_on[e]].signal = True
                for r in self.all_res:
                    r.last_writer = None
                    r.readers = []
                while i < n and ops[i].bar:
                    i += 1
                continue
            deps = set()
            for r in o.reads:
                if r.last_writer is not None:
                    deps.add(r.last_writer)
            for w in o.writes:
                if w.last_writer is not None:
                    deps.add(w.last_writer)
                for rd in w.readers:
                    deps.add(rd)
            deps.discard(i)
            dl = []
            for j in deps:
                oj = ops[j]
                if oj.eng == o.eng and not oj.dma and not o.dma:
                    if o.eng == "tensor":
                        continue
                    if not any((r.last_writer == j) for r in o.reads):
                        continue
                dl.append(j)
            o.deps = dl
            for j in dl:
                ops[j].signal = True
            for r in o.reads:
                r.readers.append(i)
            for w in o.writes:
                w.last_writer = i
                w.readers = []
            if not o.dma and o.eng in last_on:
                last_on[o.eng] = i
            i += 1
        for e in COMPUTE:
            if last_on[e] is not None:
                ops[last_on[e]].signal = True
        sems = {e: None for e in COMPUTE}
        counts = {e: 0 for e in COMPUTE}
        active = []
        free_sems = []
        sem_final = {}
        bar_snap = {}
        for idx, o in enumerate(ops):
            if o.bar:
                if idx not in bar_snap:
                    snap = [("e", e, sems[e], counts[e]) for e in COMPUTE if counts[e] > 0]
                    snap += [("d", id(r.sem), r.sem, r.issued) for r in active]
                    for r in active:
                        free_sems.append((r.sem, r.issued))
                        r.sem = None
                    active = []
                    j = idx
                    while j < len(ops) and ops[j].bar:
                        bar_snap[j] = snap
                        j += 1
                continue
            if o.dma:
                r = o.semres
                if r.sem is None:
                    if free_sems:
                        r.sem, r.issued = free_sems.pop()
                    else:
                        r.sem = nc.alloc_semaphore(name=f"d_{r.name}")
                        r.issued = 0
                    active.append(r)
                r.issued += 16
                o.token = ("d", r.sem, r.issued)
                sem_final[id(r.sem)] = (r.sem, r.issued)
            elif o.signal:
                if sems[o.eng] is None:
                    sems[o.eng] = nc.alloc_semaphore(name=f"e_{o.eng}")
                counts[o.eng] += 1
                o.token = ("e", o.eng, counts[o.eng])
        waited = {e: {} for e in ENGINES}
        issued_sofar = {}
        for idx, o in enumerate(ops):
            need = {}
            if o.bar:
                for kind, key, semh, val in bar_snap[idx]:
                    need[(kind, key)] = (semh if kind == "d" else sems[key], val)
            else:
                for j in o.deps:
                    kind, key, val = ops[j].token
                    if kind == "d":
                        val = max(val, issued_sofar.get(id(key), 0))
                        k = ("d", id(key))
                        semh = key
                    else:
                        k = ("e", key)
                        semh = sems[key]
                    if need.get(k, (None, 0))[1] < val:
                        need[k] = (semh, val)
            w = []
            wd = waited[o.eng]
            for k, (semh, val) in need.items():
                if wd.get(k, 0) >= val:
                    continue
                wd[k] = val
                w.append((semh, val))
            o.waits = w
            if o.dma:
                issued_sofar[id(o.token[1])] = o.token[2]
        per_eng = {e: [o for o in ops if o.eng == e] for e in ENGINES}
        final = list(sem_final.values())
        final += [(sems[e], counts[e]) for e in COMPUTE if counts[e] > 0]
        self.n_ops = len(ops)
        with nc.Block() as block:
            def make(ename):
                def body(eng):
                    for o in per_eng[ename]:
                        for semh, val in o.waits:
                            eng.wait_ge(semh, val)
                        if o.fn is None:
                            continue
                        name, a, k = o.fn
                        ins = getattr(eng, name)(*a, **k)
                        if o.dma:
                            ins.then_inc(o.token[1], 16)
                        elif o.signal:
                            ins.then_inc(sems[o.eng], 1)
                    if ename == "sync":
                        for semh, val in final:
                            eng.wait_ge(semh, val)
                return body
            block.tensor(make("tensor"))
            block.scalar(make("scalar"))
            block.vector(make("vector"))
            block.gpsimd(make("gpsimd"))
            block.sync(make("sync"))
        self.stack.close()


class Arena:
    def __init__(self, ap2d, n):
        self.ap = ap2d
        self.n = n
        self.off = 0

    def reset(self):
        self.off = 0

    def take(self, *shape):
        size = int(np.prod(shape))
        assert self.off + size <= self.n, (self.off, size, self.n)
        v = self.ap[:, self.off:self.off + size]
        self.off += size
        if len(shape) == 2:
            v = v.rearrange("p (a b) -> p a b", a=shape[0])
        elif len(shape) == 3:
            v = v.rearrange("p (a b c) -> p a b c", a=shape[0], b=shape[1])
        return v


def bc(ap, shape):
    return ap.to_broadcast(list(shape))


def build_program(nl=DEPTH, debug=False, stop_after=None):
    nc = bass.Bass("TRN2", target_bir_lowering=False)

    def din(name, shape, dt=F32):
        return nc.dram_tensor(name, list(shape), dt, kind="ExternalInput").ap()

    skind = "ExternalOutput" if debug else "Internal"

    def dscr(name, shape, dt):
        return nc.dram_tensor(name, list(shape), dt, kind=skind).ap()

    xin = din("xin", [T, D])
    cT_d = din("cT", [128, 8, 3])
    flag_d = din("flag", [128, 1])
    w_ada = din("w_ada", [DEPTH, D, 6 * D])
    b_ada = din("b_ada", [DEPTH, 6 * D])
    gvec = {k: din(k, [DEPTH, D]) for k in ("g_pre_mix", "g_post_mix", "g_pre_ffn", "g_post_ffn", "g_ssd")}
    w_in = din("w_in", [DEPTH, D, IN_COLS])
    caw_d = din("caw", [DEPTH, 128, 4, 31])
    cab_d = din("cab", [DEPTH, 128, 4])
    lng_d = din("lng", [DEPTH, 128, 4])
    lnb_d = din("lnb", [DEPTH, 128, 4])
    w_a_out = din("w_a_out", [DEPTH, 512, D])
    csw_d = din("csw", [DEPTH, 128, 12, 5])
    csb_d = din("csb", [DEPTH, 128, 12])
    dtb_d = din("dtb", [DEPTH, 32])
    alog_d = din("alog", [DEPTH, 32])
    dsk_d = din("dsk", [DEPTH, 16])
    w_b_out = din("w_b_out", [DEPTH, D, D])
    w_c_out = din("w_c_out", [DEPTH, 512, D])
    w_out = din("w_out", [DEPTH, D, D])
    w_ffn_in = din("w_ffn_in", [DEPTH, D, 2 * D_FF])
    w_ffn_out = din("w_ffn_out", [DEPTH, D_FF, D])
    ident_d = din("ident", [128, 128], BF16)
    masks_d = din("masks", [128, 5, 128])
    csc_d = din("csc", [128, 256], BF16)
    tab_d = din("tab", [5, 4, 2, 128, 16, 512], BF16)
    yout = nc.dram_tensor("yout", [T, D], F32, kind="ExternalOutput").ap()

    mod_d = dscr("mod_d", [DEPTH, 3, 6 * D], F32)
    hfm_d = dscr("hfm_d", [8, 128, T], BF16)
    aglu_d = dscr("aglu_d", [4, 128, T], BF16)
    xbc_d = dscr("xbc_d", [12, 128, T], BF16)
    u_d = dscr("u_d", [4, 128, T], BF16)
    zs_d = dscr("zs_d", [T, D], BF16)
    dt_d = dscr("dt_d", [T, 32], F32)
    gates_d = dscr("gates_d", [24, 128, T], BF16)
    acv_d = dscr("acv_d", [4, 128, T], BF16)
    xs_d = dscr("xs_d", [T, D], BF16)
    bt_d = dscr("bt_d", [T, 256], BF16)
    bc_d = dscr("bc_d", [4, 128, T], BF16)
    yf_d = dscr("yf_d", [T, D], F32)
    yfm_d = dscr("yfm_d", [8, 128, T], BF16)
    f_d = dscr("f_d", [4, 128, T], BF16)
    act_d = dscr("act_d", [22, 128, T], BF16)

    P = Prog(nc)
    ABF = 73 * 1024
    AFP = 13 * 1024
    arena_bf_t = P.sb("arena_bf", [128, ABF], BF16)
    arena_f_t = P.sb("arena_f", [128, AFP], F32)
    AB = Arena(arena_bf_t[:], ABF)
    AFa = Arena(arena_f_t[:], AFP)
    ident = P.sb("ident_sb", [128, 128], BF16)
    masks = P.sb("masks_sb", [128, 5, 128], F32)
    flag = P.sb("flag_sb", [128, 1], F32)
    r_const = P.res("const")
    psum = [P.ps(f"psb{i}", [128, 512], F32) for i in range(8)]
    r_ps = [P.res(f"ps{i}") for i in range(8)]

    def fm_tile(dram, c0, c1, t0, n):
        return dram[c0:c1, :, t0:t0 + n].rearrange("c p t -> p c t")

    P.dma("sync", lambda e: e.dma_start(out=ident[:], in_=ident_d), writes=[r_const], semres=r_const)
    P.dma("sync", lambda e: e.dma_start(out=masks[:], in_=masks_d), writes=[r_const], semres=r_const)
    P.dma("sync", lambda e: e.dma_start(out=flag[:], in_=flag_d), writes=[r_const], semres=r_const)
    M_GT, M_LT, M_LE, M_GE, M_ONE = range(5)

    def new_phase():
        P.barrier()
        AB.reset()
        AFa.reset()

    def phase_mod():
        new_phase()
        cT = AFa.take(8, 3)
        r_cT = P.res("cT")
        sil = AFa.take(8, 3)
        r_sil = P.res("sil")
        P.dma("sync", lambda e: e.dma_start(out=cT, in_=cT_d), writes=[r_cT], semres=r_cT)
        P.op("scalar", lambda e: e.activation(out=sil, in_=cT, func=AF.Silu), reads=[r_cT], writes=[r_sil])
        NB = 4
        wbuf = [AFa.take(8, 256) for _ in range(NB)]
        r_w = [P.res(f"wada{i}") for i in range(NB)]
        brow = [AFa.take(256) for _ in range(2)]
        mrow = [AFa.take(256) for _ in range(2)]
        r_b = [P.res("brow0"), P.res("brow1")]
        r_m = [P.res("mrow0"), P.res("mrow1")]
        k = 0
        for l in range(nl):
            for ct in range(24):
                wb, rw = wbuf[k % NB], r_w[k % NB]
                mr, rm = mrow[k % 2], r_m[k % 2]
                br, rb = brow[k % 2], r_b[k % 2]
                pb = k % 2
                q = "sync" if k % 2 == 0 else "gpsimd"
                k += 1
                src = w_ada[l, :, ct * 256:(ct + 1) * 256].rearrange("(c p) n -> p c n", p=128)
                P.dma(q, lambda e, wb=wb, src=src: e.dma_start(out=wb, in_=src), writes=[rw], semres=rw)
                bsrc = b_ada[l:l + 1, ct * 256:(ct + 1) * 256].partition_broadcast(3)
                P.dma("sync", lambda e, bsrc=bsrc, br=br: e.dma_start(out=br[0:3, :], in_=bsrc), writes=[rb], semres=rb)
                for c in range(8):
                    P.op("tensor", lambda e, c=c, wb=wb, pb=pb: e.matmul(psum[pb][0:3, 0:256], lhsT=sil[:, c, :],
                                                                          rhs=wb[:, c, :], start=(c == 0), stop=(c == 7)),
                         reads=[r_sil, rw], writes=[r_ps[pb]])
                P.op("vector", lambda e, mr=mr, br=br, pb=pb: e.tensor_tensor(out=mr[0:3, :], in0=psum[pb][0:3, 0:256],
                                                                               in1=br[0:3, :], op=ALU.add),
                     reads=[r_ps[pb], rb], writes=[rm])
                dst = mod_d[l, :, ct * 256:(ct + 1) * 256]
                P.dma("sync", lambda e, mr=mr, dst=dst: e.dma_start(out=dst, in_=mr[0:3, :]), reads=[rm], semres=rm)

    def load_rows(l, slot, arena_tiles, spec):
        for dst, rr, kind, gname, part, tmp, rtmp in spec:
            msrc = mod_d[l, slot:slot + 1, part * D:(part + 1) * D].partition_broadcast(128)
            if kind == "shift":
                P.dma("sync", lambda e, dst=dst, msrc=msrc: e.dma_start(out=dst, in_=msrc), writes=[rr], semres=rr)
                continue
            gsrc = gvec[gname][l:l + 1, :].partition_broadcast(128)
            P.dma("sync", lambda e, dst=dst, msrc=msrc: e.dma_start(out=dst, in_=msrc), writes=[rr], semres=rr)
            P.dma("sync", lambda e, tmp=tmp, gsrc=gsrc: e.dma_start(out=tmp, in_=gsrc), writes=[rtmp], semres=rtmp)
            if kind == "scale":
                P.op("vector", lambda e, dst=dst, tmp=tmp: e.scalar_tensor_tensor(out=dst, in0=dst, scalar=1.0, in1=tmp,
                                                                                  op0=ALU.add, op1=ALU.mult),
                     reads=[rr, rtmp], writes=[rr])
            else:
                P.op("gpsimd", lambda e, dst=dst, tmp=tmp: e.tensor_tensor(out=dst, in0=dst, in1=tmp, op=ALU.mult),
                     reads=[rr, rtmp], writes=[rr])

    def load_w(dst, src, rr):
        P.dma("gpsimd", lambda e: e.dma_start(out=dst, in_=src), writes=[rr], semres=rr)

    def rstd_from_ss(ss, rstd, r_ss, r_rstd, n_feat):
        P.op("scalar", lambda e: e.activation(out=rstd, in_=ss, func=AF.Ln, scale=1.0 / n_feat, bias=EPS),
             reads=[r_ss], writes=[r_rstd])
        P.op("scalar", lambda e: e.activation(out=rstd, in_=rstd, func=AF.Exp, scale=-0.5),
             reads=[r_rstd], writes=[r_rstd])

    def load_x_tile(xsrc, t, xt, r_xt):
        tok0 = t * 512
        for j in range(4):
            q = (t % 2) * 4 + j
            P.dma("sync", lambda e, j=j, q=q: e.dma_start(out=xt[q], in_=xsrc[tok0 + j * 128: tok0 + (j + 1) * 128, :]),
                  writes=[r_xt[q]], semres=r_xt[q])

    def norm_transpose_tile(xsrc, t, xt, r_xt, junk, r_junk, ss, r_ss, rstd, r_rstd, gs, r_gs, sh, r_sh, tmp, r_tmp,
                            xh, r_xh, hfm, r_hfm, pbank):
        for j in range(4):
            q = (t % 2) * 4 + j
            P.op("scalar", lambda e, j=j, q=q: e.activation(out=junk, in_=xt[q], func=AF.Square, accum_out=ss[:, j:j + 1]),
                 reads=[r_xt[q]], writes=[r_junk, r_ss])
        rstd_from_ss(ss, rstd, r_ss, r_rstd, D)
        yield
        for j in range(4):
            q = (t % 2) * 4 + j
            P.op("vector", lambda e, j=j, q=q: e.scalar_tensor_tensor(out=tmp, in0=xt[q], scalar=rstd[:, j:j + 1], in1=gs,
                                                                      op0=ALU.mult, op1=ALU.mult),
                 reads=[r_xt[q], r_rstd, r_gs], writes=[r_tmp])
            P.op("vector", lambda e, j=j: e.tensor_tensor(out=xh[j % 2], in0=tmp, in1=sh, op=ALU.add),
                 reads=[r_tmp, r_sh], writes=[r_xh[j % 2]])
            pT = psum[pbank][:, :].bitcast(BF16)
            for c in range(8):
                P.op("tensor", lambda e, j=j, c=c, pT=pT: e.transpose(out=pT[:, c * 128:(c + 1) * 128],
                                                                       in_=xh[j % 2][:, c * 128:(c + 1) * 128],
                                                                       identity=ident[:]),
                     reads=[r_xh[j % 2], r_const], writes=[r_ps[pbank]])
            P.op("scalar", lambda e, j=j, pT=pT: e.copy(out=hfm[:, :, j * 128:(j + 1) * 128],
                                                        in_=pT.rearrange("p (c t) -> p c t", c=8)),
                 reads=[r_ps[pbank]], writes=[r_hfm])
            yield

    def phase_a(l, xsrc):
        new_phase()
        NCOL = O_GATE
        wA = AB.take(8, NCOL)
        bounds = [0, 1024, 2048, 3072, NCOL]
        r_wAp = [P.res(f"wA{i}") for i in range(4)]
        for pi in (0, 2, 3, 1):
            c0, c1 = bounds[pi], bounds[pi + 1]
            load_w(wA[:, :, c0:c1], w_in[l, :, c0:c1].rearrange("(c p) n -> p c n", p=128), r_wAp[pi])

        def rwA(col):
            return r_wAp[min(col // 1024, 3)]
        hfm = [AB.take(8, 512) for _ in range(2)]
        r_hfm = [P.res("hfm0"), P.res("hfm1")]
        xh = [AB.take(D) for _ in range(2)]
        r_xh = [P.res("xh0"), P.res("xh1")]
        sg = AB.take(4, 512)
        r_sg = P.res("sg")
        oc = [AB.take(512) for _ in range(8)]
        r_oc = [P.res(f"oc{i}") for i in range(8)]
        zs = [AB.take(D) for _ in range(2)]
        r_zs = [P.res("zs0"), P.res("zs1")]
        xt = [AFa.take(D) for _ in range(8)]
        r_xt = [P.res(f"xt{i}") for i in range(8)]
        gs = AFa.take(D)
        r_gs = P.res("gs")
        sh = AFa.take(D)
        r_sh = P.res("sh")
        tmp = AFa.take(D)
        r_tmp = P.res("tmp")
        gtmp = AFa.take(D)
        r_gtmp = P.res("gtmp")
        ss = AFa.take(4)
        r_ss = P.res("ss")
        rstd = AFa.take(4)
        r_rstd = P.res("rstd")
        dtr = [AFa.take(32) for _ in range(2)]
        r_dtr = [P.res("dtr0"), P.res("dtr1")]
        r_junk = P.res("junk")
        ocn = [0]
        pb = [0]

        def next_oc():
            i = ocn[0] % 8
            ocn[0] += 1
            return oc[i], r_oc[i]

        def next_pb():
            i = 1 + (pb[0] % 6)
            pb[0] += 1
            return i

        def prep(tt):
            if tt % 4 == 0:
                load_rows(l, tt // 4, None, [(gs, r_gs, "scale", "g_pre_mix", 1, gtmp, r_gtmp),
                                             (sh, r_sh, "shift", None, 0, None, None)])
            hh, rhh = hfm[tt % 2], r_hfm[tt % 2]
            yield from norm_transpose_tile(xsrc, tt, xt, r_xt, tmp, r_tmp, ss, r_ss, rstd, r_rstd, gs, r_gs, sh, r_sh, tmp,
                                           r_tmp, xh, r_xh, hh, rhh, 0)
            P.dma("sync", lambda e: e.dma_start(out=fm_tile(hfm_d, 0, 8, tt * 512, 512), in_=hh), reads=[rhh], semres=rhh)

        load_x_tile(xsrc, 0, xt, r_xt)
        load_x_tile(xsrc, 1, xt, r_xt)
        for _ in prep(0):
            pass
        for t in range(NT):
            slot = t // 4
            tok0 = t * 512
            if t + 2 < NT:
                load_x_tile(xsrc, t + 2, xt, r_xt)
            h, rh = hfm[t % 2], r_hfm[t % 2]
            nchunk = [0]
            gen = prep(t + 1) if t + 1 < NT else iter(())

            def fm_chunk(col0, epi, gen=gen):
                nchunk[0] += 1
                if nchunk[0] in (5, 9, 12, 15, 18, 21):
                    next(gen, None)
                b = next_pb()
                for c in range(8):
                    P.op("tensor", lambda e, c=c, b=b: e.matmul(psum[b][:, :], lhsT=wA[:, c, col0:col0 + 128],
                                                                 rhs=h[:, c, :], start=(c == 0), stop=(c == 7)),
                         reads=[rwA(col0), rh], writes=[r_ps[b]])
                epi(b)

            for i in range(4):
                fm_chunk(O_AG + i * 128, lambda b, i=i: P.op(
                    "scalar", lambda e: e.activation(out=sg[:, i, :], in_=psum[b][:, :], func=AF.Sigmoid),
                    reads=[r_ps[b]], writes=[r_sg]))
            for i in range(4):
                def epi(b, i=i):
                    o, ro = next_oc()
                    P.op("vector", lambda e: e.tensor_tensor(out=o, in0=psum[b][:, :], in1=sg[:, i, :], op=ALU.mult),
                         reads=[r_ps[b], r_sg], writes=[ro])
                    P.dma("sync", lambda e: e.dma_start(out=aglu_d[i, :, tok0:tok0 + 512], in_=o), reads=[ro], semres=ro)
                fm_chunk(O_AV + i * 128, epi)
            for i in range(12):
                def epi(b, i=i):
                    o, ro = next_oc()
                    P.op("scalar", lambda e: e.copy(out=o, in_=psum[b][:, :]), reads=[r_ps[b]], writes=[ro])
                    P.dma("sync", lambda e: e.dma_start(out=xbc_d[i, :, tok0:tok0 + 512], in_=o), reads=[ro], semres=ro)
                fm_chunk(O_XBC + i * 128, epi)
            for i in range(4):
                def epi(b, i=i):
                    o, ro = next_oc()
                    P.op("vector", lambda e: e.tensor_copy(out=o, in_=psum[b][:, :]), reads=[r_ps[b]], writes=[ro])
                    P.dma("sync", lambda e: e.dma_start(out=u_d[i, :, tok0:tok0 + 512], in_=o), reads=[ro], semres=ro)
                fm_chunk(O_UC + i * 128, epi)
            for j in range(4):
                z, rz = zs[j % 2], r_zs[j % 2]
                for half in range(2):
                    b = next_pb()
                    for c in range(8):
                        P.op("tensor", lambda e, c=c, b=b, j=j, half=half: e.matmul(
                            psum[b][:, :], lhsT=h[:, c, j * 128:(j + 1) * 128],
                            rhs=wA[:, c, O_Z + half * 512: O_Z + (half + 1) * 512], start=(c == 0), stop=(c == 7)),
                            reads=[rwA(O_Z), rh], writes=[r_ps[b]])
                    P.op("scalar", lambda e, b=b, z=z, half=half: e.activation(out=z[:, half * 512:(half + 1) * 512],
                                                                               in_=psum[b][:, :], func=AF.Silu),
                         reads=[r_ps[b]], writes=[rz])
                P.dma("sync", lambda e, z=z, j=j: e.dma_start(out=zs_d[tok0 + j * 128: tok0 + (j + 1) * 128, :], in_=z),
                      reads=[rz], semres=rz)
                b = next_pb()
                dd, rd = dtr[j % 2], r_dtr[j % 2]
                for c in range(8):
                    P.op("tensor", lambda e, c=c, b=b, j=j: e.matmul(
                        psum[b][:, 0:32], lhsT=h[:, c, j * 128:(j + 1) * 128], rhs=wA[:, c, O_DT:O_DT + 32],
                        start=(c == 0), stop=(c == 7)), reads=[rwA(O_DT), rh], writes=[r_ps[b]])
                P.op("vector", lambda e, b=b, dd=dd: e.tensor_copy(out=dd, in_=psum[b][:, 0:32]),
                     reads=[r_ps[b]], writes=[rd])
                P.dma("sync", lambda e, dd=dd, j=j: e.dma_start(out=dt_d[tok0 + j * 128: tok0 + (j + 1) * 128, :], in_=dd),
                      reads=[rd], semres=rd)
            for _ in gen:
                pass

    def phase_gates(l):
        new_phase()
        wG = AB.take(8, 3072)
        r_wGp = [P.res(f"wG{i}") for i in range(6)]
        for pi in range(6):
            c0 = pi * 512
            load_w(wG[:, :, c0:c0 + 512], w_in[l, :, O_GATE + c0:O_GATE + c0 + 512].rearrange("(c p) n -> p c n", p=128),
                   r_wGp[pi])
        hfm = [AB.take(8, 512) for _ in range(2)]
        r_hfm = [P.res("ghfm0"), P.res("ghfm1")]
        oc = [AB.take(512) for _ in range(8)]
        r_oc = [P.res(f"goc{i}") for i in range(8)]
        k = 0

        def ldh(t):
            h, rh = hfm[t % 2], r_hfm[t % 2]
            P.dma("sync", lambda e: e.dma_start(out=h, in_=fm_tile(hfm_d, 0, 8, t * 512, 512)), writes=[rh], semres=rh)

        ldh(0)
        for t in range(NT):
            tok0 = t * 512
            h, rh = hfm[t % 2], r_hfm[t % 2]
            if t + 1 < NT:
                ldh(t + 1)
            for i in range(24):
                b = k % 8
                o, ro = oc[k % 8], r_oc[k % 8]
                k += 1
                for c in range(8):
                    P.op("tensor", lambda e, c=c, b=b, i=i, h=h: e.matmul(psum[b][:, :], lhsT=wG[:, c, i * 128:(i + 1) * 128],
                                                                           rhs=h[:, c, :], start=(c == 0), stop=(c == 7)),
                         reads=[r_wGp[i // 4], rh], writes=[r_ps[b]])
                P.op("scalar", lambda e, b=b, o=o: e.activation(out=o, in_=psum[b][:, :], func=AF.Sigmoid),
                     reads=[r_ps[b]], writes=[ro])
                P.dma("sync", lambda e, o=o, i=i, tok0=tok0: e.dma_start(out=gates_d[i, :, tok0:tok0 + 512], in_=o),
                      reads=[ro], semres=ro)

    def load_halo(dst, rr, dram, C, t, hw):
        tok0 = t * 512
        seg_start = (t % 4 == 0)
        seg_end = (t % 4 == 3)
        lo = tok0 - hw
        hi = tok0 + 512 + hw
        d0 = 0
        if seg_start and t != 4:
            lo = tok0
            d0 = hw
        if seg_end and t != 3:
            hi = tok0 + 512
        if lo > tok0 - hw:
            P.op("gpsimd", lambda e: e.memset(dst[:, :, 0:hw], 0.0), writes=[rr])
        if hi < tok0 + 512 + hw:
            P.op("gpsimd", lambda e: e.memset(dst[:, :, 512 + hw:512 + 2 * hw], 0.0), writes=[rr])
        P.dma("sync", lambda e: e.dma_start(out=dst[:, :, d0:d0 + (hi - lo)], in_=fm_tile(dram, 0, C, lo, hi - lo)),
              writes=[rr], semres=rr)
        if t == 4:
            P.op("gpsimd", lambda e: e.tensor_scalar(out=dst[:, :, 0:hw], in0=dst[:, :, 0:hw], scalar1=flag[:, 0:1],
                                                     scalar2=None, op0=ALU.mult), reads=[rr, r_const], writes=[rr])
        if t == 3:
            P.op("gpsimd", lambda e: e.tensor_scalar(out=dst[:, :, 512 + hw:512 + 2 * hw],
                                                     in0=dst[:, :, 512 + hw:512 + 2 * hw], scalar1=flag[:, 0:1],
                                                     scalar2=None, op0=ALU.mult), reads=[rr, r_const], writes=[rr])

    def phase_conv_a(l):
        new_phase()
        caw = AFa.take(4, 31)
        cab = AFa.take(4)
        lng = AFa.take(4)
        lnb = AFa.take(4)
        r_par = P.res("cpar")
        for dst, src in ((caw, caw_d[l]), (cab, cab_d[l]), (lng, lng_d[l]), (lnb, lnb_d[l])):
            P.dma("sync", lambda e, dst=dst, src=src: e.dma_start(out=dst, in_=src), writes=[r_par], semres=r_par)
        dg = AB.take(4 * 31, 128)
        r_dg = P.res("dg")
        identf = AFa.take(128)
        r_idf = P.res("identf")
        P.op("vector", lambda e: e.tensor_copy(out=identf, in_=ident[:]), reads=[r_const], writes=[r_idf])
        for i in range(4):
            P.op("vector", lambda e, i=i: e.tensor_tensor(
                out=dg[:, i * 31:(i + 1) * 31, :], in0=bc(identf.unsqueeze(1), [128, 31, 128]),
                in1=bc(caw[:, i, :].unsqueeze(2), [128, 31, 128]), op=ALU.mult), reads=[r_idf, r_par], writes=[r_dg])
        ain = [AB.take(4, 542) for _ in range(2)]
        r_ain = [P.res("ain0"), P.res("ain1")]
        acc = AFa.take(4, 512)
        r_acc = [P.res(f"acc{i}") for i in range(4)]
        sq = [AFa.take(512) for _ in range(2)]
        r_sq = [P.res("sq0"), P.res("sq1")]
        mean = AFa.take(512)
        r_mean = P.res("mean")
        rs = AFa.take(512)
        r_rs = P.res("rs")
        xc = [AFa.take(512) for _ in range(2)]
        r_xc = [P.res("xc0"), P.res("xc1")]
        ob = [AB.take(512) for _ in range(8)]
        r_ob = [P.res(f"ob{i}") for i in range(8)]
        ones = masks[:, M_ONE, :]
        nb = 0
        load_halo(ain[0], r_ain[0], aglu_d, 4, 0, 15)
        for t in range(NT):
            tok0 = t * 512
            a, ra = ain[t % 2], r_ain[t % 2]
            if t + 1 < NT:
                load_halo(ain[(t + 1) % 2], r_ain[(t + 1) % 2], aglu_d, 4, t + 1, 15)
            for i in range(4):
                b = 2 + nb % 6
                nb += 1
                for k in range(31):
                    P.op("tensor", lambda e, i=i, k=k, a=a, b=b: e.matmul(psum[b][:, :], lhsT=dg[:, i * 31 + k, :],
                                                                           rhs=a[:, i, k:k + 512], start=(k == 0),
                                                                           stop=(k == 30)),
                         reads=[r_dg, ra], writes=[r_ps[b]])
                P.op("scalar", lambda e, i=i, b=b: e.activation(out=acc[:, i, :], in_=psum[b][:, :], func=AF.Identity,
                                                                bias=cab[:, i:i + 1]),
                     reads=[r_ps[b], r_par], writes=[r_acc[i]])
            for i in range(4):
                P.op("tensor", lambda e, i=i: e.matmul(psum[0][:, :], lhsT=ones, rhs=acc[:, i, :], start=(i == 0),
                                                       stop=(i == 3)), reads=[r_acc[i], r_const], writes=[r_ps[0]])
            for i in range(4):
                P.op("gpsimd", lambda e, i=i: e.tensor_tensor(out=sq[i % 2], in0=acc[:, i, :], in1=acc[:, i, :],
                                                              op=ALU.mult), reads=[r_acc[i]], writes=[r_sq[i % 2]])
                P.op("tensor", lambda e, i=i: e.matmul(psum[1][:, :], lhsT=ones, rhs=sq[i % 2], start=(i == 0),
                                                       stop=(i == 3)), reads=[r_sq[i % 2], r_const], writes=[r_ps[1]])
            P.op("vector", lambda e: e.tensor_scalar(out=mean, in0=psum[0][:, :], scalar1=1.0 / 512, scalar2=None,
                                                     op0=ALU.mult), reads=[r_ps[0]], writes=[r_mean])
            P.op("vector", lambda e: e.tensor_tensor(out=rs, in0=mean, in1=mean, op=ALU.mult), reads=[r_mean], writes=[r_rs])
            P.op("vector", lambda e: e.scalar_tensor_tensor(out=rs, in0=psum[1][:, :], scalar=1.0 / 512, in1=rs,
                                                            op0=ALU.mult, op1=ALU.subtract),
                 reads=[r_ps[1], r_rs], writes=[r_rs])
            P.op("scalar", lambda e: e.activation(out=rs, in_=rs, func=AF.Ln, bias=EPS), reads=[r_rs], writes=[r_rs])
            P.op("scalar", lambda e: e.activation(out=rs, in_=rs, func=AF.Exp, scale=-0.5), reads=[r_rs], writes=[r_rs])
            for i in range(4):
                o, ro = ob[(t * 4 + i) % 8], r_ob[(t * 4 + i) % 8]
                x_, rx_ = xc[i % 2], r_xc[i % 2]
                P.op("vector", lambda e, i=i, x_=x_: e.tensor_tensor(out=x_, in0=acc[:, i, :], in1=mean, op=ALU.subtract),
                     reads=[r_acc[i], r_mean], writes=[rx_])
                P.op("gpsimd", lambda e, x_=x_: e.tensor_tensor(out=x_, in0=x_, in1=rs, op=ALU.mult), reads=[rx_, r_rs],
                     writes=[rx_])
                P.op("scalar", lambda e, i=i, o=o, x_=x_: e.activation(out=o, in_=x_, func=AF.Silu, scale=lng[:, i:i + 1],
                                                                       bias=lnb[:, i:i + 1]),
                     reads=[rx_, r_par], writes=[ro])
                P.dma("sync", lambda e, i=i, o=o, tok0=tok0: e.dma_start(out=acv_d[i, :, tok0:tok0 + 512], in_=o),
                      reads=[ro], semres=ro)

    def phase_conv_s(l):
        new_phase()
        csw = AFa.take(12, 5)
        csb = AFa.take(12)
        r_par = P.res("spar")
        P.dma("sync", lambda e: e.dma_start(out=csw, in_=csw_d[l]), writes=[r_par], semres=r_par)
        P.dma("sync", lambda e: e.dma_start(out=csb, in_=csb_d[l]), writes=[r_par], semres=r_par)
        dgs = AB.take(60, 128)
        r_dgs = P.res("dgs")
        identf = AFa.take(128)
        r_idf = P.res("sidentf")
        P.op("vector", lambda e: e.tensor_copy(out=identf, in_=ident[:]), reads=[r_const], writes=[r_idf])
        for i in range(12):
            P.op("vector", lambda e, i=i: e.tensor_tensor(
                out=dgs[:, i * 5:(i + 1) * 5, :], in0=bc(identf.unsqueeze(1), [128, 5, 128]),
                in1=bc(csw[:, i, :].unsqueeze(2), [128, 5, 128]), op=ALU.mult), reads=[r_idf, r_par], writes=[r_dgs])
        xi = [AB.take(12, 516) for _ in range(2)]
        r_xi = [P.res("xi0"), P.res("xi1")]
        xo = [AB.take(12, 512) for _ in range(2)]
        r_xo = [[P.res(f"xo{b}_{i}") for i in range(12)] for b in range(2)]
        xs = [AB.take(D) for _ in range(2)]
        r_xs = [P.res("xs0"), P.res("xs1")]
        bt = [AB.take(256) for _ in range(2)]
        r_bt = [P.res("bt0"), P.res("bt1")]
        nb = 0
        load_halo(xi[0], r_xi[0], xbc_d, 12, 0, 2)
        for t in range(NT):
            tok0 = t * 512
            a, ra = xi[t % 2], r_xi[t % 2]
            o, ro = xo[t % 2], r_xo[t % 2]
            if t + 1 < NT:
                load_halo(xi[(t + 1) % 2], r_xi[(t + 1) % 2], xbc_d, 12, t + 1, 2)
            for i in range(12):
                b = 4 + nb % 4
                nb += 1
                for k in range(5):
                    P.op("tensor", lambda e, i=i, k=k, a=a, b=b: e.matmul(psum[b][:, :], lhsT=dgs[:, i * 5 + k, :],
                                                                           rhs=a[:, i, k:k + 512], start=(k == 0),
                                                                           stop=(k == 4)),
                         reads=[r_dgs, ra], writes=[r_ps[b]])
                P.op("scalar", lambda e, i=i, o=o, b=b: e.activation(out=o[:, i, :], in_=psum[b][:, :], func=AF.Silu,
                                                                     bias=csb[:, i:i + 1]),
                     reads=[r_ps[b], r_par], writes=[ro[i]])
            P.dma("sync", lambda e, o=o, tok0=tok0: e.dma_start(out=fm_tile(bc_d, 0, 4, tok0, 512), in_=o[:, 8:12, :]),
                  reads=ro[8:12], semres=ro[8])
            for j in range(4):
                pT = psum[j % 2][:, :].bitcast(BF16)
                x_, rx = xs[j % 2], r_xs[j % 2]
                for c in range(8):
                    P.op("tensor", lambda e, c=c, j=j, o=o, pT=pT: e.transpose(out=pT[:, c * 128:(c + 1) * 128],
                                                                               in_=o[:, c, j * 128:(j + 1) * 128],
                                                                               identity=ident[:]),
                         reads=[ro[c], r_const], writes=[r_ps[j % 2]])
                P.op("vector", lambda e, x_=x_, pT=pT: e.tensor_copy(out=x_, in_=pT), reads=[r_ps[j % 2]], writes=[rx])
                P.dma("sync", lambda e, x_=x_, j=j, tok0=tok0: e.dma_start(
                    out=xs_d[tok0 + j * 128: tok0 + (j + 1) * 128, :], in_=x_), reads=[rx], semres=rx)
                pB = psum[2 + j % 2][:, :].bitcast(BF16)
                b_, rb = bt[j % 2], r_bt[j % 2]
                for c in range(2):
                    P.op("tensor", lambda e, c=c, j=j, o=o, pB=pB: e.transpose(out=pB[:, c * 128:(c + 1) * 128],
                                                                               in_=o[:, 8 + c, j * 128:(j + 1) * 128],
                                                                               identity=ident[:]),
                         reads=[ro[8 + c], r_const], writes=[r_ps[2 + j % 2]])
                P.op("scalar", lambda e, b_=b_, pB=pB: e.copy(out=b_, in_=pB[:, 0:256]), reads=[r_ps[2 + j % 2]],
                     writes=[rb])
                P.dma("sync", lambda e, b_=b_, j=j, tok0=tok0: e.dma_start(
                    out=bt_d[tok0 + j * 128: tok0 + (j + 1) * 128, :], in_=b_), reads=[rb], semres=rb)

    def phase_ssd(l):
        new_phase()
        dtb = AFa.take(32)
        arow = AFa.take(32)
        dsk = AFa.take(16)
        r_par = P.res("dpar")
        P.dma("sync", lambda e: e.dma_start(out=dtb, in_=dtb_d[l:l + 1, :].partition_broadcast(128)), writes=[r_par],
              semres=r_par)
        P.dma("sync", lambda e: e.dma_start(out=arow, in_=alog_d[l:l + 1, :].partition_broadcast(128)), writes=[r_par],
              semres=r_par)
        P.dma("sync", lambda e: e.dma_start(out=dsk, in_=dsk_d[l:l + 1, :].partition_broadcast(128)), writes=[r_par],
              semres=r_par)
        P.op("scalar", lambda e: e.activation(out=arow, in_=arow, func=AF.Exp), reads=[r_par], writes=[r_par])
        P.op("vector", lambda e: e.tensor_scalar(out=arow, in0=arow, scalar1=-1.0, scalar2=None, op0=ALU.mult),
             reads=[r_par], writes=[r_par])
        gsr = AFa.take(D)
        r_gsr = P.res("gsr")
        P.dma("sync", lambda e: e.dma_start(out=gsr, in_=gvec["g_ssd"][l:l + 1, :].partition_broadcast(128)),
              writes=[r_gsr], semres=r_gsr)
        r_yfd = [P.res(f"yfd{i}") for i in range(NSUB)]

        class S:
            pass

        def mk(d):
            b = S()
            n = f"d{d}"
            b.H = AFa.take(2, 512); b.r_H = P.res(n + "H")
            b.R = AFa.take(16, 128); b.r_R = P.res(n + "R")
            b.ytmp = AFa.take(D); b.r_ytmp = P.res(n + "ytmp")
            b.yfl = AFa.take(D); b.r_yfl = P.res(n + "yfl")
            b.small = [AFa.take(8, 16) for _ in range(2)]
            b.r_sm = [[P.res(n + f"sm{q}_{i}") for i in range(8)] for q in range(2)]
            b.cst = AFa.take(32); b.r_cst = P.res(n + "cst")
            b.dtr = [AFa.take(32) for _ in range(3)]; b.r_dtr = [P.res(n + f"dtr{i}") for i in range(3)]
            b.ss1 = AFa.take(1); b.r_ss1 = P.res(n + "ss1")
            b.Hb = AB.take(2, 512); b.r_Hb = P.res(n + "Hb")
            b.xs = [AB.take(D) for _ in range(3)]; b.r_xs = [P.res(n + f"xs{i}") for i in range(3)]
            b.bt = [AB.take(256) for _ in range(3)]; b.r_bt = [P.res(n + f"bt{i}") for i in range(3)]
            b.bcf = [AB.take(4, 128) for _ in range(3)]; b.r_bcf = [P.res(n + f"bcf{i}") for i in range(3)]
            b.zt = [AB.take(D) for _ in range(3)]; b.r_zt = [P.res(n + f"zt{i}") for i in range(3)]
            b.xdt = AB.take(D); b.r_xdt = P.res(n + "xdt")
            b.xw = AB.take(D); b.r_xw = P.res(n + "xw")
            b.Lm = AB.take(16, 128); b.r_Lm = P.res(n + "Lm")
            b.Mm = AB.take(16, 128); b.r_Mm = P.res(n + "Mm")
            b.smk = AB.take(2, 128); b.r_smk = P.res(n + "smk")
            b.stb = [AB.take(D) for _ in range(2)]; b.r_stb = [P.res(n + "stb0"), P.res(n + "stb1")]
            b.s16 = [AB.take(2, 16) for _ in range(2)]
            b.r_s16 = [[P.res(n + f"s16_{q}_{i}") for i in range(2)] for q in range(2)]
            b.ydg = [AB.take(D) for _ in range(2)]; b.r_ydg = [P.res(n + "ydg0"), P.res(n + "ydg1")]
            b.yn = AB.take(D); b.r_yn = P.res(n + "yn")
            b.yfm = [AB.take(8, 128) for _ in range(2)]; b.r_yfm = [P.res(n + "yfm0"), P.res(n + "yfm1")]
            return b

        BUF = [mk(0), mk(1)]
        HALF = NSUB // 2

        def chunk_of(d, n):
            return n if d == 0 else NSUB - 1 - n

        def loads(d, n):
            B = BUF[d]
            ci = chunk_of(d, n)
            tok0 = ci * 128
            q = n % 3
            P.dma("sync", lambda e: e.dma_start(out=B.dtr[q], in_=dt_d[tok0:tok0 + 128, :]), writes=[B.r_dtr[q]],
                  semres=B.r_dtr[q])
            P.dma("sync", lambda e: e.dma_start(out=B.bcf[q], in_=fm_tile(bc_d, 0, 4, tok0, 128)), writes=[B.r_bcf[q]],
                  semres=B.r_bcf[q])
            P.dma("sync", lambda e: e.dma_start(out=B.xs[q], in_=xs_d[tok0:tok0 + 128, :]), writes=[B.r_xs[q]],
                  semres=B.r_xs[q])
            P.dma("sync", lambda e: e.dma_start(out=B.bt[q], in_=bt_d[tok0:tok0 + 128, :]), writes=[B.r_bt[q]],
                  semres=B.r_bt[q])
            if n >= HALF:
                P.dma("sync", lambda e: e.dma_start(out=B.zt[q], in_=zs_d[tok0:tok0 + 128, :]), writes=[B.r_zt[q]],
                      semres=B.r_zt[q])

        def names(d, n):
            B = BUF[d]
            v = S()
            q = n % 3
            p2 = n % 2
            v.B = B
            v.ci = chunk_of(d, n)
            v.tok0 = v.ci * 128
            pb = 4 * d
            v.L0, v.L1, v.X, v.Y = pb, pb + 1, pb + 2, pb + 3
            v.x_, v.rx = B.xs[q], B.r_xs[q]
            v.b_, v.rb = B.bt[q], B.r_bt[q]
            v.f_, v.rf = B.bcf[q], B.r_bcf[q]
            v.dr, v.rdr = B.dtr[q], B.r_dtr[q]
            v.z_, v.rz = B.zt[q], B.r_zt[q]
            small = B.small[p2]
            rs_ = B.r_sm[p2]
            v.dt_, v.a_, v.d1_, v.dtw_ = small[:, 0, :], small[:, 1, :], small[:, 2, :], small[:, 6, :]
            v.E3 = small[:, 3:6, :]
            v.r_dt, v.r_a, v.r_d1, v.r_E, v.r_dtw = rs_[0], rs_[1], rs_[2], rs_[3], rs_[6]
            v.w_out_ = v.E3[:, 0, :] if d == 0 else v.E3[:, 1, :]
            v.w_st = v.E3[:, 1, :] if d == 0 else v.E3[:, 0, :]
            v.dec = v.E3[:, 2, :]
            v.stb, v.r_stb = B.stb[p2], B.r_stb[p2]
            v.dt_b, v.dtw_b = B.s16[p2][:, 0, :], B.s16[p2][:, 1, :]
            v.r_dt_b, v.r_dtw_b = B.r_s16[p2][0], B.r_s16[p2][1]
            v.ydg, v.r_ydg = B.ydg[p2], B.r_ydg[p2]
            v.x3 = v.x_.rearrange("p (k q) -> p k q", k=16)
            return v

        def local(d, n):
            v = names(d, n)
            B = v.B
            dc = slice(d * 16, d * 16 + 16)
            dt_, a_, d1_, dtw_, E3 = v.dt_, v.a_, v.d1_, v.dtw_, v.E3
            L0, L1, X, Y = v.L0, v.L1, v.X, v.Y
            f_, rf = v.f_, v.rf
            P.op("vector", lambda e: e.tensor_tensor(out=dt_, in0=v.dr[:, dc], in1=dtb[:, dc], op=ALU.add),
                 reads=[v.rdr, r_par], writes=[v.r_dt])
            P.op("scalar", lambda e: e.activation(out=dt_, in_=dt_, func=AF.Exp), reads=[v.r_dt], writes=[v.r_dt])
            P.op("scalar", lambda e: e.activation(out=dt_, in_=dt_, func=AF.Ln, bias=1.0), reads=[v.r_dt],
                 writes=[v.r_dt])
            P.op("vector", lambda e: e.tensor_tensor(out=a_, in0=dt_, in1=arow[:, dc], op=ALU.mult),
                 reads=[v.r_dt, r_par], writes=[v.r_a])
            yield
            tri = masks[:, M_LE, :] if d == 0 else masks[:, M_LT, :]
            P.op("tensor", lambda e: e.matmul(psum[X][:, 0:16], lhsT=tri, rhs=a_, start=True, stop=True),
                 reads=[v.r_a, r_const], writes=[r_ps[X]])
            P.op("tensor", lambda e: e.matmul(psum[X][:, 16:32], lhsT=masks[:, M_ONE, :], rhs=a_, start=True, stop=True),
                 reads=[v.r_a, r_const], writes=[r_ps[X]])
            for g in range(2):
                P.op("tensor", lambda e, g=g: e.matmul(psum[X][:, 64 + g * 128: 64 + (g + 1) * 128], lhsT=f_[:, g, :],
                                                       rhs=f_[:, 2 + g, :], start=True, stop=True),
                     reads=[rf], writes=[r_ps[X]])
            P.op("vector", lambda e: e.tensor_copy(out=B.cst, in_=psum[X][:, 0:32]), reads=[r_ps[X]], writes=[B.r_cst])
            m1 = masks[:, M_LE, :] if d == 0 else masks[:, M_GE, :]
            m2 = masks[:, M_GT, :] if d == 0 else masks[:, M_LT, :]
            P.op("vector", lambda e: e.tensor_tensor(out=B.smk, in0=psum[X][:, 64:320].rearrange("p (g l) -> p g l", g=2),
                                                     in1=bc(m1.unsqueeze(1), [128, 2, 128]), op=ALU.mult),
                 reads=[r_ps[X], r_const], writes=[B.r_smk])
            yield
            cst = B.cst
            P.op("vector", lambda e: e.tensor_tensor(out=d1_, in0=cst[:, 16:32], in1=cst[:, 0:16], op=ALU.subtract),
                 reads=[B.r_cst], writes=[v.r_d1])
            P.op("scalar", lambda e: e.activation(out=E3[:, 0, :], in_=cst[:, 0:16], func=AF.Exp), reads=[B.r_cst],
                 writes=[v.r_E])
            P.op("scalar", lambda e: e.activation(out=E3[:, 1, :], in_=d1_, func=AF.Exp), reads=[v.r_d1], writes=[v.r_E])
            P.op("scalar", lambda e: e.activation(out=E3[:, 2, :], in_=cst[:, 16:32], func=AF.Exp), reads=[B.r_cst],
                 writes=[v.r_E])
            P.op("vector", lambda e: e.tensor_tensor(out=v.dtw_b, in0=dt_, in1=v.w_st, op=ALU.mult), reads=[v.r_dt, v.r_E],
                 writes=[v.r_dtw_b])
            P.op("gpsimd", lambda e: e.tensor_copy(out=v.dt_b, in_=dt_), reads=[v.r_dt], writes=[v.r_dt_b])
            yield
            for k in range(16):
                P.op("scalar", lambda e, k=k: e.activation(out=B.R[:, k, :], in_=m1, func=AF.Identity,
                                                           scale=a_[:, k:k + 1]),
                     reads=[v.r_a, r_const], writes=[B.r_R])
            P.op("vector", lambda e: e.tensor_tensor(out=B.xw.rearrange("p (k q) -> p k q", k=16), in0=v.x3,
                                                     in1=bc(v.dtw_b.unsqueeze(2), [128, 16, 64]), op=ALU.mult),
                 reads=[v.rx, v.r_dtw_b], writes=[B.r_xw])
            P.op("vector", lambda e: e.tensor_tensor(out=B.xdt.rearrange("p (k q) -> p k q", k=16), in0=v.x3,
                                                     in1=bc(v.dt_b.unsqueeze(2), [128, 16, 64]), op=ALU.mult),
                 reads=[v.rx, v.r_dt_b], writes=[B.r_xdt])
            yield
            for q in range(4):
                bq = L0 + q % 2
                P.op("tensor", lambda e, q=q, bq=bq: e.matmul(psum[bq][:, :], lhsT=m2,
                                                              rhs=B.R[:, q * 4:(q + 1) * 4, :].rearrange("p k l -> p (k l)"),
                                                              start=True, stop=True), reads=[B.r_R, r_const],
                     writes=[r_ps[bq]])
                P.op("scalar", lambda e, q=q, bq=bq: e.activation(
                    out=B.Lm[:, q * 4:(q + 1) * 4, :].rearrange("p k l -> p (k l)"), in_=psum[bq][:, :], func=AF.Exp),
                    reads=[r_ps[bq]], writes=[B.r_Lm])
                if q == 1:
                    yield
            yield
            for g in range(2):
                P.op("tensor", lambda e, g=g: e.matmul(psum[X + g][:, :], lhsT=v.b_[:, g * 128:(g + 1) * 128],
                                                       rhs=B.xw[:, g * 512:(g + 1) * 512], start=True, stop=True),
                     reads=[v.rb, B.r_xw], writes=[r_ps[X + g]])
                P.op("scalar", lambda e, g=g: e.copy(out=v.stb[:, g * 512:(g + 1) * 512], in_=psum[X + g][:, :]),
                     reads=[r_ps[X + g]], writes=[v.r_stb])
            P.op("vector", lambda e: e.tensor_tensor(out=B.Mm.rearrange("p (g k) l -> p g k l", g=2),
                                                     in0=B.Lm.rearrange("p (g k) l -> p g k l", g=2),
                                                     in1=bc(B.smk.unsqueeze(2), [128, 2, 8, 128]), op=ALU.mult),
                 reads=[B.r_Lm, B.r_smk], writes=[B.r_Mm])
            yield
            for k in range(16):
                bk = L0 + k // 8
                P.op("tensor", lambda e, k=k, bk=bk: e.matmul(psum[bk][:, (k % 8) * 64:(k % 8 + 1) * 64],
                                                              lhsT=B.Mm[:, k, :], rhs=B.xdt[:, k * 64:(k + 1) * 64],
                                                              start=True, stop=True),
                     reads=[B.r_Mm, B.r_xdt], writes=[r_ps[bk]])
            for g in range(2):
                P.op("scalar", lambda e, g=g: e.copy(out=v.ydg[:, g * 512:(g + 1) * 512], in_=psum[L0 + g][:, :]),
                     reads=[r_ps[L0 + g]], writes=[v.r_ydg])
            yield

        def recur(d, n):
            v = names(d, n)
            B = v.B
            ci, tok0 = v.ci, v.tok0
            fin = n >= HALF
            L0, L1, X, Y = v.L0, v.L1, v.X, v.Y
            H, r_H, Hb, r_Hb = B.H, B.r_H, B.Hb, B.r_Hb
            ytmp, r_ytmp, yfl, r_yfl = B.ytmp, B.r_ytmp, B.yfl, B.r_yfl
            for g in range(2):
                P.op("tensor", lambda e, g=g: e.matmul(psum[X + g][:, :], lhsT=v.f_[:, 2 + g, :], rhs=Hb[:, g, :],
                                                       start=True, stop=True), reads=[v.rf, r_Hb], writes=[r_ps[X + g]])
            for g in range(2):
                P.op("gpsimd", lambda e, g=g: e.tensor_tensor(out=H[:, g, :].rearrange("p (k q) -> p k q", k=8),
                                                              in0=H[:, g, :].rearrange("p (k q) -> p k q", k=8),
                                                              in1=bc(v.dec[:, g * 8:(g + 1) * 8].unsqueeze(2), [128, 8, 64]),
                                                              op=ALU.mult), reads=[r_H, v.r_E], writes=[r_H])
            P.op("gpsimd", lambda e: e.tensor_tensor(out=H.rearrange("p g q -> p (g q)"),
                                                     in0=H.rearrange("p g q -> p (g q)"), in1=v.stb, op=ALU.add),
                 reads=[r_H, v.r_stb], writes=[r_H])
            nxt = ci + 1 if d == 0 else ci - 1
            at_edge = (nxt % 16 == 0) if d == 0 else (ci % 16 == 0)
            if at_edge:
                coupled = (d == 0 and nxt == 16) or (d == 1 and ci == 16)
                if coupled:
                    P.op("vector", lambda e: e.tensor_scalar(out=H, in0=H, scalar1=flag[:, 0:1], scalar2=None,
                                                             op0=ALU.mult), reads=[r_H, r_const], writes=[r_H])
                else:
                    P.op("vector", lambda e: e.memset(H, 0.0), writes=[r_H])
            P.op("scalar", lambda e: e.copy(out=Hb, in_=H), reads=[r_H], writes=[r_Hb])
            for g in range(2):
                P.op("vector", lambda e, g=g: e.tensor_tensor(
                    out=ytmp[:, g * 512:(g + 1) * 512].rearrange("p (k q) -> p k q", k=8),
                    in0=psum[X + g][:, :].rearrange("p (k q) -> p k q", k=8),
                    in1=bc(v.w_out_[:, g * 8:(g + 1) * 8].unsqueeze(2), [128, 8, 64]), op=ALU.mult),
                    reads=[r_ps[X + g], v.r_E], writes=[r_ytmp])
            yield
            P.op("gpsimd", lambda e: e.tensor_tensor(out=ytmp, in0=ytmp, in1=v.ydg, op=ALU.add),
                 reads=[r_ytmp, v.r_ydg], writes=[r_ytmp])
            yield
            if not fin:
                P.dma("sync", lambda e: e.dma_start(out=yf_d[tok0:tok0 + 128, :], in_=ytmp), reads=[r_ytmp],
                      writes=[r_yfd[ci]], semres=r_ytmp)
                return
            ss1, r_ss1, yn, r_yn = B.ss1, B.r_ss1, B.yn, B.r_yn
            P.dma("sync", lambda e: e.dma_start(out=yfl, in_=yf_d[tok0:tok0 + 128, :]), reads=[r_yfd[ci]],
                  writes=[r_yfl], semres=r_yfl)
            P.op("gpsimd", lambda e: e.tensor_tensor(out=ytmp, in0=ytmp, in1=yfl, op=ALU.add),
                 reads=[r_ytmp, r_yfl], writes=[r_ytmp])
            P.op("gpsimd", lambda e: e.tensor_tensor(out=yfl.rearrange("p (k q) -> p k q", k=16), in0=v.x3,
                                                     in1=bc(dsk.unsqueeze(2), [128, 16, 64]), op=ALU.mult),
                 reads=[v.rx, r_par, r_yfl], writes=[r_yfl])
            yield
            P.op("vector", lambda e: e.tensor_tensor(out=ytmp, in0=ytmp, in1=yfl, op=ALU.add),
                 reads=[r_ytmp, r_yfl], writes=[r_ytmp])
            P.op("vector", lambda e: e.tensor_tensor(out=ytmp, in0=ytmp, in1=v.z_, op=ALU.mult),
                 reads=[r_ytmp, v.rz], writes=[r_ytmp])
            P.op("scalar", lambda e: e.activation(out=yfl, in_=ytmp, func=AF.Square, accum_out=ss1),
                 reads=[r_ytmp], writes=[r_yfl, r_ss1])
            rstd_from_ss(ss1, ss1, r_ss1, r_ss1, D)
            yield
            P.op("vector", lambda e: e.scalar_tensor_tensor(out=yn, in0=ytmp, scalar=ss1[:, 0:1], in1=gsr,
                                                            op0=ALU.mult, op1=ALU.mult),
                 reads=[r_ytmp, r_ss1, r_gsr], writes=[r_yn])
            pT = psum[L0][:, :].bitcast(BF16)
            for c in range(8):
                P.op("tensor", lambda e, c=c: e.transpose(out=pT[:, c * 128:(c + 1) * 128],
                                                          in_=yn[:, c * 128:(c + 1) * 128], identity=ident[:]),
                     reads=[r_yn, r_const], writes=[r_ps[L0]])
            yo, ryo = B.yfm[n % 2], B.r_yfm[n % 2]
            P.op("scalar", lambda e: e.copy(out=yo, in_=pT.rearrange("p (c t) -> p c t", c=8)), reads=[r_ps[L0]],
                 writes=[ryo])
            P.dma("sync", lambda e: e.dma_start(out=fm_tile(yfm_d, 0, 8, tok0, 128), in_=yo), reads=[ryo], semres=ryo)
            yield

        for d in range(2):
            B = BUF[d]
            P.op("vector", lambda e, B=B: e.memset(B.H, 0.0), writes=[B.r_H])
            P.op("scalar", lambda e, B=B: e.copy(out=B.Hb, in_=B.H), reads=[B.r_H], writes=[B.r_Hb])
            loads(d, 0)
        for m in range(NSUB + 1):
            for d in range(2):
                if m + 1 < NSUB:
                    loads(d, m + 1)
            gens = []
            for d in range(2):
                if m < NSUB:
                    gens.append(local(d, m))
            for d in range(2):
                if m >= 1:
                    gens.append(recur(d, m - 1))
            while gens:
                alive = []
                for g_ in gens:
                    try:
                        next(g_)
                        alive.append(g_)
                    except StopIteration:
                        pass
                gens = alive

    def phase_fnet(l):
        new_phase()
        csc = AB.take(256)
        r_csc = P.res("csc")
        P.dma("sync", lambda e: e.dma_start(out=csc, in_=csc_d), writes=[r_csc], semres=r_csc)
        uv = AB.take(NSUB, D)
        r_uv = P.res("uv")
        ut = [AB.take(4, 512) for _ in range(2)]
        r_ut = [P.res("ut0"), P.res("ut1")]
        tab = [AB.take(16, 512) for _ in range(2)]
        r_tab = [P.res("tab0"), P.res("tab1")]
        ft = [AB.take(4, 512) for _ in range(2)]
        r_ft = [P.res("ft0"), P.res("ft1")]
        def ld_u(tt):
            P.dma("sync", lambda e: e.dma_start(out=ut[tt % 2], in_=fm_tile(u_d, 0, 4, tt * 512, 512)),
                  writes=[r_ut[tt % 2]], semres=r_ut[tt % 2])

        ld_u(0)
        for t in range(NT):
            tok0 = t * 512
            u, ru = ut[t % 2], r_ut[t % 2]
            if t + 1 < NT:
                ld_u(t + 1)
            for j in range(4):
                for half in range(2):
                    b = (j * 2 + half) % 8
                    for gg in range(2):
                        g = half * 2 + gg
                        P.op("tensor", lambda e, g=g, gg=gg, j=j, b=b, u=u: e.matmul(
                            psum[b][:, gg * 256:(gg + 1) * 256], lhsT=u[:, g, j * 128:(j + 1) * 128], rhs=csc,
                            start=True, stop=True), reads=[ru, r_csc], writes=[r_ps[b]])
                    eng = "vector" if half == 0 else "scalar"
                    dst = uv[:, t * 4 + j, half * 512:(half + 1) * 512]
                    if eng == "vector":
                        P.op("vector", lambda e, b=b, dst=dst: e.tensor_copy(out=dst, in_=psum[b][:, :]),
                             reads=[r_ps[b]], writes=[r_uv])
                    else:
                        P.op("scalar", lambda e, b=b, dst=dst: e.copy(out=dst, in_=psum[b][:, :]),
                             reads=[r_ps[b]], writes=[r_uv])
        blocks = {0: [(0, 0), (1, 1)], 1: [(2, 0), (3, 1)], 2: [(4, 2)]}
        allsteps = []
        for oseg in range(3):
            for kt in range(4):
                steps = [(bi, iseg, cs) for (bi, iseg) in blocks[oseg] for cs in range(2)]
                for si, (bi, iseg, cs) in enumerate(steps):
                    allsteps.append((oseg, kt, bi, iseg, cs, si, len(steps)))

        def ld_tab(n):
            oseg, kt, bi, iseg, cs, si, ns = allsteps[n]
            P.dma("sync", lambda e: e.dma_start(out=tab[n % 2], in_=tab_d[bi, kt, cs]), writes=[r_tab[n % 2]],
                  semres=r_tab[n % 2])

        ld_tab(0)
        for n, (oseg, kt, bi, iseg, cs, si, ns) in enumerate(allsteps):
            if n + 1 < len(allsteps):
                ld_tab(n + 1)
            tb, rt = tab[n % 2], r_tab[n % 2]
            tot = ns * 16
            for ncn in range(16):
                cnt = si * 16 + ncn
                for g in range(4):
                    P.op("tensor", lambda e, g=g, ncn=ncn, cnt=cnt: e.matmul(
                        psum[g + 4 * ((oseg * 4 + kt) % 2)][:, :],
                        lhsT=uv[:, iseg * 16 + ncn, g * 256 + cs * 128: g * 256 + (cs + 1) * 128],
                        rhs=tb[:, ncn, :], start=(cnt == 0), stop=(cnt == tot - 1)),
                        reads=[r_uv, rt], writes=[r_ps[g + 4 * ((oseg * 4 + kt) % 2)]])
            if si == ns - 1:
                pb0 = 4 * ((oseg * 4 + kt) % 2)
                fo, rfo = ft[(oseg * 4 + kt) % 2], r_ft[(oseg * 4 + kt) % 2]
                for g in range(4):
                    if g % 2 == 0:
                        P.op("vector", lambda e, g=g: e.tensor_copy(out=fo[:, g, :], in_=psum[pb0 + g][:, :]),
                             reads=[r_ps[pb0 + g]], writes=[rfo])
                    else:
                        P.op("scalar", lambda e, g=g: e.copy(out=fo[:, g, :], in_=psum[pb0 + g][:, :]),
                             reads=[r_ps[pb0 + g]], writes=[rfo])
                tok0 = oseg * SEG + kt * 512
                P.dma("sync", lambda e: e.dma_start(out=fm_tile(f_d, 0, 4, tok0, 512), in_=fo), reads=[rfo], semres=rfo)

    def phase_merge(l, xsrc):
        new_phase()
        wa = AB.take(4, D)
        wb = AB.take(8, D)
        wc = AB.take(4, D)
        wo = AB.take(8, D)
        r_w1 = [P.res("wE0"), P.res("wE1"), P.res("wE2"), P.res("wE3")]
        r_wo = [P.res("wo0"), P.res("wo1")]
        for hq in range(4):
            cq = slice(hq * 256, (hq + 1) * 256)
            load_w(wa[:, :, cq], w_a_out[l][:, cq].rearrange("(c p) n -> p c n", p=128), r_w1[hq])
            load_w(wb[:, :, cq], w_b_out[l][:, cq].rearrange("(c p) n -> p c n", p=128), r_w1[hq])
            load_w(wc[:, :, cq], w_c_out[l][:, cq].rearrange("(c p) n -> p c n", p=128), r_w1[hq])
        for hq in range(2):
            cq = slice(hq * 512, (hq + 1) * 512)
            load_w(wo[:, :, cq], w_out[l][:, cq].rearrange("(c p) n -> p c n", p=128), r_wo[hq])
        ia = [AB.take(4, 512) for _ in range(2)]
        iy = [AB.take(8, 512) for _ in range(2)]
        if_ = [AB.take(4, 512) for _ in range(2)]
        ig = [AB.take(24, 512) for _ in range(2)]
        r_in = [P.res("Ein0"), P.res("Ein1")]
        mb = [AB.take(8, 512) for _ in range(2)]
        r_mb = [P.res("mb0"), P.res("mb1")]
        mf = [AFa.take(512) for _ in range(2)]
        r_mf = [P.res("mf0"), P.res("mf1")]
        tp = [AFa.take(512) for _ in range(2)]
        r_tp = [P.res("tp0"), P.res("tp1")]
        tq = [AFa.take(512) for _ in range(2)]
        r_tq = [P.res("tq0"), P.res("tq1")]
        xt = [AFa.take(D) for _ in range(2)]
        r_xt = [P.res("ext0"), P.res("ext1")]
        gg = [AFa.take(D) for _ in range(2)]
        r_gg = [P.res("gg0"), P.res("gg1")]
        gtmp = AFa.take(D)
        r_gtmp = P.res("egtmp")
        tmp = [AFa.take(D) for _ in range(2)]
        r_tmp = [P.res("etmp0"), P.res("etmp1")]
        xo = [AFa.take(D) for _ in range(2)]
        r_xo = [P.res("exo0"), P.res("exo1")]
        ss = [AFa.take(2) for _ in range(2)]
        r_ss = [P.res("ess0"), P.res("ess1")]
        kk = 0
        for t in range(NT):
            tok0 = t * 512
            slot = t // 4
            if t % 4 == 0:
                load_rows(l, slot, None, [(gg[slot % 2], r_gg[slot % 2], "gate", "g_post_mix", 2, gtmp, r_gtmp)])
            ia_, iy_, if__, ig_, rin = ia[t % 2], iy[t % 2], if_[t % 2], ig[t % 2], r_in[t % 2]
            mb_, rmb = mb[t % 2], r_mb[t % 2]

            def ld_in(tt):
                for dst, src, C in ((ia[tt % 2], acv_d, 4), (iy[tt % 2], yfm_d, 8), (if_[tt % 2], f_d, 4),
                                    (ig[tt % 2], gates_d, 24)):
                    P.dma("sync", lambda e, dst=dst, src=src, C=C: e.dma_start(out=dst,
                                                                               in_=fm_tile(src, 0, C, tt * 512, 512)),
                          writes=[r_in[tt % 2]], semres=r_in[tt % 2])
            if t == 0:
                ld_in(0)
            if t + 1 < NT:
                ld_in(t + 1)
            for i in range(8):
                cs_ = slice(i * 128, (i + 1) * 128)
                b0 = 3 * (kk % 2)
                mf_, rmf = mf[kk % 2], r_mf[kk % 2]
                tp_, rtp = tp[kk % 2], r_tp[kk % 2]
                kk += 1
                for c in range(4):
                    P.op("tensor", lambda e, c=c, cs_=cs_, b0=b0: e.matmul(psum[b0][:, :], lhsT=wa[:, c, cs_],
                                                                            rhs=ia_[:, c, :], start=(c == 0), stop=(c == 3)),
                         reads=[r_w1[i // 2], rin], writes=[r_ps[b0]])
                for c in range(8):
                    P.op("tensor", lambda e, c=c, cs_=cs_, b0=b0: e.matmul(psum[b0 + 1][:, :], lhsT=wb[:, c, cs_],
                                                                            rhs=iy_[:, c, :], start=(c == 0), stop=(c == 7)),
                         reads=[r_w1[i // 2], rin], writes=[r_ps[b0 + 1]])
                for c in range(4):
                    P.op("tensor", lambda e, c=c, cs_=cs_, b0=b0: e.matmul(psum[b0 + 2][:, :], lhsT=wc[:, c, cs_],
                                                                            rhs=if__[:, c, :], start=(c == 0), stop=(c == 3)),
                         reads=[r_w1[i // 2], rin], writes=[r_ps[b0 + 2]])
                P.op("vector", lambda e, i=i, b0=b0: e.tensor_tensor(out=mf_, in0=psum[b0][:, :], in1=ig_[:, i, :],
                                                                     op=ALU.mult),
                     reads=[r_ps[b0], rin], writes=[rmf])
                P.op("vector", lambda e, i=i, b0=b0: e.tensor_tensor(out=tp_, in0=psum[b0 + 1][:, :], in1=ig_[:, 8 + i, :],
                                                                     op=ALU.mult),
                     reads=[r_ps[b0 + 1], rin], writes=[rtp])
                tq_, rtq = tq[(kk - 1) % 2], r_tq[(kk - 1) % 2]
                P.op("vector", lambda e, i=i, b0=b0: e.tensor_tensor(out=tq_, in0=psum[b0 + 2][:, :], in1=ig_[:, 16 + i, :],
                                                                     op=ALU.mult),
                     reads=[r_ps[b0 + 2], rin], writes=[rtq])
                P.op("gpsimd", lambda e: e.tensor_tensor(out=mf_, in0=mf_, in1=tp_, op=ALU.add), reads=[rmf, rtp],
                     writes=[rmf])
                P.op("gpsimd", lambda e, i=i: e.tensor_tensor(out=mb_[:, i, :], in0=mf_, in1=tq_, op=ALU.add),
                     reads=[rmf, rtq], writes=[rmb])
                if t >= 1 and i % 2 == 1:
                    tp_t = t - 1
                    out_proj_residual(tp_t, mb[tp_t % 2], r_mb[tp_t % 2], wo, r_wo, 8, xsrc, xt, r_xt,
                                      gg[(tp_t // 4) % 2], r_gg[(tp_t // 4) % 2], tmp, r_tmp, xo, r_xo, ss, r_ss,
                                      ((6, 7),), js=(i // 2,))
        tp_t = NT - 1
        out_proj_residual(tp_t, mb[tp_t % 2], r_mb[tp_t % 2], wo, r_wo, 8, xsrc, xt, r_xt, gg[(tp_t // 4) % 2],
                          r_gg[(tp_t // 4) % 2], tmp, r_tmp, xo, r_xo, ss, r_ss, ((6, 7), (4, 5)))

    def out_proj_residual(t, act, r_act, w, r_wh, nk, xsrc, xt, r_xt, gg, r_gg, tmps, r_tmps, xo, r_xo, sss, r_sss, banks,
                          js=(0, 1, 2, 3)):
        tok0 = t * 512
        for j in js:
            x_, rx = xt[j % 2], r_xt[j % 2]
            o_, ro = xo[j % 2], r_xo[j % 2]
            tmp, r_tmp = tmps[j % 2], r_tmps[j % 2]
            ss, r_ss = sss[j % 2], r_sss[j % 2]
            r0 = tok0 + j * 128
            P.dma("sync", lambda e, x_=x_, r0=r0: e.dma_start(out=x_, in_=xsrc[r0:r0 + 128, :]), writes=[rx], semres=rx)
            bk = banks[j % len(banks)]
            for half in range(2):
                b = bk[half]
                for c in range(nk):
                    P.op("tensor", lambda e, c=c, b=b, j=j, half=half: e.matmul(
                        psum[b][:, :], lhsT=act[:, c, j * 128:(j + 1) * 128], rhs=w[:, c, half * 512:(half + 1) * 512],
                        start=(c == 0), stop=(c == nk - 1)), reads=[r_act, r_wh[half]], writes=[r_ps[b]])
            P.op("scalar", lambda e, o_=o_, bk=bk, ss=ss: e.activation(out=o_[:, 0:512], in_=psum[bk[0]][:, :],
                                                                       func=AF.Square, accum_out=ss[:, 0:1]),
                 reads=[r_ps[bk[0]]], writes=[ro, r_ss])
            P.op("scalar", lambda e, o_=o_, bk=bk, ss=ss: e.activation(out=o_[:, 512:1024], in_=psum[bk[1]][:, :],
                                                                       func=AF.Square, accum_out=ss[:, 1:2]),
                 reads=[r_ps[bk[1]]], writes=[ro, r_ss])
            P.op("vector", lambda e, ss=ss: e.tensor_tensor(out=ss[:, 0:1], in0=ss[:, 0:1], in1=ss[:, 1:2], op=ALU.add),
                 reads=[r_ss], writes=[r_ss])
            rstd_from_ss(ss[:, 0:1], ss[:, 0:1], r_ss, r_ss, D)
            for half in range(2):
                hs = slice(half * 512, (half + 1) * 512)
                P.op("vector", lambda e, half=half, hs=hs, bk=bk, ss=ss, tmp=tmp: e.scalar_tensor_tensor(
                    out=tmp[:, hs], in0=psum[bk[half]][:, :], scalar=ss[:, 0:1], in1=gg[:, hs], op0=ALU.mult,
                    op1=ALU.mult), reads=[r_ps[bk[half]], r_ss, r_gg], writes=[r_tmp])
            P.op("gpsimd", lambda e, o_=o_, x_=x_, tmp=tmp: e.tensor_tensor(out=o_, in0=tmp, in1=x_, op=ALU.add),
                 reads=[r_tmp, rx], writes=[ro])
            P.dma("sync", lambda e, o_=o_, r0=r0: e.dma_start(out=yout[r0:r0 + 128, :], in_=o_), reads=[ro], semres=ro)

    def phase_ffn1(l):
        new_phase()
        NCOL = 2 * D_FF
        wF = AB.take(8, NCOL)
        r_wFp = [P.res(f"wF{i}") for i in range(6)]
        for pi in (0, 2, 3, 1, 4, 5):
            c0, c1 = pi * 1024, min(NCOL, (pi + 1) * 1024)
            load_w(wF[:, :, c0:c1], w_ffn_in[l, :, c0:c1].rearrange("(c p) n -> p c n", p=128), r_wFp[pi])
        hfm = [AB.take(8, 512) for _ in range(2)]
        r_hfm = [P.res("fh0"), P.res("fh1")]
        xh = [AB.take(D) for _ in range(2)]
        r_xh = [P.res("fxh0"), P.res("fxh1")]
        sgt = [AB.take(512) for _ in range(2)]
        r_sgt = [P.res("sgt0"), P.res("sgt1")]
        oc = [AB.take(512) for _ in range(6)]
        r_oc = [P.res(f"foc{i}") for i in range(6)]
        xt = [AFa.take(D) for _ in range(8)]
        r_xt = [P.res(f"fxt{i}") for i in range(8)]
        gs = AFa.take(D)
        r_gs = P.res("fgs")
        sh = AFa.take(D)
        r_sh = P.res("fsh")
        tmp = AFa.take(D)
        r_tmp = P.res("ftmp")
        gtmp = AFa.take(D)
        r_gtmp = P.res("fgtmp")
        ss = AFa.take(4)
        r_ss = P.res("fss")
        rstd = AFa.take(4)
        r_rstd = P.res("frstd")
        r_junk = P.res("fjunk")
        k = 0

        def prep(tt):
            if tt % 4 == 0:
                load_rows(l, tt // 4, None, [(gs, r_gs, "scale", "g_pre_ffn", 4, gtmp, r_gtmp),
                                             (sh, r_sh, "shift", None, 3, None, None)])
            yield from norm_transpose_tile(yout, tt, xt, r_xt, tmp, r_tmp, ss, r_ss, rstd, r_rstd, gs, r_gs, sh, r_sh, tmp,
                                           r_tmp, xh, r_xh, hfm[tt % 2], r_hfm[tt % 2], 0)

        load_x_tile(yout, 0, xt, r_xt)
        load_x_tile(yout, 1, xt, r_xt)
        for _ in prep(0):
            pass
        for t in range(NT):
            slot = t // 4
            tok0 = t * 512
            if t + 2 < NT:
                load_x_tile(yout, t + 2, xt, r_xt)
            h, rh = hfm[t % 2], r_hfm[t % 2]
            gen = prep(t + 1) if t + 1 < NT else iter(())
            for i in range(22):
                if i in (3, 6, 9, 12, 15, 18):
                    next(gen, None)
                b1 = 1 + (2 * k) % 6
                b2 = 1 + (2 * k + 1) % 6
                sg_, rsg = sgt[k % 2], r_sgt[k % 2]
                o, ro = oc[k % 6], r_oc[k % 6]
                k += 1
                for c in range(8):
                    P.op("tensor", lambda e, c=c, b1=b1, i=i, h=h: e.matmul(psum[b1][:, :], lhsT=wF[:, c, i * 128:(i + 1) * 128],
                                                                             rhs=h[:, c, :], start=(c == 0), stop=(c == 7)),
                         reads=[r_wFp[(i * 128) // 1024], rh], writes=[r_ps[b1]])
                for c in range(8):
                    P.op("tensor", lambda e, c=c, b2=b2, i=i, h=h: e.matmul(
                        psum[b2][:, :], lhsT=wF[:, c, D_FF + i * 128: D_FF + (i + 1) * 128], rhs=h[:, c, :],
                        start=(c == 0), stop=(c == 7)), reads=[r_wFp[(D_FF + i * 128) // 1024], rh], writes=[r_ps[b2]])
                P.op("scalar", lambda e, b1=b1, sg_=sg_: e.activation(out=sg_, in_=psum[b1][:, :], func=AF.Silu),
                     reads=[r_ps[b1]], writes=[rsg])
                P.op("vector", lambda e, b2=b2, sg_=sg_, o=o: e.tensor_tensor(out=o, in0=psum[b2][:, :], in1=sg_,
                                                                              op=ALU.mult),
                     reads=[r_ps[b2], rsg], writes=[ro])
                P.dma("sync", lambda e, o=o, i=i, tok0=tok0: e.dma_start(out=act_d[i, :, tok0:tok0 + 512], in_=o),
                      reads=[ro], semres=ro)
            for _ in gen:
                pass

    def phase_ffn2(l):
        new_phase()
        wD = AB.take(22, D)
        r_wD = [P.res("wD0"), P.res("wD1")]
        for hq in range(2):
            cq = slice(hq * 512, (hq + 1) * 512)
            for c0 in range(0, 22, 6):
                c1 = min(22, c0 + 6)
                load_w(wD[:, c0:c1, cq], w_ffn_out[l, c0 * 128:c1 * 128, cq].rearrange("(c p) n -> p c n", p=128),
                       r_wD[hq])
        act = [AB.take(22, 512) for _ in range(2)]
        r_act = [P.res("act0"), P.res("act1")]
        xt = [AFa.take(D) for _ in range(2)]
        r_xt = [P.res("gxt0"), P.res("gxt1")]
        gg = AFa.take(D)
        r_gg = P.res("ggg")
        gtmp = AFa.take(D)
        r_gtmp = P.res("ggtmp")
        tmp = [AFa.take(D) for _ in range(2)]
        r_tmp = [P.res("gtmp2a"), P.res("gtmp2b")]
        xo = [AFa.take(D) for _ in range(2)]
        r_xo = [P.res("gxo0"), P.res("gxo1")]
        ss = [AFa.take(2) for _ in range(2)]
        r_ss = [P.res("gss0"), P.res("gss1")]
        for t in range(NT):
            tok0 = t * 512
            slot = t // 4
            if t % 4 == 0:
                load_rows(l, slot, None, [(gg, r_gg, "gate", "g_post_ffn", 5, gtmp, r_gtmp)])
            a, ra = act[t % 2], r_act[t % 2]

            def ld_act(tt):
                P.dma("sync", lambda e: e.dma_start(out=act[tt % 2], in_=fm_tile(act_d, 0, 22, tt * 512, 512)),
                      writes=[r_act[tt % 2]], semres=r_act[tt % 2])
            if t == 0:
                ld_act(0)
            if t + 1 < NT:
                ld_act(t + 1)
            out_proj_residual(t, a, ra, wD, r_wD, 22, yout, xt, r_xt, gg, r_gg, tmp, r_tmp, xo, r_xo, ss, r_ss,
                              ((0, 1), (2, 3), (4, 5), (6, 7)))

    phases = []
    phase_mod()
    done = (stop_after == "mod")
    for l in range(nl):
        if done:
            break
        xsrc = xin if l == 0 else yout
        for name, fn in (("a", lambda: phase_a(l, xsrc)), ("gates", lambda: phase_gates(l)),
                         ("conv_a", lambda: phase_conv_a(l)), ("conv_s", lambda: phase_conv_s(l)),
                         ("ssd", lambda: phase_ssd(l)), ("fnet", lambda: phase_fnet(l)),
                         ("merge", lambda: phase_merge(l, xsrc)), ("ffn1", lambda: phase_ffn1(l)),
                         ("ffn2", lambda: phase_ffn2(l))):
            fn()
            if stop_after == name:
                done = True
                break
    P.finalize()
    return nc, P


def _dft_tables(coupled):
    bf = ml_dtypes.bfloat16
    tab = np.zeros((5, 4, 2, 128, 16, 512), dtype=bf)
    n = np.arange(SEG, dtype=np.float64)
    k = np.arange(SEG, dtype=np.float64)

    def fill(bi, ang, scale):
        for cs, m in ((0, np.cos(ang) * scale), (1, -np.sin(ang) * scale)):
            m = m.reshape(16, 128, 4, 512).transpose(2, 1, 0, 3)
            tab[bi, :, cs] = m.astype(bf)

    if coupled:
        L = 2 * SEG
        sc = 1.0 / np.sqrt(L * 128.0)
        for bi, (os_, is_) in enumerate(((0, 0), (0, 1), (1, 0), (1, 1))):
            prod = np.outer(n + is_ * SEG, k + os_ * SEG) % L
            fill(bi, 2 * np.pi * prod / L, sc)
    else:
        L = SEG
        sc = 1.0 / np.sqrt(L * 128.0)
        ang = 2 * np.pi * (np.outer(n, k) % L) / L
        fill(0, ang, sc)
        tab[3] = tab[0]
    sc = 1.0 / np.sqrt(SEG * 128.0)
    ang = 2 * np.pi * (np.outer(n, k) % SEG) / SEG
    fill(4, ang, sc)
    return tab


def _host_inputs(inp):
    f32 = np.float32
    bf = ml_dtypes.bfloat16
    shared = {}
    for k in ("w_ada", "b_ada", "g_pre_mix", "g_post_mix", "g_pre_ffn", "g_post_ffn", "g_ssd", "w_in", "w_a_out",
              "w_b_out", "w_c_out", "w_out", "w_ffn_in", "w_ffn_out"):
        shared[k] = np.ascontiguousarray(np.asarray(inp[k], dtype=f32))
    shared["caw"] = np.ascontiguousarray(np.asarray(inp["conv_a_w"], f32).reshape(DEPTH, 31, 4, 128).transpose(0, 3, 2, 1))
    for nm, src in (("cab", "conv_a_b"), ("lng", "ln_a_g"), ("lnb", "ln_a_b")):
        shared[nm] = np.ascontiguousarray(np.asarray(inp[src], f32).reshape(DEPTH, 4, 128).transpose(0, 2, 1))
    shared["csw"] = np.ascontiguousarray(np.asarray(inp["conv_s_w"], f32).reshape(DEPTH, 5, 12, 128).transpose(0, 3, 2, 1))
    shared["csb"] = np.ascontiguousarray(np.asarray(inp["conv_s_b"], f32).reshape(DEPTH, 12, 128).transpose(0, 2, 1))
    shared["dtb"] = np.ascontiguousarray(np.concatenate([np.asarray(inp["dt_bias_f"], f32),
                                                         np.asarray(inp["dt_bias_b"], f32)], axis=1))
    shared["alog"] = np.ascontiguousarray(np.concatenate([np.asarray(inp["a_log_f"], f32),
                                                          np.asarray(inp["a_log_b"], f32)], axis=1))
    shared["dsk"] = np.ascontiguousarray(np.asarray(inp["d_skip"], f32))
    shared["ident"] = np.eye(128, dtype=f32).astype(bf)
    j = np.arange(128)[:, None]
    s = np.arange(128)[None, :]
    masks = np.stack([(j > s), (j < s), (j <= s), (j >= s), np.ones((128, 128), bool)], axis=1).astype(f32)
    shared["masks"] = np.ascontiguousarray(masks)
    c = np.arange(128, dtype=np.float64)
    ang = 2 * np.pi * np.outer(c, c) / 128.0
    shared["csc"] = np.concatenate([np.cos(ang), np.sin(ang)], axis=1).astype(bf)
    tabs = {True: _dft_tables(True), False: _dft_tables(False)}
    xp = np.asarray(inp["x_prompt"], f32)
    xs = np.asarray(inp["x_sample"], f32)
    cp = np.asarray(inp["c_prompt"], f32)
    csm = np.asarray(inp["c_sample"], f32)
    maps = []
    for core in range(8):
        if core < 4:
            xin = np.concatenate([xp[core], xs[core]], axis=0)
            cc = np.stack([cp[core], cp[core], csm[core]], axis=0)
            coupled = True
        else:
            ids = [4 + 3 * (core - 4) + q for q in range(3)]
            xin = np.concatenate([xs[q] for q in ids], axis=0)
            cc = np.stack([csm[q] for q in ids], axis=0)
            coupled = False
        m = dict(shared)
        m["xin"] = np.ascontiguousarray(xin)
        m["cT"] = np.ascontiguousarray(cc.reshape(3, 8, 128).transpose(2, 1, 0))
        m["flag"] = np.full((128, 1), 1.0 if coupled else 0.0, f32)
        m["tab"] = tabs[coupled]
        maps.append(m)
    return maps


_CACHE = {}


def kernel(**inputs):
    maps = _host_inputs(inputs)
    if "nc" not in _CACHE:
        _CACHE["nc"] = build_program()[0]
    nc = _CACHE["nc"]
    res = run_bass_kernel_spmd(nc, maps, core_ids=list(range(8)))
    outs = [np.asarray(r["yout"], dtype=np.float32) for r in res.results]
    y_prompt = np.stack([outs[c][:2 * SEG] for c in range(4)], axis=0)
    ys = [None] * 16
    for c in range(4):
        ys[c] = outs[c][2 * SEG:]
    for c in range(4, 8):
        for q in range(3):
            ys[4 + 3 * (c - 4) + q] = outs[c][q * SEG:(q + 1) * SEG]
    y_sample = np.stack(ys, axis=0)
    return (y_prompt, y_sample)
```

```python
import contextlib
import numpy as np
import ml_dtypes
import concourse.bass as bass
import concourse.mybir as mybir
from concourse.bass_utils import run_bass_kernel_spmd

F32 = mybir.dt.float32
BF16 = mybir.dt.bfloat16
AF = mybir.ActivationFunctionType
ALU = mybir.AluOpType

ENGINES = ("tensor", "scalar", "vector", "gpsimd", "sync")
COMPUTE = ("tensor", "scalar", "vector", "gpsimd")

D = 1024
T = 6144
SEG = 2048
NT = 12
NSUB = 48
DEPTH = 4
D_FF = 2816
IN_COLS = 7200
EPS = 1e-6
O_AV, O_AG, O_Z, O_XBC, O_DT, O_UC, O_GATE = 0, 512, 1024, 2048, 3584, 3616, 4128


class Res:
    __slots__ = ("name", "last_writer", "readers", "sem", "issued")

    def __init__(self, name):
        self.name = name
        self.last_writer = None
        self.readers = []
        self.sem = None
        self.issued = 0


class Op:
    __slots__ = ("eng", "fn", "reads", "writes", "dma", "semres", "deps", "signal", "token", "waits", "bar")

    def __init__(self, eng, fn, reads, writes, dma, semres, bar=False):
        self.eng = eng
        self.fn = fn
        self.reads = reads
        self.writes = writes
        self.dma = dma
        self.semres = semres
        self.deps = ()
        self.signal = False
        self.token = None
        self.waits = ()
        self.bar = bar


class _Rec:
    __slots__ = ("call",)

    def __init__(self):
        self.call = None

    def __getattr__(self, name):
        def f(*a, **k):
            self.call = (name, a, k)
            return None
        return f


class Prog:
    def __init__(self, nc):
        self.nc = nc
        self.ops = []
        self.stack = contextlib.ExitStack()
        self.all_res = []

    def sb(self, name, shape, dtype):
        return self.stack.enter_context(self.nc.sbuf_tensor(name, list(shape), dtype))

    def ps(self, name, shape, dtype=F32):
        return self.stack.enter_context(self.nc.psum_tensor(name, list(shape), dtype))

    def res(self, name=None):
        r = Res(name or f"r{len(self.all_res)}")
        self.all_res.append(r)
        return r

    def op(self, eng, fn, reads=(), writes=()):
        rec = _Rec()
        fn(rec)
        assert rec.call is not None
        self.ops.append(Op(eng, rec.call, tuple(reads), tuple(writes), False, None))

    def dma(self, eng, fn, reads=(), writes=(), semres=None):
        assert semres is not None
        rec = _Rec()
        fn(rec)
        assert rec.call is not None
        self.ops.append(Op(eng, rec.call, tuple(reads), tuple(writes), True, semres))

    def barrier(self):
        for e in ENGINES:
            self.ops.append(Op(e, None, (), (), False, None, bar=True))

    def finalize(self):
        nc = self.nc
        ops = self.ops
        last_on = {e: None for e in COMPUTE}
        i = 0
        n = len(ops)
        while i < n:
            o = ops[i]
            if o.bar:
                for e in COMPUTE:
                    if last_on[e] is not None:
                        ops[last_on[e]].signal = True
                for r in self.all_res:
                    r.last_writer = None
                    r.readers = []
                while i < n and ops[i].bar:
                    i += 1
                continue
            deps = set()
            for r in o.reads:
                if r.last_writer is not None:
                    deps.add(r.last_writer)
            for w in o.writes:
                if w.last_writer is not None:
                    deps.add(w.last_writer)
                for rd in w.readers:
                    deps.add(rd)
            deps.discard(i)
            dl = []
            for j in deps:
                oj = ops[j]
                if oj.eng == o.eng and not oj.dma and not o.dma:
                    if o.eng == "tensor":
                        continue
                    if not any((r.last_writer == j) for r in o.reads):
                        continue
                dl.append(j)
            o.deps = dl
            for j in dl:
                ops[j].signal = True
            for r in o.reads:
                r.readers.append(i)
            for w in o.writes:
                w.last_writer = i
                w.readers = []
            if not o.dma and o.eng in last_on:
                last_on[o.eng] = i
            i += 1
        for e in COMPUTE:
            if last_on[e] is not None:
                ops[last_on[e]].signal = True
        sems = {e: None for e in COMPUTE}
        counts = {e: 0 for e in COMPUTE}
        active = []
        free_sems = []
        sem_final = {}
        bar_snap = {}
        for idx, o in enumerate(ops):
            if o.bar:
                if idx not in bar_snap:
                    snap = [("e", e, sems[e], counts[e]) for e in COMPUTE if counts[e] > 0]
                    snap += [("d", id(r.sem), r.sem, r.issued) for r in active]
                    for r in active:
                        free_sems.append((r.sem, r.issued))
                        r.sem = None
                    active = []
                    j = idx
                    while j < len(ops) and ops[j].bar:
                        bar_snap[j] = snap
                        j += 1
                continue
            if o.dma:
                r = o.semres
                if r.sem is None:
                    if free_sems:
                        r.sem, r.issued = free_sems.pop()
                    else:
                        r.sem = nc.alloc_semaphore(name=f"d_{r.name}")
                        r.issued = 0
                    active.append(r)
                r.issued += 16
                o.token = ("d", r.sem, r.issued)
                sem_final[id(r.sem)] = (r.sem, r.issued)
            elif o.signal:
                if sems[o.eng] is None:
                    sems[o.eng] = nc.alloc_semaphore(name=f"e_{o.eng}")
                counts[o.eng] += 1
                o.token = ("e", o.eng, counts[o.eng])
        waited = {e: {} for e in ENGINES}
        issued_sofar = {}
        for idx, o in enumerate(ops):
            need = {}
            if o.bar:
                for kind, key, semh, val in bar_snap[idx]:
                    need[(kind, key)] = (semh if kind == "d" else sems[key], val)
            else:
                for j in o.deps:
                    kind, key, val = ops[j].token
                    if kind == "d":
                        val = max(val, issued_sofar.get(id(key), 0))
                        k = ("d", id(key))
                        semh = key
                    else:
                        k = ("e", key)
                        semh = sems[key]
                    if need.get(k, (None, 0))[1] < val:
                        need[k] = (semh, val)
            w = []
            wd = waited[o.eng]
            for k, (semh, val) in need.items():
                if wd.get(k, 0) >= val:
                    continue
                wd[k] = val
                w.append((semh, val))
            o.waits = w
            if o.dma:
                issued_sofar[id(o.token[1])] = o.token[2]
        per_eng = {e: [o for o in ops if o.eng == e] for e in ENGINES}
        final = list(sem_final.values())
        final += [(sems[e], counts[e]) for e in COMPUTE if counts[e] > 0]
        self.n_ops = len(ops)
        with nc.Block() as block:
            def make(ename):
                def body(eng):
                    for o in per_eng[ename]:
                        for semh, val in o.waits:
                            eng.wait_ge(semh, val)
                        if o.fn is None:
                            continue
                        name, a, k = o.fn
                        ins = getattr(eng, name)(*a, **k)
                        if o.dma:
                            ins.then_inc(o.token[1], 16)
                        elif o.signal:
                            ins.then_inc(sems[o.eng], 1)
                    if ename == "sync":
                        for semh, val in final:
                            eng.wait_ge(semh, val)
                return body
            block.tensor(make("tensor"))
            block.scalar(make("scalar"))
            block.vector(make("vector"))
            block.gpsimd(make("gpsimd"))
            block.sync(make("sync"))
        self.stack.close()


class Arena:
    def __init__(self, ap2d, n):
        self.ap = ap2d
        self.n = n
        self.off = 0

    def reset(self):
        self.off = 0

    def take(self, *shape):
        size = int(np.prod(shape))
        assert self.off + size <= self.n, (self.off, size, self.n)
        v = self.ap[:, self.off:self.off + size]
        self.off += size
        if len(shape) == 2:
            v = v.rearrange("p (a b) -> p a b", a=shape[0])
        elif len(shape) == 3:
            v = v.rearrange("p (a b c) -> p a b c", a=shape[0], b=shape[1])
        return v


def bc(ap, shape):
    return ap.to_broadcast(list(shape))


def build_program(nl=DEPTH, debug=False, stop_after=None):
    nc = bass.Bass("TRN2", target_bir_lowering=False)

    def din(name, shape, dt=F32):
        return nc.dram_tensor(name, list(shape), dt, kind="ExternalInput").ap()

    skind = "ExternalOutput" if debug else "Internal"

    def dscr(name, shape, dt):
        return nc.dram_tensor(name, list(shape), dt, kind=skind).ap()

    xin = din("xin", [T, D])
    cT_d = din("cT", [128, 8, 3])
    flag_d = din("flag", [128, 1])
    w_ada = din("w_ada", [DEPTH, D, 6 * D])
    b_ada = din("b_ada", [DEPTH, 6 * D])
    gvec = {k: din(k, [DEPTH, D]) for k in ("g_pre_mix", "g_post_mix", "g_pre_ffn", "g_post_ffn", "g_ssd")}
    w_in = din("w_in", [DEPTH, D, IN_COLS])
    caw_d = din("caw", [DEPTH, 128, 4, 31])
    cab_d = din("cab", [DEPTH, 128, 4])
    lng_d = din("lng", [DEPTH, 128, 4])
    lnb_d = din("lnb", [DEPTH, 128, 4])
    w_a_out = din("w_a_out", [DEPTH, 512, D])
    csw_d = din("csw", [DEPTH, 128, 12, 5])
    csb_d = din("csb", [DEPTH, 128, 12])
    dtb_d = din("dtb", [DEPTH, 32])
    alog_d = din("alog", [DEPTH, 32])
    dsk_d = din("dsk", [DEPTH, 16])
    w_b_out = din("w_b_out", [DEPTH, D, D])
    w_c_out = din("w_c_out", [DEPTH, 512, D])
    w_out = din("w_out", [DEPTH, D, D])
    w_ffn_in = din("w_ffn_in", [DEPTH, D, 2 * D_FF])
    w_ffn_out = din("w_ffn_out", [DEPTH, D_FF, D])
    ident_d = din("ident", [128, 128], BF16)
    masks_d = din("masks", [128, 5, 128])
    csc_d = din("csc", [128, 256], BF16)
    tab_d = din("tab", [5, 4, 2, 128, 16, 512], BF16)
    yout = nc.dram_tensor("yout", [T, D], F32, kind="ExternalOutput").ap()

    mod_d = dscr("mod_d", [DEPTH, 3, 6 * D], F32)
    hfm_d = dscr("hfm_d", [8, 128, T], BF16)
    aglu_d = dscr("aglu_d", [4, 128, T], BF16)
    xbc_d = dscr("xbc_d", [12, 128, T], BF16)
    u_d = dscr("u_d", [4, 128, T], BF16)
    zs_d = dscr("zs_d", [T, D], BF16)
    dt_d = dscr("dt_d", [T, 32], F32)
    gates_d = dscr("gates_d", [24, 128, T], BF16)
    acv_d = dscr("acv_d", [4, 128, T], BF16)
    xs_d = dscr("xs_d", [T, D], BF16)
    bt_d = dscr("bt_d", [T, 256], BF16)
    bc_d = dscr("bc_d", [4, 128, T], BF16)
    yf_d = dscr("yf_d", [T, D], F32)
    yfm_d = dscr("yfm_d", [8, 128, T], BF16)
    f_d = dscr("f_d", [4, 128, T], BF16)
    act_d = dscr("act_d", [22, 128, T], BF16)

    P = Prog(nc)
    ABF = 73 * 1024
    AFP = 13 * 1024
    arena_bf_t = P.sb("arena_bf", [128, ABF], BF16)
    arena_f_t = P.sb("arena_f", [128, AFP], F32)
    AB = Arena(arena_bf_t[:], ABF)
    AFa = Arena(arena_f_t[:], AFP)
    ident = P.sb("ident_sb", [128, 128], BF16)
    masks = P.sb("masks_sb", [128, 5, 128], F32)
    flag = P.sb("flag_sb", [128, 1], F32)
    r_const = P.res("const")
    psum = [P.ps(f"psb{i}", [128, 512], F32) for i in range(8)]
    r_ps = [P.res(f"ps{i}") for i in range(8)]

    def fm_tile(dram, c0, c1, t0, n):
        return dram[c0:c1, :, t0:t0 + n].rearrange("c p t -> p c t")

    P.dma("sync", lambda e: e.dma_start(out=ident[:], in_=ident_d), writes=[r_const], semres=r_const)
    P.dma("sync", lambda e: e.dma_start(out=masks[:], in_=masks_d), writes=[r_const], semres=r_const)
    P.dma("sync", lambda e: e.dma_start(out=flag[:], in_=flag_d), writes=[r_const], semres=r_const)
    M_GT, M_LT, M_LE, M_GE, M_ONE = range(5)

    def new_phase():
        P.barrier()
        AB.reset()
        AFa.reset()

    def phase_mod():
        new_phase()
        cT = AFa.take(8, 3)
        r_cT = P.res("cT")
        sil = AFa.take(8, 3)
        r_sil = P.res("sil")
        P.dma("sync", lambda e: e.dma_start(out=cT, in_=cT_d), writes=[r_cT], semres=r_cT)
        P.op("scalar", lambda e: e.activation(out=sil, in_=cT, func=AF.Silu), reads=[r_cT], writes=[r_sil])
        NB = 4
        wbuf = [AFa.take(8, 256) for _ in range(NB)]
        r_w = [P.res(f"wada{i}") for i in range(NB)]
        brow = [AFa.take(256) for _ in range(2)]
        mrow = [AFa.take(256) for _ in range(2)]
        r_b = [P.res("brow0"), P.res("brow1")]
        r_m = [P.res("mrow0"), P.res("mrow1")]
        k = 0
        for l in range(nl):
            for ct in range(24):
                wb, rw = wbuf[k % NB], r_w[k % NB]
                mr, rm = mrow[k % 2], r_m[k % 2]
                br, rb = brow[k % 2], r_b[k % 2]
                pb = k % 2
                q = "sync" if k % 2 == 0 else "gpsimd"
                k += 1
                src = w_ada[l, :, ct * 256:(ct + 1) * 256].rearrange("(c p) n -> p c n", p=128)
                P.dma(q, lambda e, wb=wb, src=src: e.dma_start(out=wb, in_=src), writes=[rw], semres=rw)
                bsrc = b_ada[l:l + 1, ct * 256:(ct + 1) * 256].partition_broadcast(3)
                P.dma("sync", lambda e, bsrc=bsrc, br=br: e.dma_start(out=br[0:3, :], in_=bsrc), writes=[rb], semres=rb)
                for c in range(8):
                    P.op("tensor", lambda e, c=c, wb=wb, pb=pb: e.matmul(psum[pb][0:3, 0:256], lhsT=sil[:, c, :],
                                                                          rhs=wb[:, c, :], start=(c == 0), stop=(c == 7)),
                         reads=[r_sil, rw], writes=[r_ps[pb]])
                P.op("vector", lambda e, mr=mr, br=br, pb=pb: e.tensor_tensor(out=mr[0:3, :], in0=psum[pb][0:3, 0:256],
                                                                               in1=br[0:3, :], op=ALU.add),
                     reads=[r_ps[pb], rb], writes=[rm])
                dst = mod_d[l, :, ct * 256:(ct + 1) * 256]
                P.dma("sync", lambda e, mr=mr, dst=dst: e.dma_start(out=dst, in_=mr[0:3, :]), reads=[rm], semres=rm)

    def load_rows(l, slot, arena_tiles, spec):
        for dst, rr, kind, gname, part, tmp, rtmp in spec:
            msrc = mod_d[l, slot:slot + 1, part * D:(part + 1) * D].partition_broadcast(128)
            if kind == "shift":
                P.dma("sync", lambda e, dst=dst, msrc=msrc: e.dma_start(out=dst, in_=msrc), writes=[rr], semres=rr)
                continue
            gsrc = gvec[gname][l:l + 1, :].partition_broadcast(128)
            P.dma("sync", lambda e, dst=dst, msrc=msrc: e.dma_start(out=dst, in_=msrc), writes=[rr], semres=rr)
            P.dma("sync", lambda e, tmp=tmp, gsrc=gsrc: e.dma_start(out=tmp, in_=gsrc), writes=[rtmp], semres=rtmp)
            if kind == "scale":
                P.op("vector", lambda e, dst=dst, tmp=tmp: e.scalar_tensor_tensor(out=dst, in0=dst, scalar=1.0, in1=tmp,
                                                                                  op0=ALU.add, op1=ALU.mult),
                     reads=[rr, rtmp], writes=[rr])
            else:
                P.op("gpsimd", lambda e, dst=dst, tmp=tmp: e.tensor_tensor(out=dst, in0=dst, in1=tmp, op=ALU.mult),
                     reads=[rr, rtmp], writes=[rr])

    def load_w(dst, src, rr):
        P.dma("gpsimd", lambda e: e.dma_start(out=dst, in_=src), writes=[rr], semres=rr)

    def rstd_from_ss(ss, rstd, r_ss, r_rstd, n_feat):
        P.op("scalar", lambda e: e.activation(out=rstd, in_=ss, func=AF.Ln, scale=1.0 / n_feat, bias=EPS),
             reads=[r_ss], writes=[r_rstd])
        P.op("scalar", lambda e: e.activation(out=rstd, in_=rstd, func=AF.Exp, scale=-0.5),
             reads=[r_rstd], writes=[r_rstd])

    def load_x_tile(xsrc, t, xt, r_xt):
        tok0 = t * 512
        for j in range(4):
            q = (t % 2) * 4 + j
            P.dma("sync", lambda e, j=j, q=q: e.dma_start(out=xt[q], in_=xsrc[tok0 + j * 128: tok0 + (j + 1) * 128, :]),
                  writes=[r_xt[q]], semres=r_xt[q])

    def norm_transpose_tile(xsrc, t, xt, r_xt, junk, r_junk, ss, r_ss, rstd, r_rstd, gs, r_gs, sh, r_sh, tmp, r_tmp,
                            xh, r_xh, hfm, r_hfm, pbank):
        for j in range(4):
            q = (t % 2) * 4 + j
            P.op("scalar", lambda e, j=j, q=q: e.activation(out=junk, in_=xt[q], func=AF.Square, accum_out=ss[:, j:j + 1]),
                 reads=[r_xt[q]], writes=[r_junk, r_ss])
        rstd_from_ss(ss, rstd, r_ss, r_rstd, D)
        yield

        def part_a(j):
            q = (t % 2) * 4 + j
            P.op("vector", lambda e: e.scalar_tensor_tensor(out=tmp, in0=xt[q], scalar=rstd[:, j:j + 1], in1=gs,
                                                            op0=ALU.mult, op1=ALU.mult),
                 reads=[r_xt[q], r_rstd, r_gs], writes=[r_tmp])
            P.op("vector", lambda e: e.tensor_tensor(out=xh[j % 2], in0=tmp, in1=sh, op=ALU.add),
                 reads=[r_tmp, r_sh], writes=[r_xh[j % 2]])

        def part_b(j):
            pT = psum[pbank][:, :].bitcast(BF16)
            for c in range(8):
                P.op("tensor", lambda e, c=c: e.transpose(out=pT[:, c * 128:(c + 1) * 128],
                                                          in_=xh[j % 2][:, c * 128:(c + 1) * 128], identity=ident[:]),
                     reads=[r_xh[j % 2], r_const], writes=[r_ps[pbank]])
            P.op("scalar", lambda e: e.copy(out=hfm[:, :, j * 128:(j + 1) * 128],
                                            in_=pT.rearrange("p (c t) -> p c t", c=8)),
                 reads=[r_ps[pbank]], writes=[r_hfm])

        part_a(0)
        yield
        for j in range(4):
            part_b(j)
            if j < 3:
                part_a(j + 1)
            yield

    def phase_a(l, xsrc):
        new_phase()
        NCOL = O_GATE
        wA = AB.take(8, NCOL)
        bounds = [0, 1024, 2048, 3072, NCOL]
        r_wAp = [P.res(f"wA{i}") for i in range(4)]
        for pi in (0, 2, 3, 1):
            c0, c1 = bounds[pi], bounds[pi + 1]
            load_w(wA[:, :, c0:c1], w_in[l, :, c0:c1].rearrange("(c p) n -> p c n", p=128), r_wAp[pi])

        def rwA(col):
            return r_wAp[min(col // 1024, 3)]
        hfm = [AB.take(8, 512) for _ in range(2)]
        r_hfm = [P.res("hfm0"), P.res("hfm1")]
        xh = [AB.take(D) for _ in range(2)]
        r_xh = [P.res("xh0"), P.res("xh1")]
        sg = AB.take(4, 512)
        r_sg = P.res("sg")
        oc = [AB.take(512) for _ in range(8)]
        r_oc = [P.res(f"oc{i}") for i in range(8)]
        zs = [AB.take(D) for _ in range(2)]
        r_zs = [P.res("zs0"), P.res("zs1")]
        xt = [AFa.take(D) for _ in range(8)]
        r_xt = [P.res(f"xt{i}") for i in range(8)]
        gs = AFa.take(D)
        r_gs = P.res("gs")
        sh = AFa.take(D)
        r_sh = P.res("sh")
        tmp = AFa.take(D)
        r_tmp = P.res("tmp")
        gtmp = AFa.take(D)
        r_gtmp = P.res("gtmp")
        ss = AFa.take(4)
        r_ss = P.res("ss")
        rstd = AFa.take(4)
        r_rstd = P.res("rstd")
        dtr = [AFa.take(32) for _ in range(2)]
        r_dtr = [P.res("dtr0"), P.res("dtr1")]
        r_junk = P.res("junk")
        ocn = [0]
        pb = [0]

        def next_oc():
            i = ocn[0] % 8
            ocn[0] += 1
            return oc[i], r_oc[i]

        def next_pb():
            i = 1 + (pb[0] % 6)
            pb[0] += 1
            return i

        def prep(tt):
            if tt % 4 == 0:
                load_rows(l, tt // 4, None, [(gs, r_gs, "scale", "g_pre_mix", 1, gtmp, r_gtmp),
                                             (sh, r_sh, "shift", None, 0, None, None)])
            hh, rhh = hfm[tt % 2], r_hfm[tt % 2]
            yield from norm_transpose_tile(xsrc, tt, xt, r_xt, tmp, r_tmp, ss, r_ss, rstd, r_rstd, gs, r_gs, sh, r_sh, tmp,
                                           r_tmp, xh, r_xh, hh, rhh, 0)
            P.dma("sync", lambda e: e.dma_start(out=fm_tile(hfm_d, 0, 8, tt * 512, 512), in_=hh), reads=[rhh], semres=rhh)

        load_x_tile(xsrc, 0, xt, r_xt)
        load_x_tile(xsrc, 1, xt, r_xt)
        for _ in prep(0):
            pass
        for t in range(NT):
            slot = t // 4
            tok0 = t * 512
            if t + 2 < NT:
                load_x_tile(xsrc, t + 2, xt, r_xt)
            h, rh = hfm[t % 2], r_hfm[t % 2]
            nchunk = [0]
            gen = prep(t + 1) if t + 1 < NT else iter(())

            def fm_chunk(col0, epi, gen=gen):
                nchunk[0] += 1
                if nchunk[0] in (4, 7, 10, 13, 16, 19, 22):
                    next(gen, None)
                b = next_pb()
                for c in range(8):
                    P.op("tensor", lambda e, c=c, b=b: e.matmul(psum[b][:, :], lhsT=wA[:, c, col0:col0 + 128],
                                                                 rhs=h[:, c, :], start=(c == 0), stop=(c == 7)),
                         reads=[rwA(col0), rh], writes=[r_ps[b]])
                epi(b)

            for i in range(4):
                fm_chunk(O_AG + i * 128, lambda b, i=i: P.op(
                    "scalar", lambda e: e.activation(out=sg[:, i, :], in_=psum[b][:, :], func=AF.Sigmoid),
                    reads=[r_ps[b]], writes=[r_sg]))
            for i in range(4):
                def epi(b, i=i):
                    o, ro = next_oc()
                    P.op("vector", lambda e: e.tensor_tensor(out=o, in0=psum[b][:, :], in1=sg[:, i, :], op=ALU.mult),
                         reads=[r_ps[b], r_sg], writes=[ro])
                    P.dma("sync", lambda e: e.dma_start(out=aglu_d[i, :, tok0:tok0 + 512], in_=o), reads=[ro], semres=ro)
                fm_chunk(O_AV + i * 128, epi)
            for i in range(12):
                def epi(b, i=i):
                    o, ro = next_oc()
                    P.op("scalar", lambda e: e.copy(out=o, in_=psum[b][:, :]), reads=[r_ps[b]], writes=[ro])
                    P.dma("sync", lambda e: e.dma_start(out=xbc_d[i, :, tok0:tok0 + 512], in_=o), reads=[ro], semres=ro)
                fm_chunk(O_XBC + i * 128, epi)
            for i in range(4):
                def epi(b, i=i):
                    o, ro = next_oc()
                    P.op("vector", lambda e: e.tensor_copy(out=o, in_=psum[b][:, :]), reads=[r_ps[b]], writes=[ro])
                    P.dma("sync", lambda e: e.dma_start(out=u_d[i, :, tok0:tok0 + 512], in_=o), reads=[ro], semres=ro)
                fm_chunk(O_UC + i * 128, epi)
            for j in range(4):
                z, rz = zs[j % 2], r_zs[j % 2]
                for half in range(2):
                    b = next_pb()
                    for c in range(8):
                        P.op("tensor", lambda e, c=c, b=b, j=j, half=half: e.matmul(
                            psum[b][:, :], lhsT=h[:, c, j * 128:(j + 1) * 128],
                            rhs=wA[:, c, O_Z + half * 512: O_Z + (half + 1) * 512], start=(c == 0), stop=(c == 7)),
                            reads=[rwA(O_Z), rh], writes=[r_ps[b]])
                    P.op("scalar", lambda e, b=b, z=z, half=half: e.activation(out=z[:, half * 512:(half + 1) * 512],
                                                                               in_=psum[b][:, :], func=AF.Silu),
                         reads=[r_ps[b]], writes=[rz])
                P.dma("sync", lambda e, z=z, j=j: e.dma_start(out=zs_d[tok0 + j * 128: tok0 + (j + 1) * 128, :], in_=z),
                      reads=[rz], semres=rz)
                b = next_pb()
                dd, rd = dtr[j % 2], r_dtr[j % 2]
                for c in range(8):
                    P.op("tensor", lambda e, c=c, b=b, j=j: e.matmul(
                        psum[b][:, 0:32], lhsT=h[:, c, j * 128:(j + 1) * 128], rhs=wA[:, c, O_DT:O_DT + 32],
                        start=(c == 0), stop=(c == 7)), reads=[rwA(O_DT), rh], writes=[r_ps[b]])
                P.op("vector", lambda e, b=b, dd=dd: e.tensor_copy(out=dd, in_=psum[b][:, 0:32]),
                     reads=[r_ps[b]], writes=[rd])
                P.dma("sync", lambda e, dd=dd, j=j: e.dma_start(out=dt_d[tok0 + j * 128: tok0 + (j + 1) * 128, :], in_=dd),
                      reads=[rd], semres=rd)
            for _ in gen:
                pass

    def phase_gates(l):
        new_phase()
        wG = AB.take(8, 3072)
        r_wGp = [P.res(f"wG{i}") for i in range(6)]
        for pi in range(6):
            c0 = pi * 512
            load_w(wG[:, :, c0:c0 + 512], w_in[l, :, O_GATE + c0:O_GATE + c0 + 512].rearrange("(c p) n -> p c n", p=128),
                   r_wGp[pi])
        hfm = [AB.take(8, 512) for _ in range(2)]
        r_hfm = [P.res("ghfm0"), P.res("ghfm1")]
        oc = [AB.take(512) for _ in range(8)]
        r_oc = [P.res(f"goc{i}") for i in range(8)]
        k = 0

        def ldh(t):
            h, rh = hfm[t % 2], r_hfm[t % 2]
            P.dma("sync", lambda e: e.dma_start(out=h, in_=fm_tile(hfm_d, 0, 8, t * 512, 512)), writes=[rh], semres=rh)

        ldh(0)
        for t in range(NT):
            tok0 = t * 512
            h, rh = hfm[t % 2], r_hfm[t % 2]
            if t + 1 < NT:
                ldh(t + 1)
            for i in range(24):
                b = k % 8
                o, ro = oc[k % 8], r_oc[k % 8]
                k += 1
                for c in range(8):
                    P.op("tensor", lambda e, c=c, b=b, i=i, h=h: e.matmul(psum[b][:, :], lhsT=wG[:, c, i * 128:(i + 1) * 128],
                                                                           rhs=h[:, c, :], start=(c == 0), stop=(c == 7)),
                         reads=[r_wGp[i // 4], rh], writes=[r_ps[b]])
                P.op("scalar", lambda e, b=b, o=o: e.activation(out=o, in_=psum[b][:, :], func=AF.Sigmoid),
                     reads=[r_ps[b]], writes=[ro])
                P.dma("sync", lambda e, o=o, i=i, tok0=tok0: e.dma_start(out=gates_d[i, :, tok0:tok0 + 512], in_=o),
                      reads=[ro], semres=ro)

    def load_halo(dst, rr, dram, C, t, hw):
        tok0 = t * 512
        seg_start = (t % 4 == 0)
        seg_end = (t % 4 == 3)
        lo = tok0 - hw
        hi = tok0 + 512 + hw
        d0 = 0
        if seg_start and t != 4:
            lo = tok0
            d0 = hw
        if seg_end and t != 3:
            hi = tok0 + 512
        if lo > tok0 - hw:
            P.op("gpsimd", lambda e: e.memset(dst[:, :, 0:hw], 0.0), writes=[rr])
        if hi < tok0 + 512 + hw:
            P.op("gpsimd", lambda e: e.memset(dst[:, :, 512 + hw:512 + 2 * hw], 0.0), writes=[rr])
        P.dma("sync", lambda e: e.dma_start(out=dst[:, :, d0:d0 + (hi - lo)], in_=fm_tile(dram, 0, C, lo, hi - lo)),
              writes=[rr], semres=rr)
        if t == 4:
            P.op("gpsimd", lambda e: e.tensor_scalar(out=dst[:, :, 0:hw], in0=dst[:, :, 0:hw], scalar1=flag[:, 0:1],
                                                     scalar2=None, op0=ALU.mult), reads=[rr, r_const], writes=[rr])
        if t == 3:
            P.op("gpsimd", lambda e: e.tensor_scalar(out=dst[:, :, 512 + hw:512 + 2 * hw],
                                                     in0=dst[:, :, 512 + hw:512 + 2 * hw], scalar1=flag[:, 0:1],
                                                     scalar2=None, op0=ALU.mult), reads=[rr, r_const], writes=[rr])

    def phase_conv_a(l):
        new_phase()
        caw = AFa.take(4, 31)
        cab = AFa.take(4)
        lng = AFa.take(4)
        lnb = AFa.take(4)
        r_par = P.res("cpar")
        for dst, src in ((caw, caw_d[l]), (cab, cab_d[l]), (lng, lng_d[l]), (lnb, lnb_d[l])):
            P.dma("sync", lambda e, dst=dst, src=src: e.dma_start(out=dst, in_=src), writes=[r_par], semres=r_par)
        dg = AB.take(4 * 31, 128)
        r_dg = P.res("dg")
        identf = AFa.take(128)
        r_idf = P.res("identf")
        P.op("vector", lambda e: e.tensor_copy(out=identf, in_=ident[:]), reads=[r_const], writes=[r_idf])
        for i in range(4):
            P.op("vector", lambda e, i=i: e.tensor_tensor(
                out=dg[:, i * 31:(i + 1) * 31, :], in0=bc(identf.unsqueeze(1), [128, 31, 128]),
                in1=bc(caw[:, i, :].unsqueeze(2), [128, 31, 128]), op=ALU.mult), reads=[r_idf, r_par], writes=[r_dg])
        ain = [AB.take(4, 542) for _ in range(2)]
        r_ain = [P.res("ain0"), P.res("ain1")]
        acc = AFa.take(4, 512)
        r_acc = [P.res(f"acc{i}") for i in range(4)]
        sq = [AFa.take(512) for _ in range(2)]
        r_sq = [P.res("sq0"), P.res("sq1")]
        mean = AFa.take(512)
        r_mean = P.res("mean")
        rs = AFa.take(512)
        r_rs = P.res("rs")
        xc = [AFa.take(512) for _ in range(2)]
        r_xc = [P.res("xc0"), P.res("xc1")]
        ob = [AB.take(512) for _ in range(8)]
        r_ob = [P.res(f"ob{i}") for i in range(8)]
        ones = masks[:, M_ONE, :]
        nb = 0
        load_halo(ain[0], r_ain[0], aglu_d, 4, 0, 15)
        for t in range(NT):
            tok0 = t * 512
            a, ra = ain[t % 2], r_ain[t % 2]
            if t + 1 < NT:
                load_halo(ain[(t + 1) % 2], r_ain[(t + 1) % 2], aglu_d, 4, t + 1, 15)
            for i in range(4):
                b = 2 + nb % 6
                nb += 1
                for k in range(31):
                    P.op("tensor", lambda e, i=i, k=k, a=a, b=b: e.matmul(psum[b][:, :], lhsT=dg[:, i * 31 + k, :],
                                                                           rhs=a[:, i, k:k + 512], start=(k == 0),
                                                                           stop=(k == 30)),
                         reads=[r_dg, ra], writes=[r_ps[b]])
                P.op("scalar", lambda e, i=i, b=b: e.activation(out=acc[:, i, :], in_=psum[b][:, :], func=AF.Identity,
                                                                bias=cab[:, i:i + 1]),
                     reads=[r_ps[b], r_par], writes=[r_acc[i]])
            for i in range(4):
                P.op("tensor", lambda e, i=i: e.matmul(psum[0][:, :], lhsT=ones, rhs=acc[:, i, :], start=(i == 0),
                                                       stop=(i == 3)), reads=[r_acc[i], r_const], writes=[r_ps[0]])
            for i in range(4):
                P.op("gpsimd", lambda e, i=i: e.tensor_tensor(out=sq[i % 2], in0=acc[:, i, :], in1=acc[:, i, :],
                                                              op=ALU.mult), reads=[r_acc[i]], writes=[r_sq[i % 2]])
                P.op("tensor", lambda e, i=i: e.matmul(psum[1][:, :], lhsT=ones, rhs=sq[i % 2], start=(i == 0),
                                                       stop=(i == 3)), reads=[r_sq[i % 2], r_const], writes=[r_ps[1]])
            P.op("vector", lambda e: e.tensor_scalar(out=mean, in0=psum[0][:, :], scalar1=1.0 / 512, scalar2=None,
                                                     op0=ALU.mult), reads=[r_ps[0]], writes=[r_mean])
            P.op("vector", lambda e: e.tensor_tensor(out=rs, in0=mean, in1=mean, op=ALU.mult), reads=[r_mean], writes=[r_rs])
            P.op("vector", lambda e: e.scalar_tensor_tensor(out=rs, in0=psum[1][:, :], scalar=1.0 / 512, in1=rs,
                                                            op0=ALU.mult, op1=ALU.subtract),
                 reads=[r_ps[1], r_rs], writes=[r_rs])
            P.op("scalar", lambda e: e.activation(out=rs, in_=rs, func=AF.Ln, bias=EPS), reads=[r_rs], writes=[r_rs])
            P.op("scalar", lambda e: e.activation(out=rs, in_=rs, func=AF.Exp, scale=-0.5), reads=[r_rs], writes=[r_rs])
            for i in range(4):
                o, ro = ob[(t * 4 + i) % 8], r_ob[(t * 4 + i) % 8]
                x_, rx_ = xc[i % 2], r_xc[i % 2]
                P.op("vector", lambda e, i=i, x_=x_: e.tensor_tensor(out=x_, in0=acc[:, i, :], in1=mean, op=ALU.subtract),
                     reads=[r_acc[i], r_mean], writes=[rx_])
                P.op("gpsimd", lambda e, x_=x_: e.tensor_tensor(out=x_, in0=x_, in1=rs, op=ALU.mult), reads=[rx_, r_rs],
                     writes=[rx_])
                P.op("scalar", lambda e, i=i, o=o, x_=x_: e.activation(out=o, in_=x_, func=AF.Silu, scale=lng[:, i:i + 1],
                                                                       bias=lnb[:, i:i + 1]),
                     reads=[rx_, r_par], writes=[ro])
                P.dma("sync", lambda e, i=i, o=o, tok0=tok0: e.dma_start(out=acv_d[i, :, tok0:tok0 + 512], in_=o),
                      reads=[ro], semres=ro)

    def phase_conv_s(l):
        new_phase()
        csw = AFa.take(12, 5)
        csb = AFa.take(12)
        r_par = P.res("spar")
        P.dma("sync", lambda e: e.dma_start(out=csw, in_=csw_d[l]), writes=[r_par], semres=r_par)
        P.dma("sync", lambda e: e.dma_start(out=csb, in_=csb_d[l]), writes=[r_par], semres=r_par)
        dgs = AB.take(60, 128)
        r_dgs = P.res("dgs")
        identf = AFa.take(128)
        r_idf = P.res("sidentf")
        P.op("vector", lambda e: e.tensor_copy(out=identf, in_=ident[:]), reads=[r_const], writes=[r_idf])
        for i in range(12):
            P.op("vector", lambda e, i=i: e.tensor_tensor(
                out=dgs[:, i * 5:(i + 1) * 5, :], in0=bc(identf.unsqueeze(1), [128, 5, 128]),
                in1=bc(csw[:, i, :].unsqueeze(2), [128, 5, 128]), op=ALU.mult), reads=[r_idf, r_par], writes=[r_dgs])
        xi = [AB.take(12, 516) for _ in range(2)]
        r_xi = [P.res("xi0"), P.res("xi1")]
        xo = [AB.take(12, 512) for _ in range(2)]
        r_xo = [[P.res(f"xo{b}_{i}") for i in range(12)] for b in range(2)]
        xs = [AB.take(D) for _ in range(2)]
        r_xs = [P.res("xs0"), P.res("xs1")]
        bt = [AB.take(256) for _ in range(2)]
        r_bt = [P.res("bt0"), P.res("bt1")]
        nb = 0
        load_halo(xi[0], r_xi[0], xbc_d, 12, 0, 2)
        for t in range(NT):
            tok0 = t * 512
            a, ra = xi[t % 2], r_xi[t % 2]
            o, ro = xo[t % 2], r_xo[t % 2]
            if t + 1 < NT:
                load_halo(xi[(t + 1) % 2], r_xi[(t + 1) % 2], xbc_d, 12, t + 1, 2)
            for i in range(12):
                b = 4 + nb % 4
                nb += 1
                for k in range(5):
                    P.op("tensor", lambda e, i=i, k=k, a=a, b=b: e.matmul(psum[b][:, :], lhsT=dgs[:, i * 5 + k, :],
                                                                           rhs=a[:, i, k:k + 512], start=(k == 0),
                                                                           stop=(k == 4)),
                         reads=[r_dgs, ra], writes=[r_ps[b]])
                P.op("scalar", lambda e, i=i, o=o, b=b: e.activation(out=o[:, i, :], in_=psum[b][:, :], func=AF.Silu,
                                                                     bias=csb[:, i:i + 1]),
                     reads=[r_ps[b], r_par], writes=[ro[i]])
            P.dma("sync", lambda e, o=o, tok0=tok0: e.dma_start(out=fm_tile(bc_d, 0, 4, tok0, 512), in_=o[:, 8:12, :]),
                  reads=ro[8:12], semres=ro[8])
            for j in range(4):
                pT = psum[j % 2][:, :].bitcast(BF16)
                x_, rx = xs[j % 2], r_xs[j % 2]
                for c in range(8):
                    P.op("tensor", lambda e, c=c, j=j, o=o, pT=pT: e.transpose(out=pT[:, c * 128:(c + 1) * 128],
                                                                               in_=o[:, c, j * 128:(j + 1) * 128],
                                                                               identity=ident[:]),
                         reads=[ro[c], r_const], writes=[r_ps[j % 2]])
                P.op("vector", lambda e, x_=x_, pT=pT: e.tensor_copy(out=x_, in_=pT), reads=[r_ps[j % 2]], writes=[rx])
                P.dma("sync", lambda e, x_=x_, j=j, tok0=tok0: e.dma_start(
                    out=xs_d[tok0 + j * 128: tok0 + (j + 1) * 128, :], in_=x_), reads=[rx], semres=rx)
                pB = psum[2 + j % 2][:, :].bitcast(BF16)
                b_, rb = bt[j % 2], r_bt[j % 2]
                for c in range(2):
                    P.op("tensor", lambda e, c=c, j=j, o=o, pB=pB: e.transpose(out=pB[:, c * 128:(c + 1) * 128],
                                                                               in_=o[:, 8 + c, j * 128:(j + 1) * 128],
                                                                               identity=ident[:]),
                         reads=[ro[8 + c], r_const], writes=[r_ps[2 + j % 2]])
                P.op("scalar", lambda e, b_=b_, pB=pB: e.copy(out=b_, in_=pB[:, 0:256]), reads=[r_ps[2 + j % 2]],
                     writes=[rb])
                P.dma("sync", lambda e, b_=b_, j=j, tok0=tok0: e.dma_start(
                    out=bt_d[tok0 + j * 128: tok0 + (j + 1) * 128, :], in_=b_), reads=[rb], semres=rb)

    def phase_ssd(l):
        new_phase()
        dtb = AFa.take(32)
        arow = AFa.take(32)
        dsk = AFa.take(16)
        r_par = P.res("dpar")
        P.dma("sync", lambda e: e.dma_start(out=dtb, in_=dtb_d[l:l + 1, :].partition_broadcast(128)), writes=[r_par],
              semres=r_par)
        P.dma("sync", lambda e: e.dma_start(out=arow, in_=alog_d[l:l + 1, :].partition_broadcast(128)), writes=[r_par],
              semres=r_par)
        P.dma("sync", lambda e: e.dma_start(out=dsk, in_=dsk_d[l:l + 1, :].partition_broadcast(128)), writes=[r_par],
              semres=r_par)
        P.op("scalar", lambda e: e.activation(out=arow, in_=arow, func=AF.Exp), reads=[r_par], writes=[r_par])
        P.op("vector", lambda e: e.tensor_scalar(out=arow, in0=arow, scalar1=-1.0, scalar2=None, op0=ALU.mult),
             reads=[r_par], writes=[r_par])
        gsr = AFa.take(D)
        r_gsr = P.res("gsr")
        P.dma("sync", lambda e: e.dma_start(out=gsr, in_=gvec["g_ssd"][l:l + 1, :].partition_broadcast(128)),
              writes=[r_gsr], semres=r_gsr)
        r_yfd = [P.res(f"yfd{i}") for i in range(NSUB)]

        class S:
            pass

        def mk(d):
            b = S()
            n = f"d{d}"
            b.H = AFa.take(2, 512); b.r_H = P.res(n + "H")
            b.R = AFa.take(16, 128); b.r_R = P.res(n + "R")
            b.ytmp = AFa.take(D); b.r_ytmp = P.res(n + "ytmp")
            b.yfl = AFa.take(D); b.r_yfl = P.res(n + "yfl")
            b.small = [AFa.take(8, 16) for _ in range(2)]
            b.r_sm = [[P.res(n + f"sm{q}_{i}") for i in range(8)] for q in range(2)]
            b.cst = AFa.take(32); b.r_cst = P.res(n + "cst")
            b.dtr = [AFa.take(32) for _ in range(3)]; b.r_dtr = [P.res(n + f"dtr{i}") for i in range(3)]
            b.ss1 = AFa.take(1); b.r_ss1 = P.res(n + "ss1")
            b.Hb = AB.take(2, 512); b.r_Hb = P.res(n + "Hb")
            b.xs = [AB.take(D) for _ in range(3)]; b.r_xs = [P.res(n + f"xs{i}") for i in range(3)]
            b.bt = [AB.take(256) for _ in range(3)]; b.r_bt = [P.res(n + f"bt{i}") for i in range(3)]
            b.bcf = [AB.take(4, 128) for _ in range(3)]; b.r_bcf = [P.res(n + f"bcf{i}") for i in range(3)]
            b.zt = [AB.take(D) for _ in range(3)]; b.r_zt = [P.res(n + f"zt{i}") for i in range(3)]
            b.xdt = AB.take(D); b.r_xdt = P.res(n + "xdt")
            b.xw = AB.take(D); b.r_xw = P.res(n + "xw")
            b.Lm = AB.take(16, 128); b.r_Lm = P.res(n + "Lm")
            b.Mm = AB.take(16, 128); b.r_Mm = P.res(n + "Mm")
            b.smk = AB.take(2, 128); b.r_smk = P.res(n + "smk")
            b.stb = [AB.take(D) for _ in range(2)]; b.r_stb = [P.res(n + "stb0"), P.res(n + "stb1")]
            b.s16 = [AB.take(2, 16) for _ in range(2)]
            b.r_s16 = [[P.res(n + f"s16_{q}_{i}") for i in range(2)] for q in range(2)]
            b.ydg = [AB.take(D) for _ in range(2)]; b.r_ydg = [P.res(n + "ydg0"), P.res(n + "ydg1")]
            b.yn = AB.take(D); b.r_yn = P.res(n + "yn")
            b.yfm = [AB.take(8, 128) for _ in range(2)]; b.r_yfm = [P.res(n + "yfm0"), P.res(n + "yfm1")]
            return b

        BUF = [mk(0), mk(1)]
        HALF = NSUB // 2

        def chunk_of(d, n):
            return n if d == 0 else NSUB - 1 - n

        def loads(d, n):
            B = BUF[d]
            ci = chunk_of(d, n)
            tok0 = ci * 128
            q = n % 3
            P.dma("sync", lambda e: e.dma_start(out=B.dtr[q], in_=dt_d[tok0:tok0 + 128, :]), writes=[B.r_dtr[q]],
                  semres=B.r_dtr[q])
            P.dma("sync", lambda e: e.dma_start(out=B.bcf[q], in_=fm_tile(bc_d, 0, 4, tok0, 128)), writes=[B.r_bcf[q]],
                  semres=B.r_bcf[q])
            P.dma("sync", lambda e: e.dma_start(out=B.xs[q], in_=xs_d[tok0:tok0 + 128, :]), writes=[B.r_xs[q]],
                  semres=B.r_xs[q])
            P.dma("sync", lambda e: e.dma_start(out=B.bt[q], in_=bt_d[tok0:tok0 + 128, :]), writes=[B.r_bt[q]],
                  semres=B.r_bt[q])
            if n >= HALF:
                P.dma("sync", lambda e: e.dma_start(out=B.zt[q], in_=zs_d[tok0:tok0 + 128, :]), writes=[B.r_zt[q]],
                      semres=B.r_zt[q])

        def names(d, n):
            B = BUF[d]
            v = S()
            q = n % 3
            p2 = n % 2
            v.B = B
            v.ci = chunk_of(d, n)
            v.tok0 = v.ci * 128
            pb = 4 * d
            v.L0, v.L1, v.X, v.Y = pb, pb + 1, pb + 2, pb + 3
            v.x_, v.rx = B.xs[q], B.r_xs[q]
            v.b_, v.rb = B.bt[q], B.r_bt[q]
            v.f_, v.rf = B.bcf[q], B.r_bcf[q]
            v.dr, v.rdr = B.dtr[q], B.r_dtr[q]
            v.z_, v.rz = B.zt[q], B.r_zt[q]
            small = B.small[p2]
            rs_ = B.r_sm[p2]
            v.dt_, v.a_, v.d1_, v.dtw_ = small[:, 0, :], small[:, 1, :], small[:, 2, :], small[:, 6, :]
            v.E3 = small[:, 3:6, :]
            v.r_dt, v.r_a, v.r_d1, v.r_E, v.r_dtw = rs_[0], rs_[1], rs_[2], rs_[3], rs_[6]
            v.w_out_ = v.E3[:, 0, :] if d == 0 else v.E3[:, 1, :]
            v.w_st = v.E3[:, 1, :] if d == 0 else v.E3[:, 0, :]
            v.dec = v.E3[:, 2, :]
            v.stb, v.r_stb = B.stb[p2], B.r_stb[p2]
            v.dt_b, v.dtw_b = B.s16[p2][:, 0, :], B.s16[p2][:, 1, :]
            v.r_dt_b, v.r_dtw_b = B.r_s16[p2][0], B.r_s16[p2][1]
            v.ydg, v.r_ydg = B.ydg[p2], B.r_ydg[p2]
            v.x3 = v.x_.rearrange("p (k q) -> p k q", k=16)
            return v

        def local(d, n):
            v = names(d, n)
            B = v.B
            dc = slice(d * 16, d * 16 + 16)
            dt_, a_, d1_, dtw_, E3 = v.dt_, v.a_, v.d1_, v.dtw_, v.E3
            L0, L1, X, Y = v.L0, v.L1, v.X, v.Y
            f_, rf = v.f_, v.rf
            P.op("vector", lambda e: e.tensor_tensor(out=dt_, in0=v.dr[:, dc], in1=dtb[:, dc], op=ALU.add),
                 reads=[v.rdr, r_par], writes=[v.r_dt])
            P.op("scalar", lambda e: e.activation(out=dt_, in_=dt_, func=AF.Exp), reads=[v.r_dt], writes=[v.r_dt])
            P.op("scalar", lambda e: e.activation(out=dt_, in_=dt_, func=AF.Ln, bias=1.0), reads=[v.r_dt],
                 writes=[v.r_dt])
            P.op("vector", lambda e: e.tensor_tensor(out=a_, in0=dt_, in1=arow[:, dc], op=ALU.mult),
                 reads=[v.r_dt, r_par], writes=[v.r_a])
            yield
            tri = masks[:, M_LE, :] if d == 0 else masks[:, M_LT, :]
            P.op("tensor", lambda e: e.matmul(psum[X][:, 0:16], lhsT=tri, rhs=a_, start=True, stop=True),
                 reads=[v.r_a, r_const], writes=[r_ps[X]])
            P.op("tensor", lambda e: e.matmul(psum[X][:, 16:32], lhsT=masks[:, M_ONE, :], rhs=a_, start=True, stop=True),
                 reads=[v.r_a, r_const], writes=[r_ps[X]])
            for g in range(2):
                P.op("tensor", lambda e, g=g: e.matmul(psum[X][:, 64 + g * 128: 64 + (g + 1) * 128], lhsT=f_[:, g, :],
                                                       rhs=f_[:, 2 + g, :], start=True, stop=True),
                     reads=[rf], writes=[r_ps[X]])
            P.op("vector", lambda e: e.tensor_copy(out=B.cst, in_=psum[X][:, 0:32]), reads=[r_ps[X]], writes=[B.r_cst])
            m1 = masks[:, M_LE, :] if d == 0 else masks[:, M_GE, :]
            m2 = masks[:, M_GT, :] if d == 0 else masks[:, M_LT, :]
            P.op("vector", lambda e: e.tensor_tensor(out=B.smk, in0=psum[X][:, 64:320].rearrange("p (g l) -> p g l", g=2),
                                                     in1=bc(m1.unsqueeze(1), [128, 2, 128]), op=ALU.mult),
                 reads=[r_ps[X], r_const], writes=[B.r_smk])
            yield
            cst = B.cst
            P.op("vector", lambda e: e.tensor_tensor(out=d1_, in0=cst[:, 16:32], in1=cst[:, 0:16], op=ALU.subtract),
                 reads=[B.r_cst], writes=[v.r_d1])
            P.op("scalar", lambda e: e.activation(out=E3[:, 0, :], in_=cst[:, 0:16], func=AF.Exp), reads=[B.r_cst],
                 writes=[v.r_E])
            P.op("scalar", lambda e: e.activation(out=E3[:, 1, :], in_=d1_, func=AF.Exp), reads=[v.r_d1], writes=[v.r_E])
            P.op("scalar", lambda e: e.activation(out=E3[:, 2, :], in_=cst[:, 16:32], func=AF.Exp), reads=[B.r_cst],
                 writes=[v.r_E])
            P.op("vector", lambda e: e.tensor_tensor(out=v.dtw_b, in0=dt_, in1=v.w_st, op=ALU.mult), reads=[v.r_dt, v.r_E],
                 writes=[v.r_dtw_b])
            P.op("gpsimd", lambda e: e.tensor_copy(out=v.dt_b, in_=dt_), reads=[v.r_dt], writes=[v.r_dt_b])
            yield
            for k in range(16):
                P.op("scalar", lambda e, k=k: e.activation(out=B.R[:, k, :], in_=m1, func=AF.Identity,
                                                           scale=a_[:, k:k + 1]),
                     reads=[v.r_a, r_const], writes=[B.r_R])
            P.op("vector", lambda e: e.tensor_tensor(out=B.xw.rearrange("p (k q) -> p k q", k=16), in0=v.x3,
                                                     in1=bc(v.dtw_b.unsqueeze(2), [128, 16, 64]), op=ALU.mult),
                 reads=[v.rx, v.r_dtw_b], writes=[B.r_xw])
            P.op("vector", lambda e: e.tensor_tensor(out=B.xdt.rearrange("p (k q) -> p k q", k=16), in0=v.x3,
                                                     in1=bc(v.dt_b.unsqueeze(2), [128, 16, 64]), op=ALU.mult),
                 reads=[v.rx, v.r_dt_b], writes=[B.r_xdt])
            yield
            for q in range(4):
                bq = L0 + q % 2
                P.op("tensor", lambda e, q=q, bq=bq: e.matmul(psum[bq][:, :], lhsT=m2,
                                                              rhs=B.R[:, q * 4:(q + 1) * 4, :].rearrange("p k l -> p (k l)"),
                                                              start=True, stop=True), reads=[B.r_R, r_const],
                     writes=[r_ps[bq]])
                P.op("scalar", lambda e, q=q, bq=bq: e.activation(
                    out=B.Lm[:, q * 4:(q + 1) * 4, :].rearrange("p k l -> p (k l)"), in_=psum[bq][:, :], func=AF.Exp),
                    reads=[r_ps[bq]], writes=[B.r_Lm])
                if q == 1:
                    yield
            yield
            for g in range(2):
                P.op("tensor", lambda e, g=g: e.matmul(psum[X + g][:, :], lhsT=v.b_[:, g * 128:(g + 1) * 128],
                                                       rhs=B.xw[:, g * 512:(g + 1) * 512], start=True, stop=True),
                     reads=[v.rb, B.r_xw], writes=[r_ps[X + g]])
                P.op("scalar", lambda e, g=g: e.copy(out=v.stb[:, g * 512:(g + 1) * 512], in_=psum[X + g][:, :]),
                     reads=[r_ps[X + g]], writes=[v.r_stb])
            P.op("vector", lambda e: e.tensor_tensor(out=B.Mm.rearrange("p (g k) l -> p g k l", g=2),
                                                     in0=B.Lm.rearrange("p (g k) l -> p g k l", g=2),
                                                     in1=bc(B.smk.unsqueeze(2), [128, 2, 8, 128]), op=ALU.mult),
                 reads=[B.r_Lm, B.r_smk], writes=[B.r_Mm])
            yield
            for k in range(16):
                bk = L0 + k // 8
                P.op("tensor", lambda e, k=k, bk=bk: e.matmul(psum[bk][:, (k % 8) * 64:(k % 8 + 1) * 64],
                                                              lhsT=B.Mm[:, k, :], rhs=B.xdt[:, k * 64:(k + 1) * 64],
                                                              start=True, stop=True),
                     reads=[B.r_Mm, B.r_xdt], writes=[r_ps[bk]])
            for g in range(2):
                P.op("scalar", lambda e, g=g: e.copy(out=v.ydg[:, g * 512:(g + 1) * 512], in_=psum[L0 + g][:, :]),
                     reads=[r_ps[L0 + g]], writes=[v.r_ydg])
            yield

        def recur(d, n):
            v = names(d, n)
            B = v.B
            ci, tok0 = v.ci, v.tok0
            fin = n >= HALF
            L0, L1, X, Y = v.L0, v.L1, v.X, v.Y
            H, r_H, Hb, r_Hb = B.H, B.r_H, B.Hb, B.r_Hb
            ytmp, r_ytmp, yfl, r_yfl = B.ytmp, B.r_ytmp, B.yfl, B.r_yfl
            for g in range(2):
                P.op("tensor", lambda e, g=g: e.matmul(psum[X + g][:, :], lhsT=v.f_[:, 2 + g, :], rhs=Hb[:, g, :],
                                                       start=True, stop=True), reads=[v.rf, r_Hb], writes=[r_ps[X + g]])
            for g in range(2):
                P.op("gpsimd", lambda e, g=g: e.tensor_tensor(out=H[:, g, :].rearrange("p (k q) -> p k q", k=8),
                                                              in0=H[:, g, :].rearrange("p (k q) -> p k q", k=8),
                                                              in1=bc(v.dec[:, g * 8:(g + 1) * 8].unsqueeze(2), [128, 8, 64]),
                                                              op=ALU.mult), reads=[r_H, v.r_E], writes=[r_H])
            P.op("gpsimd", lambda e: e.tensor_tensor(out=H.rearrange("p g q -> p (g q)"),
                                                     in0=H.rearrange("p g q -> p (g q)"), in1=v.stb, op=ALU.add),
                 reads=[r_H, v.r_stb], writes=[r_H])
            nxt = ci + 1 if d == 0 else ci - 1
            at_edge = (nxt % 16 == 0) if d == 0 else (ci % 16 == 0)
            if at_edge:
                coupled = (d == 0 and nxt == 16) or (d == 1 and ci == 16)
                if coupled:
                    P.op("vector", lambda e: e.tensor_scalar(out=H, in0=H, scalar1=flag[:, 0:1], scalar2=None,
                                                             op0=ALU.mult), reads=[r_H, r_const], writes=[r_H])
                else:
                    P.op("vector", lambda e: e.memset(H, 0.0), writes=[r_H])
            P.op("scalar", lambda e: e.copy(out=Hb, in_=H), reads=[r_H], writes=[r_Hb])
            for g in range(2):
                P.op("vector", lambda e, g=g: e.tensor_tensor(
                    out=ytmp[:, g * 512:(g + 1) * 512].rearrange("p (k q) -> p k q", k=8),
                    in0=psum[X + g][:, :].rearrange("p (k q) -> p k q", k=8),
                    in1=bc(v.w_out_[:, g * 8:(g + 1) * 8].unsqueeze(2), [128, 8, 64]), op=ALU.mult),
                    reads=[r_ps[X + g], v.r_E], writes=[r_ytmp])
            yield
            P.op("gpsimd", lambda e: e.tensor_tensor(out=ytmp, in0=ytmp, in1=v.ydg, op=ALU.add),
                 reads=[r_ytmp, v.r_ydg], writes=[r_ytmp])
            yield
            if not fin:
                P.dma("sync", lambda e: e.dma_start(out=yf_d[tok0:tok0 + 128, :], in_=ytmp), reads=[r_ytmp],
                      writes=[r_yfd[ci]], semres=r_ytmp)
                return
            ss1, r_ss1, yn, r_yn = B.ss1, B.r_ss1, B.yn, B.r_yn
            P.dma("sync", lambda e: e.dma_start(out=yfl, in_=yf_d[tok0:tok0 + 128, :]), reads=[r_yfd[ci]],
                  writes=[r_yfl], semres=r_yfl)
            P.op("gpsimd", lambda e: e.tensor_tensor(out=ytmp, in0=ytmp, in1=yfl, op=ALU.add),
                 reads=[r_ytmp, r_yfl], writes=[r_ytmp])
            P.op("gpsimd", lambda e: e.tensor_tensor(out=yfl.rearrange("p (k q) -> p k q", k=16), in0=v.x3,
                                                     in1=bc(dsk.unsqueeze(2), [128, 16, 64]), op=ALU.mult),
                 reads=[v.rx, r_par, r_yfl], writes=[r_yfl])
            yield
            P.op("vector", lambda e: e.tensor_tensor(out=ytmp, in0=ytmp, in1=yfl, op=ALU.add),
                 reads=[r_ytmp, r_yfl], writes=[r_ytmp])
            P.op("vector", lambda e: e.tensor_tensor(out=ytmp, in0=ytmp, in1=v.z_, op=ALU.mult),
                 reads=[r_ytmp, v.rz], writes=[r_ytmp])
            P.op("scalar", lambda e: e.activation(out=yfl, in_=ytmp, func=AF.Square, accum_out=ss1),
                 reads=[r_ytmp], writes=[r_yfl, r_ss1])
            rstd_from_ss(ss1, ss1, r_ss1, r_ss1, D)
            yield
            P.op("vector", lambda e: e.scalar_tensor_tensor(out=yn, in0=ytmp, scalar=ss1[:, 0:1], in1=gsr,
                                                            op0=ALU.mult, op1=ALU.mult),
                 reads=[r_ytmp, r_ss1, r_gsr], writes=[r_yn])
            pT = psum[L0][:, :].bitcast(BF16)
            for c in range(8):
                P.op("tensor", lambda e, c=c: e.transpose(out=pT[:, c * 128:(c + 1) * 128],
                                                          in_=yn[:, c * 128:(c + 1) * 128], identity=ident[:]),
                     reads=[r_yn, r_const], writes=[r_ps[L0]])
            yo, ryo = B.yfm[n % 2], B.r_yfm[n % 2]
            P.op("scalar", lambda e: e.copy(out=yo, in_=pT.rearrange("p (c t) -> p c t", c=8)), reads=[r_ps[L0]],
                 writes=[ryo])
            P.dma("sync", lambda e: e.dma_start(out=fm_tile(yfm_d, 0, 8, tok0, 128), in_=yo), reads=[ryo], semres=ryo)
            yield

        for d in range(2):
            B = BUF[d]
            P.op("vector", lambda e, B=B: e.memset(B.H, 0.0), writes=[B.r_H])
            P.op("scalar", lambda e, B=B: e.copy(out=B.Hb, in_=B.H), reads=[B.r_H], writes=[B.r_Hb])
            loads(d, 0)
        for m in range(NSUB + 1):
            for d in range(2):
                if m + 1 < NSUB:
                    loads(d, m + 1)
            gens = []
            for d in range(2):
                if m < NSUB:
                    gens.append(local(d, m))
            for d in range(2):
                if m >= 1:
                    gens.append(recur(d, m - 1))
            while gens:
                alive = []
                for g_ in gens:
                    try:
                        next(g_)
                        alive.append(g_)
                    except StopIteration:
                        pass
                gens = alive

    def phase_fnet(l):
        new_phase()
        csc = AB.take(256)
        r_csc = P.res("csc")
        P.dma("sync", lambda e: e.dma_start(out=csc, in_=csc_d), writes=[r_csc], semres=r_csc)
        uv = AB.take(NSUB, D)
        r_uv = P.res("uv")
        ut = [AB.take(4, 512) for _ in range(2)]
        r_ut = [P.res("ut0"), P.res("ut1")]
        tab = [AB.take(16, 512) for _ in range(2)]
        r_tab = [P.res("tab0"), P.res("tab1")]
        ft = [AB.take(4, 512) for _ in range(2)]
        r_ft = [P.res("ft0"), P.res("ft1")]
        def ld_u(tt):
            P.dma("sync", lambda e: e.dma_start(out=ut[tt % 2], in_=fm_tile(u_d, 0, 4, tt * 512, 512)),
                  writes=[r_ut[tt % 2]], semres=r_ut[tt % 2])

        ld_u(0)
        for t in range(NT):
            tok0 = t * 512
            u, ru = ut[t % 2], r_ut[t % 2]
            if t + 1 < NT:
                ld_u(t + 1)
            for j in range(4):
                for half in range(2):
                    b = (j * 2 + half) % 8
                    for gg in range(2):
                        g = half * 2 + gg
                        P.op("tensor", lambda e, g=g, gg=gg, j=j, b=b, u=u: e.matmul(
                            psum[b][:, gg * 256:(gg + 1) * 256], lhsT=u[:, g, j * 128:(j + 1) * 128], rhs=csc,
                            start=True, stop=True), reads=[ru, r_csc], writes=[r_ps[b]])
                    eng = "vector" if half == 0 else "scalar"
                    dst = uv[:, t * 4 + j, half * 512:(half + 1) * 512]
                    if eng == "vector":
                        P.op("vector", lambda e, b=b, dst=dst: e.tensor_copy(out=dst, in_=psum[b][:, :]),
                             reads=[r_ps[b]], writes=[r_uv])
                    else:
                        P.op("scalar", lambda e, b=b, dst=dst: e.copy(out=dst, in_=psum[b][:, :]),
                             reads=[r_ps[b]], writes=[r_uv])
        blocks = {0: [(0, 0), (1, 1)], 1: [(2, 0), (3, 1)], 2: [(4, 2)]}
        allsteps = []
        for oseg in range(3):
            for kt in range(4):
                steps = [(bi, iseg, cs) for (bi, iseg) in blocks[oseg] for cs in range(2)]
                for si, (bi, iseg, cs) in enumerate(steps):
                    allsteps.append((oseg, kt, bi, iseg, cs, si, len(steps)))

        def ld_tab(n):
            oseg, kt, bi, iseg, cs, si, ns = allsteps[n]
            P.dma("sync", lambda e: e.dma_start(out=tab[n % 2], in_=tab_d[bi, kt, cs]), writes=[r_tab[n % 2]],
                  semres=r_tab[n % 2])

        ld_tab(0)
        for n, (oseg, kt, bi, iseg, cs, si, ns) in enumerate(allsteps):
            if n + 1 < len(allsteps):
                ld_tab(n + 1)
            tb, rt = tab[n % 2], r_tab[n % 2]
            tot = ns * 16
            for ncn in range(16):
                cnt = si * 16 + ncn
                for g in range(4):
                    P.op("tensor", lambda e, g=g, ncn=ncn, cnt=cnt: e.matmul(
                        psum[g + 4 * ((oseg * 4 + kt) % 2)][:, :],
                        lhsT=uv[:, iseg * 16 + ncn, g * 256 + cs * 128: g * 256 + (cs + 1) * 128],
                        rhs=tb[:, ncn, :], start=(cnt == 0), stop=(cnt == tot - 1)),
                        reads=[r_uv, rt], writes=[r_ps[g + 4 * ((oseg * 4 + kt) % 2)]])
            if si == ns - 1:
                pb0 = 4 * ((oseg * 4 + kt) % 2)
                fo, rfo = ft[(oseg * 4 + kt) % 2], r_ft[(oseg * 4 + kt) % 2]
                for g in range(4):
                    if g % 2 == 0:
                        P.op("vector", lambda e, g=g: e.tensor_copy(out=fo[:, g, :], in_=psum[pb0 + g][:, :]),
                             reads=[r_ps[pb0 + g]], writes=[rfo])
                    else:
                        P.op("scalar", lambda e, g=g: e.copy(out=fo[:, g, :], in_=psum[pb0 + g][:, :]),
                             reads=[r_ps[pb0 + g]], writes=[rfo])
                tok0 = oseg * SEG + kt * 512
                P.dma("sync", lambda e: e.dma_start(out=fm_tile(f_d, 0, 4, tok0, 512), in_=fo), reads=[rfo], semres=rfo)

    def phase_merge(l, xsrc):
        new_phase()
        wa = AB.take(4, D)
        wb = AB.take(8, D)
        wc = AB.take(4, D)
        wo = AB.take(8, D)
        r_w1 = [P.res("wE0"), P.res("wE1"), P.res("wE2"), P.res("wE3")]
        r_wo = [P.res("wo0"), P.res("wo1")]
        for hq in range(4):
            cq = slice(hq * 256, (hq + 1) * 256)
            load_w(wa[:, :, cq], w_a_out[l][:, cq].rearrange("(c p) n -> p c n", p=128), r_w1[hq])
            load_w(wb[:, :, cq], w_b_out[l][:, cq].rearrange("(c p) n -> p c n", p=128), r_w1[hq])
            load_w(wc[:, :, cq], w_c_out[l][:, cq].rearrange("(c p) n -> p c n", p=128), r_w1[hq])
        for hq in range(2):
            cq = slice(hq * 512, (hq + 1) * 512)
            load_w(wo[:, :, cq], w_out[l][:, cq].rearrange("(c p) n -> p c n", p=128), r_wo[hq])
        ia = [AB.take(4, 512) for _ in range(2)]
        iy = [AB.take(8, 512) for _ in range(2)]
        if_ = [AB.take(4, 512) for _ in range(2)]
        ig = [AB.take(24, 512) for _ in range(2)]
        r_in = [P.res("Ein0"), P.res("Ein1")]
        mb = [AB.take(8, 512) for _ in range(2)]
        r_mb = [P.res("mb0"), P.res("mb1")]
        mf = [AFa.take(512) for _ in range(2)]
        r_mf = [P.res("mf0"), P.res("mf1")]
        tp = [AFa.take(512) for _ in range(2)]
        r_tp = [P.res("tp0"), P.res("tp1")]
        tq = [AFa.take(512) for _ in range(2)]
        r_tq = [P.res("tq0"), P.res("tq1")]
        xt = [AFa.take(D) for _ in range(2)]
        r_xt = [P.res("ext0"), P.res("ext1")]
        gg = [AFa.take(D) for _ in range(2)]
        r_gg = [P.res("gg0"), P.res("gg1")]
        gtmp = AFa.take(D)
        r_gtmp = P.res("egtmp")
        tmp = [AFa.take(D) for _ in range(2)]
        r_tmp = [P.res("etmp0"), P.res("etmp1")]
        xo = [AFa.take(D) for _ in range(2)]
        r_xo = [P.res("exo0"), P.res("exo1")]
        ss = [AFa.take(2) for _ in range(2)]
        r_ss = [P.res("ess0"), P.res("ess1")]
        kk = 0
        for t in range(NT):
            tok0 = t * 512
            slot = t // 4
            if t % 4 == 0:
                load_rows(l, slot, None, [(gg[slot % 2], r_gg[slot % 2], "gate", "g_post_mix", 2, gtmp, r_gtmp)])
            ia_, iy_, if__, ig_, rin = ia[t % 2], iy[t % 2], if_[t % 2], ig[t % 2], r_in[t % 2]
            mb_, rmb = mb[t % 2], r_mb[t % 2]

            def ld_in(tt):
                for dst, src, C in ((ia[tt % 2], acv_d, 4), (iy[tt % 2], yfm_d, 8), (if_[tt % 2], f_d, 4),
                                    (ig[tt % 2], gates_d, 24)):
                    P.dma("sync", lambda e, dst=dst, src=src, C=C: e.dma_start(out=dst,
                                                                               in_=fm_tile(src, 0, C, tt * 512, 512)),
                          writes=[r_in[tt % 2]], semres=r_in[tt % 2])
            if t == 0:
                ld_in(0)
            if t + 1 < NT:
                ld_in(t + 1)
            for i in range(8):
                cs_ = slice(i * 128, (i + 1) * 128)
                b0 = 3 * (kk % 2)
                mf_, rmf = mf[kk % 2], r_mf[kk % 2]
                tp_, rtp = tp[kk % 2], r_tp[kk % 2]
                kk += 1
                for c in range(4):
                    P.op("tensor", lambda e, c=c, cs_=cs_, b0=b0: e.matmul(psum[b0][:, :], lhsT=wa[:, c, cs_],
                                                                            rhs=ia_[:, c, :], start=(c == 0), stop=(c == 3)),
                         reads=[r_w1[i // 2], rin], writes=[r_ps[b0]])
                for c in range(8):
                    P.op("tensor", lambda e, c=c, cs_=cs_, b0=b0: e.matmul(psum[b0 + 1][:, :], lhsT=wb[:, c, cs_],
                                                                            rhs=iy_[:, c, :], start=(c == 0), stop=(c == 7)),
                         reads=[r_w1[i // 2], rin], writes=[r_ps[b0 + 1]])
                for c in range(4):
                    P.op("tensor", lambda e, c=c, cs_=cs_, b0=b0: e.matmul(psum[b0 + 2][:, :], lhsT=wc[:, c, cs_],
                                                                            rhs=if__[:, c, :], start=(c == 0), stop=(c == 3)),
                         reads=[r_w1[i // 2], rin], writes=[r_ps[b0 + 2]])
                P.op("vector", lambda e, i=i, b0=b0: e.tensor_tensor(out=mf_, in0=psum[b0][:, :], in1=ig_[:, i, :],
                                                                     op=ALU.mult),
                     reads=[r_ps[b0], rin], writes=[rmf])
                P.op("vector", lambda e, i=i, b0=b0: e.tensor_tensor(out=tp_, in0=psum[b0 + 1][:, :], in1=ig_[:, 8 + i, :],
                                                                     op=ALU.mult),
                     reads=[r_ps[b0 + 1], rin], writes=[rtp])
                tq_, rtq = tq[(kk - 1) % 2], r_tq[(kk - 1) % 2]
                P.op("vector", lambda e, i=i, b0=b0: e.tensor_tensor(out=tq_, in0=psum[b0 + 2][:, :], in1=ig_[:, 16 + i, :],
                                                                     op=ALU.mult),
                     reads=[r_ps[b0 + 2], rin], writes=[rtq])
                P.op("gpsimd", lambda e: e.tensor_tensor(out=mf_, in0=mf_, in1=tp_, op=ALU.add), reads=[rmf, rtp],
                     writes=[rmf])
                P.op("gpsimd", lambda e, i=i: e.tensor_tensor(out=mb_[:, i, :], in0=mf_, in1=tq_, op=ALU.add),
                     reads=[rmf, rtq], writes=[rmb])
                if t >= 1 and i % 2 == 1:
                    tp_t = t - 1
                    out_proj_residual(tp_t, mb[tp_t % 2], r_mb[tp_t % 2], wo, r_wo, 8, xsrc, xt, r_xt,
                                      gg[(tp_t // 4) % 2], r_gg[(tp_t // 4) % 2], tmp, r_tmp, xo, r_xo, ss, r_ss,
                                      ((6, 7),), js=(i // 2,))
        tp_t = NT - 1
        out_proj_residual(tp_t, mb[tp_t % 2], r_mb[tp_t % 2], wo, r_wo, 8, xsrc, xt, r_xt, gg[(tp_t // 4) % 2],
                          r_gg[(tp_t // 4) % 2], tmp, r_tmp, xo, r_xo, ss, r_ss, ((6, 7), (4, 5)))

    def out_proj_residual(t, act, r_act, w, r_wh, nk, xsrc, xt, r_xt, gg, r_gg, tmps, r_tmps, xo, r_xo, sss, r_sss, banks,
                          js=(0, 1, 2, 3)):
        tok0 = t * 512
        for j in js:
            x_, rx = xt[j % 2], r_xt[j % 2]
            o_, ro = xo[j % 2], r_xo[j % 2]
            tmp, r_tmp = tmps[j % 2], r_tmps[j % 2]
            ss, r_ss = sss[j % 2], r_sss[j % 2]
            r0 = tok0 + j * 128
            P.dma("sync", lambda e, x_=x_, r0=r0: e.dma_start(out=x_, in_=xsrc[r0:r0 + 128, :]), writes=[rx], semres=rx)
            bk = banks[j % len(banks)]
            for half in range(2):
                b = bk[half]
                for c in range(nk):
                    P.op("tensor", lambda e, c=c, b=b, j=j, half=half: e.matmul(
                        psum[b][:, :], lhsT=act[:, c, j * 128:(j + 1) * 128], rhs=w[:, c, half * 512:(half + 1) * 512],
                        start=(c == 0), stop=(c == nk - 1)), reads=[r_act, r_wh[half]], writes=[r_ps[b]])
            P.op("scalar", lambda e, o_=o_, bk=bk, ss=ss: e.activation(out=o_[:, 0:512], in_=psum[bk[0]][:, :],
                                                                       func=AF.Square, accum_out=ss[:, 0:1]),
                 reads=[r_ps[bk[0]]], writes=[ro, r_ss])
            P.op("scalar", lambda e, o_=o_, bk=bk, ss=ss: e.activation(out=o_[:, 512:1024], in_=psum[bk[1]][:, :],
                                                                       func=AF.Square, accum_out=ss[:, 1:2]),
                 reads=[r_ps[bk[1]]], writes=[ro, r_ss])
            P.op("vector", lambda e, ss=ss: e.tensor_tensor(out=ss[:, 0:1], in0=ss[:, 0:1], in1=ss[:, 1:2], op=ALU.add),
                 reads=[r_ss], writes=[r_ss])
            rstd_from_ss(ss[:, 0:1], ss[:, 0:1], r_ss, r_ss, D)
            for half in range(2):
                hs = slice(half * 512, (half + 1) * 512)
                P.op("vector", lambda e, half=half, hs=hs, bk=bk, ss=ss, tmp=tmp: e.scalar_tensor_tensor(
                    out=tmp[:, hs], in0=psum[bk[half]][:, :], scalar=ss[:, 0:1], in1=gg[:, hs], op0=ALU.mult,
                    op1=ALU.mult), reads=[r_ps[bk[half]], r_ss, r_gg], writes=[r_tmp])
            P.op("gpsimd", lambda e, o_=o_, x_=x_, tmp=tmp: e.tensor_tensor(out=o_, in0=tmp, in1=x_, op=ALU.add),
                 reads=[r_tmp, rx], writes=[ro])
            P.dma("sync", lambda e, o_=o_, r0=r0: e.dma_start(out=yout[r0:r0 + 128, :], in_=o_), reads=[ro], semres=ro)

    def phase_ffn1(l):
        new_phase()
        NCOL = 2 * D_FF
        wF = AB.take(8, NCOL)
        r_wFp = [P.res(f"wF{i}") for i in range(6)]
        for pi in (0, 2, 3, 1, 4, 5):
            c0, c1 = pi * 1024, min(NCOL, (pi + 1) * 1024)
            load_w(wF[:, :, c0:c1], w_ffn_in[l, :, c0:c1].rearrange("(c p) n -> p c n", p=128), r_wFp[pi])
        hfm = [AB.take(8, 512) for _ in range(2)]
        r_hfm = [P.res("fh0"), P.res("fh1")]
        xh = [AB.take(D) for _ in range(2)]
        r_xh = [P.res("fxh0"), P.res("fxh1")]
        sgt = [AB.take(512) for _ in range(2)]
        r_sgt = [P.res("sgt0"), P.res("sgt1")]
        oc = [AB.take(512) for _ in range(6)]
        r_oc = [P.res(f"foc{i}") for i in range(6)]
        xt = [AFa.take(D) for _ in range(8)]
        r_xt = [P.res(f"fxt{i}") for i in range(8)]
        gs = AFa.take(D)
        r_gs = P.res("fgs")
        sh = AFa.take(D)
        r_sh = P.res("fsh")
        tmp = AFa.take(D)
        r_tmp = P.res("ftmp")
        gtmp = AFa.take(D)
        r_gtmp = P.res("fgtmp")
        ss = AFa.take(4)
        r_ss = P.res("fss")
        rstd = AFa.take(4)
        r_rstd = P.res("frstd")
        r_junk = P.res("fjunk")
        k = 0

        def prep(tt):
            if tt % 4 == 0:
                load_rows(l, tt // 4, None, [(gs, r_gs, "scale", "g_pre_ffn", 4, gtmp, r_gtmp),
                                             (sh, r_sh, "shift", None, 3, None, None)])
            yield from norm_transpose_tile(yout, tt, xt, r_xt, tmp, r_tmp, ss, r_ss, rstd, r_rstd, gs, r_gs, sh, r_sh, tmp,
                                           r_tmp, xh, r_xh, hfm[tt % 2], r_hfm[tt % 2], 0)

        load_x_tile(yout, 0, xt, r_xt)
        load_x_tile(yout, 1, xt, r_xt)
        for _ in prep(0):
            pass
        for t in range(NT):
            slot = t // 4
            tok0 = t * 512
            if t + 2 < NT:
                load_x_tile(yout, t + 2, xt, r_xt)
            h, rh = hfm[t % 2], r_hfm[t % 2]
            gen = prep(t + 1) if t + 1 < NT else iter(())
            for i in range(22):
                if i in (2, 5, 8, 11, 14, 17, 20):
                    next(gen, None)
                b1 = 1 + (2 * k) % 6
                b2 = 1 + (2 * k + 1) % 6
                sg_, rsg = sgt[k % 2], r_sgt[k % 2]
                o, ro = oc[k % 6], r_oc[k % 6]
                k += 1
                for c in range(8):
                    P.op("tensor", lambda e, c=c, b1=b1, i=i, h=h: e.matmul(psum[b1][:, :], lhsT=wF[:, c, i * 128:(i + 1) * 128],
                                                                             rhs=h[:, c, :], start=(c == 0), stop=(c == 7)),
                         reads=[r_wFp[(i * 128) // 1024], rh], writes=[r_ps[b1]])
                for c in range(8):
                    P.op("tensor", lambda e, c=c, b2=b2, i=i, h=h: e.matmul(
                        psum[b2][:, :], lhsT=wF[:, c, D_FF + i * 128: D_FF + (i + 1) * 128], rhs=h[:, c, :],
                        start=(c == 0), stop=(c == 7)), reads=[r_wFp[(D_FF + i * 128) // 1024], rh], writes=[r_ps[b2]])
                P.op("scalar", lambda e, b1=b1, sg_=sg_: e.activation(out=sg_, in_=psum[b1][:, :], func=AF.Silu),
                     reads=[r_ps[b1]], writes=[rsg])
                P.op("vector", lambda e, b2=b2, sg_=sg_, o=o: e.tensor_tensor(out=o, in0=psum[b2][:, :], in1=sg_,
                                                                              op=ALU.mult),
                     reads=[r_ps[b2], rsg], writes=[ro])
                P.dma("sync", lambda e, o=o, i=i, tok0=tok0: e.dma_start(out=act_d[i, :, tok0:tok0 + 512], in_=o),
                      reads=[ro], semres=ro)
            for _ in gen:
                pass

    def phase_ffn2(l):
        new_phase()
        wD = AB.take(22, D)
        r_wD = [P.res("wD0"), P.res("wD1")]
        for hq in range(2):
            cq = slice(hq * 512, (hq + 1) * 512)
            for c0 in range(0, 22, 6):
                c1 = min(22, c0 + 6)
                load_w(wD[:, c0:c1, cq], w_ffn_out[l, c0 * 128:c1 * 128, cq].rearrange("(c p) n -> p c n", p=128),
                       r_wD[hq])
        act = [AB.take(22, 512) for _ in range(2)]
        r_act = [P.res("act0"), P.res("act1")]
        xt = [AFa.take(D) for _ in range(2)]
        r_xt = [P.res("gxt0"), P.res("gxt1")]
        gg = AFa.take(D)
        r_gg = P.res("ggg")
        gtmp = AFa.take(D)
        r_gtmp = P.res("ggtmp")
        tmp = [AFa.take(D) for _ in range(2)]
        r_tmp = [P.res("gtmp2a"), P.res("gtmp2b")]
        xo = [AFa.take(D) for _ in range(2)]
        r_xo = [P.res("gxo0"), P.res("gxo1")]
        ss = [AFa.take(2) for _ in range(2)]
        r_ss = [P.res("gss0"), P.res("gss1")]
        for t in range(NT):
            tok0 = t * 512
            slot = t // 4
            if t % 4 == 0:
                load_rows(l, slot, None, [(gg, r_gg, "gate", "g_post_ffn", 5, gtmp, r_gtmp)])
            a, ra = act[t % 2], r_act[t % 2]

            def ld_act(tt):
                P.dma("sync", lambda e: e.dma_start(out=act[tt % 2], in_=fm_tile(act_d, 0, 22, tt * 512, 512)),
                      writes=[r_act[tt % 2]], semres=r_act[tt % 2])
            if t == 0:
                ld_act(0)
            if t + 1 < NT:
                ld_act(t + 1)
            out_proj_residual(t, a, ra, wD, r_wD, 22, yout, xt, r_xt, gg, r_gg, tmp, r_tmp, xo, r_xo, ss, r_ss,
                              ((0, 1), (2, 3), (4, 5), (6, 7)))

    phases = []
    phase_mod()
    done = (stop_after == "mod")
    for l in range(nl):
        if done:
            break
        xsrc = xin if l == 0 else yout
        for name, fn in (("a", lambda: phase_a(l, xsrc)), ("gates", lambda: phase_gates(l)),
                         ("conv_a", lambda: phase_conv_a(l)), ("conv_s", lambda: phase_conv_s(l)),
                         ("ssd", lambda: phase_ssd(l)), ("fnet", lambda: phase_fnet(l)),
                         ("merge", lambda: phase_merge(l, xsrc)), ("ffn1", lambda: phase_ffn1(l)),
                         ("ffn2", lambda: phase_ffn2(l))):
            fn()
            if stop_after == name:
                done = True
                break
    P.finalize()
    return nc, P


def _dft_tables(coupled):
    bf = ml_dtypes.bfloat16
    tab = np.zeros((5, 4, 2, 128, 16, 512), dtype=bf)
    n = np.arange(SEG, dtype=np.float64)
    k = np.arange(SEG, dtype=np.float64)

    def fill(bi, ang, scale):
        for cs, m in ((0, np.cos(ang) * scale), (1, -np.sin(ang) * scale)):
            m = m.reshape(16, 128, 4, 512).transpose(2, 1, 0, 3)
            tab[bi, :, cs] = m.astype(bf)

    if coupled:
        L = 2 * SEG
        sc = 1.0 / np.sqrt(L * 128.0)
        for bi, (os_, is_) in enumerate(((0, 0), (0, 1), (1, 0), (1, 1))):
            prod = np.outer(n + is_ * SEG, k + os_ * SEG) % L
            fill(bi, 2 * np.pi * prod / L, sc)
    else:
        L = SEG
        sc = 1.0 / np.sqrt(L * 128.0)
        ang = 2 * np.pi * (np.outer(n, k) % L) / L
        fill(0, ang, sc)
        tab[3] = tab[0]
    sc = 1.0 / np.sqrt(SEG * 128.0)
    ang = 2 * np.pi * (np.outer(n, k) % SEG) / SEG
    fill(4, ang, sc)
    return tab


def _host_inputs(inp):
    f32 = np.float32
    bf = ml_dtypes.bfloat16
    shared = {}
    for k in ("w_ada", "b_ada", "g_pre_mix", "g_post_mix", "g_pre_ffn", "g_post_ffn", "g_ssd", "w_in", "w_a_out",
              "w_b_out", "w_c_out", "w_out", "w_ffn_in", "w_ffn_out"):
        shared[k] = np.ascontiguousarray(np.asarray(inp[k], dtype=f32))
    shared["caw"] = np.ascontiguousarray(np.asarray(inp["conv_a_w"], f32).reshape(DEPTH, 31, 4, 128).transpose(0, 3, 2, 1))
    for nm, src in (("cab", "conv_a_b"), ("lng", "ln_a_g"), ("lnb", "ln_a_b")):
        shared[nm] = np.ascontiguousarray(np.asarray(inp[src], f32).reshape(DEPTH, 4, 128).transpose(0, 2, 1))
    shared["csw"] = np.ascontiguousarray(np.asarray(inp["conv_s_w"], f32).reshape(DEPTH, 5, 12, 128).transpose(0, 3, 2, 1))
    shared["csb"] = np.ascontiguousarray(np.asarray(inp["conv_s_b"], f32).reshape(DEPTH, 12, 128).transpose(0, 2, 1))
    shared["dtb"] = np.ascontiguousarray(np.concatenate([np.asarray(inp["dt_bias_f"], f32),
                                                         np.asarray(inp["dt_bias_b"], f32)], axis=1))
    shared["alog"] = np.ascontiguousarray(np.concatenate([np.asarray(inp["a_log_f"], f32),
                                                          np.asarray(inp["a_log_b"], f32)], axis=1))
    shared["dsk"] = np.ascontiguousarray(np.asarray(inp["d_skip"], f32))
    shared["ident"] = np.eye(128, dtype=f32).astype(bf)
    j = np.arange(128)[:, None]
    s = np.arange(128)[None, :]
    masks = np.stack([(j > s), (j < s), (j <= s), (j >= s), np.ones((128, 128), bool)], axis=1).astype(f32)
    shared["masks"] = np.ascontiguousarray(masks)
    c = np.arange(128, dtype=np.float64)
    ang = 2 * np.pi * np.outer(c, c) / 128.0
    shared["csc"] = np.concatenate([np.cos(ang), np.sin(ang)], axis=1).astype(bf)
    tabs = {True: _dft_tables(True), False: _dft_tables(False)}
    xp = np.asarray(inp["x_prompt"], f32)
    xs = np.asarray(inp["x_sample"], f32)
    cp = np.asarray(inp["c_prompt"], f32)
    csm = np.asarray(inp["c_sample"], f32)
    maps = []
    for core in range(8):
        if core < 4:
            xin = np.concatenate([xp[core], xs[core]], axis=0)
            cc = np.stack([cp[core], cp[core], csm[core]], axis=0)
            coupled = True
        else:
            ids = [4 + 3 * (core - 4) + q for q in range(3)]
            xin = np.concatenate([xs[q] for q in ids], axis=0)
            cc = np.stack([csm[q] for q in ids], axis=0)
            coupled = False
        m = dict(shared)
        m["xin"] = np.ascontiguousarray(xin)
        m["cT"] = np.ascontiguousarray(cc.reshape(3, 8, 128).transpose(2, 1, 0))
        m["flag"] = np.full((128, 1), 1.0 if coupled else 0.0, f32)
        m["tab"] = tabs[coupled]
        maps.append(m)
    return maps


_CACHE = {}


def kernel(**inputs):
    maps = _host_inputs(inputs)
    if "nc" not in _CACHE:
        _CACHE["nc"] = build_program()[0]
    nc = _CACHE["nc"]
    res = run_bass_kernel_spmd(nc, maps, core_ids=list(range(8)))
    outs = [np.asarray(r["yout"], dtype=np.float32) for r in res.results]
    y_prompt = np.stack([outs[c][:2 * SEG] for c in range(4)], axis=0)
    ys = [None] * 16
    for c in range(4):
        ys[c] = outs[c][2 * SEG:]
    for c in range(4, 8):
        for q in range(3):
            ys[4 + 3 * (c - 4) + q] = outs[c][q * SEG:(q + 1) * SEG]
    y_sample = np.stack(ys, axis=0)
    return (y_prompt, y_sample)
```

```python
import contextlib
import numpy as np
import ml_dtypes
import concourse.bass as bass
import concourse.mybir as mybir
from concourse.bass_utils import run_bass_kernel_spmd

F32 = mybir.dt.float32
BF16 = mybir.dt.bfloat16
AF = mybir.ActivationFunctionType
ALU = mybir.AluOpType

ENGINES = ("tensor", "scalar", "vector", "gpsimd", "sync")
COMPUTE = ("tensor", "scalar", "vector", "gpsimd")

D = 1024
T = 6144
SEG = 2048
NT = 12
NSUB = 48
DEPTH = 4
D_FF = 2816
IN_COLS = 7200
EPS = 1e-6
O_AV, O_AG, O_Z, O_XBC, O_DT, O_UC, O_GATE = 0, 512, 1024, 2048, 3584, 3616, 4128


class Res:
    __slots__ = ("name", "last_writer", "readers", "sem", "issued")

    def __init__(self, name):
        self.name = name
        self.last_writer = None
        self.readers = []
        self.sem = None
        self.issued = 0


class Op:
    __slots__ = ("eng", "fn", "reads", "writes", "dma", "semres", "deps", "signal", "token", "waits", "bar")

    def __init__(self, eng, fn, reads, writes, dma, semres, bar=False):
        self.eng = eng
        self.fn = fn
        self.reads = reads
        self.writes = writes
        self.dma = dma
        self.semres = semres
        self.deps = ()
        self.signal = False
        self.token = None
        self.waits = ()
        self.bar = bar


class _Rec:
    __slots__ = ("call",)

    def __init__(self):
        self.call = None

    def __getattr__(self, name):
        def f(*a, **k):
            self.call = (name, a, k)
            return None
        return f


class Prog:
    def __init__(self, nc):
        self.nc = nc
        self.ops = []
        self.stack = contextlib.ExitStack()
        self.all_res = []

    def sb(self, name, shape, dtype):
        return self.stack.enter_context(self.nc.sbuf_tensor(name, list(shape), dtype))

    def ps(self, name, shape, dtype=F32):
        return self.stack.enter_context(self.nc.psum_tensor(name, list(shape), dtype))

    def res(self, name=None):
        r = Res(name or f"r{len(self.all_res)}")
        self.all_res.append(r)
        return r

    def op(self, eng, fn, reads=(), writes=()):
        rec = _Rec()
        fn(rec)
        assert rec.call is not None
        self.ops.append(Op(eng, rec.call, tuple(reads), tuple(writes), False, None))

    def dma(self, eng, fn, reads=(), writes=(), semres=None):
        assert semres is not None
        rec = _Rec()
        fn(rec)
        assert rec.call is not None
        self.ops.append(Op(eng, rec.call, tuple(reads), tuple(writes), True, semres))

    def barrier(self):
        for e in ENGINES:
            self.ops.append(Op(e, None, (), (), False, None, bar=True))

    def finalize(self):
        nc = self.nc
        ops = self.ops
        last_on = {e: None for e in COMPUTE}
        i = 0
        n = len(ops)
        while i < n:
            o = ops[i]
            if o.bar:
                for e in COMPUTE:
                    if last_on[e] is not None:
                        ops[last_on[e]].signal = True
                for r in self.all_res:
                    r.last_writer = None
                    r.readers = []
                while i < n and ops[i].bar:
                    i += 1
                continue
            deps = set()
            for r in o.reads:
                if r.last_writer is not None:
                    deps.add(r.last_writer)
            for w in o.writes:
                if w.last_writer is not None:
                    deps.add(w.last_writer)
                for rd in w.readers:
                    deps.add(rd)
            deps.discard(i)
            dl = []
            for j in deps:
                oj = ops[j]
                if oj.eng == o.eng and not oj.dma and not o.dma:
                    if o.eng == "tensor":
                        continue
                    if not any((r.last_writer == j) for r in o.reads):
                        continue
                dl.append(j)
            o.deps = dl
            for j in dl:
                ops[j].signal = True
            for r in o.reads:
                r.readers.append(i)
            for w in o.writes:
                w.last_writer = i
                w.readers = []
            if not o.dma and o.eng in last_on:
                last_on[o.eng] = i
            i += 1
        for e in COMPUTE:
            if last_on[e] is not None:
                ops[last_on[e]].signal = True
        sems = {e: None for e in COMPUTE}
        counts = {e: 0 for e in COMPUTE}
        active = []
        free_sems = []
        sem_final = {}
        bar_snap = {}
        for idx, o in enumerate(ops):
            if o.bar:
                if idx not in bar_snap:
                    snap = [("e", e, sems[e], counts[e]) for e in COMPUTE if counts[e] > 0]
                    snap += [("d", id(r.sem), r.sem, r.issued) for r in active]
                    for r in active:
                        free_sems.append((r.sem, r.issued))
                        r.sem = None
                    active = []
                    j = idx
                    while j < len(ops) and ops[j].bar:
                        bar_snap[j] = snap
                        j += 1
                continue
            if o.dma:
                r = o.semres
                if r.sem is None:
                    if free_sems:
                        r.sem, r.issued = free_sems.pop()
                    else:
                        r.sem = nc.alloc_semaphore(name=f"d_{r.name}")
                        r.issued = 0
                    active.append(r)
                r.issued += 16
                o.token = ("d", r.sem, r.issued)
                sem_final[id(r.sem)] = (r.sem, r.issued)
            elif o.signal:
                if sems[o.eng] is None:
                    sems[o.eng] = nc.alloc_semaphore(name=f"e_{o.eng}")
                counts[o.eng] += 1
                o.token = ("e", o.eng, counts[o.eng])
        waited = {e: {} for e in ENGINES}
        issued_sofar = {}
        for idx, o in enumerate(ops):
            need = {}
            if o.bar:
                for kind, key, semh, val in bar_snap[idx]:
                    need[(kind, key)] = (semh if kind == "d" else sems[key], val)
            else:
                for j in o.deps:
                    kind, key, val = ops[j].token
                    if kind == "d":
                        val = max(val, issued_sofar.get(id(key), 0))
                        k = ("d", id(key))
                        semh = key
                    else:
                        k = ("e", key)
                        semh = sems[key]
                    if need.get(k, (None, 0))[1] < val:
                        need[k] = (semh, val)
            w = []
            wd = waited[o.eng]
            for k, (semh, val) in need.items():
                if wd.get(k, 0) >= val:
                    continue
                wd[k] = val
                w.append((semh, val))
            o.waits = w
            if o.dma:
                issued_sofar[id(o.token[1])] = o.token[2]
        per_eng = {e: [o for o in ops if o.eng == e] for e in ENGINES}
        final = list(sem_final.values())
        final += [(sems[e], counts[e]) for e in COMPUTE if counts[e] > 0]
        self.n_ops = len(ops)
        with nc.Block() as block:
            def make(ename):
                def body(eng):
                    for o in per_eng[ename]:
                        for semh, val in o.waits:
                            eng.wait_ge(semh, val)
                        if o.fn is None:
                            continue
                        name, a, k = o.fn
                        ins = getattr(eng, name)(*a, **k)
                        if o.dma:
                            ins.then_inc(o.token[1], 16)
                        elif o.signal:
                            ins.then_inc(sems[o.eng], 1)
                    if ename == "sync":
                        for semh, val in final:
                            eng.wait_ge(semh, val)
                return body
            block.tensor(make("tensor"))
            block.scalar(make("scalar"))
            block.vector(make("vector"))
            block.gpsimd(make("gpsimd"))
            block.sync(make("sync"))
        self.stack.close()


class Arena:
    def __init__(self, ap2d, n):
        self.ap = ap2d
        self.n = n
        self.off = 0

    def reset(self):
        self.off = 0

    def take(self, *shape):
        size = int(np.prod(shape))
        assert self.off + size <= self.n, (self.off, size, self.n)
        v = self.ap[:, self.off:self.off + size]
        self.off += size
        if len(shape) == 2:
            v = v.rearrange("p (a b) -> p a b", a=shape[0])
        elif len(shape) == 3:
            v = v.rearrange("p (a b c) -> p a b c", a=shape[0], b=shape[1])
        return v


def bc(ap, shape):
    return ap.to_broadcast(list(shape))


def build_program(nl=DEPTH, debug=False, stop_after=None):
    nc = bass.Bass("TRN2", target_bir_lowering=False)

    def din(name, shape, dt=F32):
        return nc.dram_tensor(name, list(shape), dt, kind="ExternalInput").ap()

    skind = "ExternalOutput" if debug else "Internal"

    def dscr(name, shape, dt):
        return nc.dram_tensor(name, list(shape), dt, kind=skind).ap()

    xin = din("xin", [T, D])
    cT_d = din("cT", [128, 8, 3])
    flag_d = din("flag", [128, 1])
    w_ada = din("w_ada", [DEPTH, D, 6 * D])
    b_ada = din("b_ada", [DEPTH, 6 * D])
    gvec = {k: din(k, [DEPTH, D]) for k in ("g_pre_mix", "g_post_mix", "g_pre_ffn", "g_post_ffn", "g_ssd")}
    w_in = din("w_in", [DEPTH, D, IN_COLS])
    caw_d = din("caw", [DEPTH, 128, 4, 31])
    cab_d = din("cab", [DEPTH, 128, 4])
    lng_d = din("lng", [DEPTH, 128, 4])
    lnb_d = din("lnb", [DEPTH, 128, 4])
    w_a_out = din("w_a_out", [DEPTH, 512, D])
    csw_d = din("csw", [DEPTH, 128, 12, 5])
    csb_d = din("csb", [DEPTH, 128, 12])
    dtb_d = din("dtb", [DEPTH, 32])
    alog_d = din("alog", [DEPTH, 32])
    dsk_d = din("dsk", [DEPTH, 16])
    w_b_out = din("w_b_out", [DEPTH, D, D])
    w_c_out = din("w_c_out", [DEPTH, 512, D])
    w_out = din("w_out", [DEPTH, D, D])
    w_ffn_in = din("w_ffn_in", [DEPTH, D, 2 * D_FF])
    w_ffn_out = din("w_ffn_out", [DEPTH, D_FF, D])
    ident_d = din("ident", [128, 128], BF16)
    masks_d = din("masks", [128, 5, 128])
    csc_d = din("csc", [128, 256], BF16)
    tab_d = din("tab", [5, 4, 2, 128, 16, 512], BF16)
    yout = nc.dram_tensor("yout", [T, D], F32, kind="ExternalOutput").ap()

    mod_d = dscr("mod_d", [DEPTH, 3, 6 * D], F32)
    hfm_d = dscr("hfm_d", [8, 128, T], BF16)
    aglu_d = dscr("aglu_d", [4, 128, T], BF16)
    xbc_d = dscr("xbc_d", [12, 128, T], BF16)
    u_d = dscr("u_d", [4, 128, T], BF16)
    zs_d = dscr("zs_d", [T, D], BF16)
    dt_d = dscr("dt_d", [T, 32], F32)
    gates_d = dscr("gates_d", [24, 128, T], BF16)
    acv_d = dscr("acv_d", [4, 128, T], BF16)
    xs_d = dscr("xs_d", [T, D], BF16)
    bt_d = dscr("bt_d", [T, 256], BF16)
    bc_d = dscr("bc_d", [4, 128, T], BF16)
    yf_d = dscr("yf_d", [T, D], F32)
    yfm_d = dscr("yfm_d", [8, 128, T], BF16)
    f_d = dscr("f_d", [4, 128, T], BF16)
    act_d = dscr("act_d", [22, 128, T], BF16)

    P = Prog(nc)
    ABF = 73 * 1024
    AFP = 13 * 1024
    arena_bf_t = P.sb("arena_bf", [128, ABF], BF16)
    arena_f_t = P.sb("arena_f", [128, AFP], F32)
    AB = Arena(arena_bf_t[:], ABF)
    AFa = Arena(arena_f_t[:], AFP)
    ident = P.sb("ident_sb", [128, 128], BF16)
    masks = P.sb("masks_sb", [128, 5, 128], F32)
    flag = P.sb("flag_sb", [128, 1], F32)
    r_const = P.res("const")
    psum = [P.ps(f"psb{i}", [128, 512], F32) for i in range(8)]
    r_ps = [P.res(f"ps{i}") for i in range(8)]

    def fm_tile(dram, c0, c1, t0, n):
        return dram[c0:c1, :, t0:t0 + n].rearrange("c p t -> p c t")

    P.dma("sync", lambda e: e.dma_start(out=ident[:], in_=ident_d), writes=[r_const], semres=r_const)
    P.dma("sync", lambda e: e.dma_start(out=masks[:], in_=masks_d), writes=[r_const], semres=r_const)
    P.dma("sync", lambda e: e.dma_start(out=flag[:], in_=flag_d), writes=[r_const], semres=r_const)
    M_GT, M_LT, M_LE, M_GE, M_ONE = range(5)

    def new_phase():
        P.barrier()
        AB.reset()
        AFa.reset()

    def phase_mod():
        new_phase()
        cT = AFa.take(8, 3)
        r_cT = P.res("cT")
        sil = AFa.take(8, 3)
        r_sil = P.res("sil")
        P.dma("sync", lambda e: e.dma_start(out=cT, in_=cT_d), writes=[r_cT], semres=r_cT)
        P.op("scalar", lambda e: e.activation(out=sil, in_=cT, func=AF.Silu), reads=[r_cT], writes=[r_sil])
        NB = 4
        wbuf = [AFa.take(8, 256) for _ in range(NB)]
        r_w = [P.res(f"wada{i}") for i in range(NB)]
        brow = [AFa.take(256) for _ in range(2)]
        mrow = [AFa.take(256) for _ in range(2)]
        r_b = [P.res("brow0"), P.res("brow1")]
        r_m = [P.res("mrow0"), P.res("mrow1")]
        k = 0
        for l in range(nl):
            for ct in range(24):
                wb, rw = wbuf[k % NB], r_w[k % NB]
                mr, rm = mrow[k % 2], r_m[k % 2]
                br, rb = brow[k % 2], r_b[k % 2]
                pb = k % 2
                q = "sync" if k % 2 == 0 else "gpsimd"
                k += 1
                src = w_ada[l, :, ct * 256:(ct + 1) * 256].rearrange("(c p) n -> p c n", p=128)
                P.dma(q, lambda e, wb=wb, src=src: e.dma_start(out=wb, in_=src), writes=[rw], semres=rw)
                bsrc = b_ada[l:l + 1, ct * 256:(ct + 1) * 256].partition_broadcast(3)
                P.dma("sync", lambda e, bsrc=bsrc, br=br: e.dma_start(out=br[0:3, :], in_=bsrc), writes=[rb], semres=rb)
                for c in range(8):
                    P.op("tensor", lambda e, c=c, wb=wb, pb=pb: e.matmul(psum[pb][0:3, 0:256], lhsT=sil[:, c, :],
                                                                          rhs=wb[:, c, :], start=(c == 0), stop=(c == 7)),
                         reads=[r_sil, rw], writes=[r_ps[pb]])
                P.op("vector", lambda e, mr=mr, br=br, pb=pb: e.tensor_tensor(out=mr[0:3, :], in0=psum[pb][0:3, 0:256],
                                                                               in1=br[0:3, :], op=ALU.add),
                     reads=[r_ps[pb], rb], writes=[rm])
                dst = mod_d[l, :, ct * 256:(ct + 1) * 256]
                P.dma("sync", lambda e, mr=mr, dst=dst: e.dma_start(out=dst, in_=mr[0:3, :]), reads=[rm], semres=rm)

    def load_rows(l, slot, arena_tiles, spec):
        for dst, rr, kind, gname, part, tmp, rtmp in spec:
            msrc = mod_d[l, slot:slot + 1, part * D:(part + 1) * D].partition_broadcast(128)
            if kind == "shift":
                P.dma("sync", lambda e, dst=dst, msrc=msrc: e.dma_start(out=dst, in_=msrc), writes=[rr], semres=rr)
                continue
            gsrc = gvec[gname][l:l + 1, :].partition_broadcast(128)
            P.dma("sync", lambda e, dst=dst, msrc=msrc: e.dma_start(out=dst, in_=msrc), writes=[rr], semres=rr)
            P.dma("sync", lambda e, tmp=tmp, gsrc=gsrc: e.dma_start(out=tmp, in_=gsrc), writes=[rtmp], semres=rtmp)
            if kind == "scale":
                P.op("vector", lambda e, dst=dst, tmp=tmp: e.scalar_tensor_tensor(out=dst, in0=dst, scalar=1.0, in1=tmp,
                                                                                  op0=ALU.add, op1=ALU.mult),
                     reads=[rr, rtmp], writes=[rr])
            else:
                P.op("gpsimd", lambda e, dst=dst, tmp=tmp: e.tensor_tensor(out=dst, in0=dst, in1=tmp, op=ALU.mult),
                     reads=[rr, rtmp], writes=[rr])

    def load_w(dst, src, rr):
        P.dma("gpsimd", lambda e: e.dma_start(out=dst, in_=src), writes=[rr], semres=rr)

    def rstd_from_ss(ss, rstd, r_ss, r_rstd, n_feat):
        P.op("scalar", lambda e: e.activation(out=rstd, in_=ss, func=AF.Ln, scale=1.0 / n_feat, bias=EPS),
             reads=[r_ss], writes=[r_rstd])
        P.op("scalar", lambda e: e.activation(out=rstd, in_=rstd, func=AF.Exp, scale=-0.5),
             reads=[r_rstd], writes=[r_rstd])

    def load_x_tile(xsrc, t, xt, r_xt):
        tok0 = t * 512
        for j in range(4):
            q = (t % 2) * 4 + j
            P.dma("sync", lambda e, j=j, q=q: e.dma_start(out=xt[q], in_=xsrc[tok0 + j * 128: tok0 + (j + 1) * 128, :]),
                  writes=[r_xt[q]], semres=r_xt[q])

    def norm_transpose_tile(xsrc, t, xt, r_xt, junk, r_junk, ss, r_ss, rstd, r_rstd, gs, r_gs, sh, r_sh, tmp, r_tmp,
                            xh, r_xh, hfm, r_hfm, pbank):
        for j in range(4):
            q = (t % 2) * 4 + j
            P.op("scalar", lambda e, j=j, q=q: e.activation(out=junk, in_=xt[q], func=AF.Square, accum_out=ss[:, j:j + 1]),
                 reads=[r_xt[q]], writes=[r_junk, r_ss])
        rstd_from_ss(ss, rstd, r_ss, r_rstd, D)
        yield

        def part_a(j):
            q = (t % 2) * 4 + j
            P.op("vector", lambda e: e.scalar_tensor_tensor(out=tmp, in0=xt[q], scalar=rstd[:, j:j + 1], in1=gs,
                                                            op0=ALU.mult, op1=ALU.mult),
                 reads=[r_xt[q], r_rstd, r_gs], writes=[r_tmp])
            P.op("vector", lambda e: e.tensor_tensor(out=xh[j % 2], in0=tmp, in1=sh, op=ALU.add),
                 reads=[r_tmp, r_sh], writes=[r_xh[j % 2]])

        def part_b(j):
            pT = psum[pbank][:, :].bitcast(BF16)
            for c in range(8):
                P.op("tensor", lambda e, c=c: e.transpose(out=pT[:, c * 128:(c + 1) * 128],
                                                          in_=xh[j % 2][:, c * 128:(c + 1) * 128], identity=ident[:]),
                     reads=[r_xh[j % 2], r_const], writes=[r_ps[pbank]])
            P.op("scalar", lambda e: e.copy(out=hfm[:, :, j * 128:(j + 1) * 128],
                                            in_=pT.rearrange("p (c t) -> p c t", c=8)),
                 reads=[r_ps[pbank]], writes=[r_hfm])

        part_a(0)
        yield
        for j in range(4):
            part_b(j)
            if j < 3:
                part_a(j + 1)
            yield

    def phase_a(l, xsrc):
        new_phase()
        NCOL = O_GATE
        wA = AB.take(8, NCOL)
        bounds = [0, 1024, 2048, 3072, NCOL]
        r_wAp = [P.res(f"wA{i}") for i in range(4)]
        for pi in (0, 2, 3, 1):
            c0, c1 = bounds[pi], bounds[pi + 1]
            load_w(wA[:, :, c0:c1], w_in[l, :, c0:c1].rearrange("(c p) n -> p c n", p=128), r_wAp[pi])

        def rwA(col):
            return r_wAp[min(col // 1024, 3)]
        hfm = [AB.take(8, 512) for _ in range(2)]
        r_hfm = [P.res("hfm0"), P.res("hfm1")]
        xh = [AB.take(D) for _ in range(2)]
        r_xh = [P.res("xh0"), P.res("xh1")]
        sg = AB.take(4, 512)
        r_sg = P.res("sg")
        oc = [AB.take(512) for _ in range(8)]
        r_oc = [P.res(f"oc{i}") for i in range(8)]
        zs = [AB.take(D) for _ in range(2)]
        r_zs = [P.res("zs0"), P.res("zs1")]
        xt = [AFa.take(D) for _ in range(8)]
        r_xt = [P.res(f"xt{i}") for i in range(8)]
        gs = AFa.take(D)
        r_gs = P.res("gs")
        sh = AFa.take(D)
        r_sh = P.res("sh")
        tmp = AFa.take(D)
        r_tmp = P.res("tmp")
        gtmp = AFa.take(D)
        r_gtmp = P.res("gtmp")
        ss = AFa.take(4)
        r_ss = P.res("ss")
        rstd = AFa.take(4)
        r_rstd = P.res("rstd")
        dtr = [AFa.take(32) for _ in range(2)]
        r_dtr = [P.res("dtr0"), P.res("dtr1")]
        r_junk = P.res("junk")
        ocn = [0]
        pb = [0]

        def next_oc():
            i = ocn[0] % 8
            ocn[0] += 1
            return oc[i], r_oc[i]

        def next_pb():
            i = 1 + (pb[0] % 6)
            pb[0] += 1
            return i

        def prep(tt):
            if tt % 4 == 0:
                load_rows(l, tt // 4, None, [(gs, r_gs, "scale", "g_pre_mix", 1, gtmp, r_gtmp),
                                             (sh, r_sh, "shift", None, 0, None, None)])
            hh, rhh = hfm[tt % 2], r_hfm[tt % 2]
            yield from norm_transpose_tile(xsrc, tt, xt, r_xt, tmp, r_tmp, ss, r_ss, rstd, r_rstd, gs, r_gs, sh, r_sh, tmp,
                                           r_tmp, xh, r_xh, hh, rhh, 0)
            P.dma("sync", lambda e: e.dma_start(out=fm_tile(hfm_d, 0, 8, tt * 512, 512), in_=hh), reads=[rhh], semres=rhh)

        load_x_tile(xsrc, 0, xt, r_xt)
        load_x_tile(xsrc, 1, xt, r_xt)
        for _ in prep(0):
            pass
        for t in range(NT):
            slot = t // 4
            tok0 = t * 512
            if t + 2 < NT:
                load_x_tile(xsrc, t + 2, xt, r_xt)
            h, rh = hfm[t % 2], r_hfm[t % 2]
            nchunk = [0]
            gen = prep(t + 1) if t + 1 < NT else iter(())

            def fm_chunk(col0, epi, gen=gen):
                nchunk[0] += 1
                if nchunk[0] in (4, 7, 10, 13, 16, 19, 22):
                    next(gen, None)
                b = next_pb()
                for c in range(8):
                    P.op("tensor", lambda e, c=c, b=b: e.matmul(psum[b][:, :], lhsT=wA[:, c, col0:col0 + 128],
                                                                 rhs=h[:, c, :], start=(c == 0), stop=(c == 7)),
                         reads=[rwA(col0), rh], writes=[r_ps[b]])
                epi(b)

            for i in range(4):
                fm_chunk(O_AG + i * 128, lambda b, i=i: P.op(
                    "scalar", lambda e: e.activation(out=sg[:, i, :], in_=psum[b][:, :], func=AF.Sigmoid),
                    reads=[r_ps[b]], writes=[r_sg]))
            for i in range(4):
                def epi(b, i=i):
                    o, ro = next_oc()
                    P.op("vector", lambda e: e.tensor_tensor(out=o, in0=psum[b][:, :], in1=sg[:, i, :], op=ALU.mult),
                         reads=[r_ps[b], r_sg], writes=[ro])
                    P.dma("sync", lambda e: e.dma_start(out=aglu_d[i, :, tok0:tok0 + 512], in_=o), reads=[ro], semres=ro)
                fm_chunk(O_AV + i * 128, epi)
            for i in range(12):
                def epi(b, i=i):
                    o, ro = next_oc()
                    P.op("scalar", lambda e: e.copy(out=o, in_=psum[b][:, :]), reads=[r_ps[b]], writes=[ro])
                    P.dma("sync", lambda e: e.dma_start(out=xbc_d[i, :, tok0:tok0 + 512], in_=o), reads=[ro], semres=ro)
                fm_chunk(O_XBC + i * 128, epi)
            for i in range(4):
                def epi(b, i=i):
                    o, ro = next_oc()
                    P.op("vector", lambda e: e.tensor_copy(out=o, in_=psum[b][:, :]), reads=[r_ps[b]], writes=[ro])
                    P.dma("sync", lambda e: e.dma_start(out=u_d[i, :, tok0:tok0 + 512], in_=o), reads=[ro], semres=ro)
                fm_chunk(O_UC + i * 128, epi)
            for j in range(4):
                z, rz = zs[j % 2], r_zs[j % 2]
                for half in range(2):
                    b = next_pb()
                    for c in range(8):
                        P.op("tensor", lambda e, c=c, b=b, j=j, half=half: e.matmul(
                            psum[b][:, :], lhsT=h[:, c, j * 128:(j + 1) * 128],
                            rhs=wA[:, c, O_Z + half * 512: O_Z + (half + 1) * 512], start=(c == 0), stop=(c == 7)),
                            reads=[rwA(O_Z), rh], writes=[r_ps[b]])
                    P.op("scalar", lambda e, b=b, z=z, half=half: e.activation(out=z[:, half * 512:(half + 1) * 512],
                                                                               in_=psum[b][:, :], func=AF.Silu),
                         reads=[r_ps[b]], writes=[rz])
                P.dma("sync", lambda e, z=z, j=j: e.dma_start(out=zs_d[tok0 + j * 128: tok0 + (j + 1) * 128, :], in_=z),
                      reads=[rz], semres=rz)
                b = next_pb()
                dd, rd = dtr[j % 2], r_dtr[j % 2]
                for c in range(8):
                    P.op("tensor", lambda e, c=c, b=b, j=j: e.matmul(
                        psum[b][:, 0:32], lhsT=h[:, c, j * 128:(j + 1) * 128], rhs=wA[:, c, O_DT:O_DT + 32],
                        start=(c == 0), stop=(c == 7)), reads=[rwA(O_DT), rh], writes=[r_ps[b]])
                P.op("vector", lambda e, b=b, dd=dd: e.tensor_copy(out=dd, in_=psum[b][:, 0:32]),
                     reads=[r_ps[b]], writes=[rd])
                P.dma("sync", lambda e, dd=dd, j=j: e.dma_start(out=dt_d[tok0 + j * 128: tok0 + (j + 1) * 128, :], in_=dd),
                      reads=[rd], semres=rd)
            for _ in gen:
                pass

    def phase_gates(l):
        new_phase()
        wG = AB.take(8, 3072)
        r_wGp = [P.res(f"wG{i}") for i in range(6)]
        for pi in range(6):
            c0 = pi * 512
            load_w(wG[:, :, c0:c0 + 512], w_in[l, :, O_GATE + c0:O_GATE + c0 + 512].rearrange("(c p) n -> p c n", p=128),
                   r_wGp[pi])
        hfm = [AB.take(8, 512) for _ in range(2)]
        r_hfm = [P.res("ghfm0"), P.res("ghfm1")]
        oc = [AB.take(512) for _ in range(8)]
        r_oc = [P.res(f"goc{i}") for i in range(8)]
        k = 0

        def ldh(t):
            h, rh = hfm[t % 2], r_hfm[t % 2]
            P.dma("sync", lambda e: e.dma_start(out=h, in_=fm_tile(hfm_d, 0, 8, t * 512, 512)), writes=[rh], semres=rh)

        ldh(0)
        for t in range(NT):
            tok0 = t * 512
            h, rh = hfm[t % 2], r_hfm[t % 2]
            if t + 1 < NT:
                ldh(t + 1)
            for i in range(24):
                b = k % 8
                o, ro = oc[k % 8], r_oc[k % 8]
                k += 1
                for c in range(8):
                    P.op("tensor", lambda e, c=c, b=b, i=i, h=h: e.matmul(psum[b][:, :], lhsT=wG[:, c, i * 128:(i + 1) * 128],
                                                                           rhs=h[:, c, :], start=(c == 0), stop=(c == 7)),
                         reads=[r_wGp[i // 4], rh], writes=[r_ps[b]])
                P.op("scalar", lambda e, b=b, o=o: e.activation(out=o, in_=psum[b][:, :], func=AF.Sigmoid),
                     reads=[r_ps[b]], writes=[ro])
                P.dma("sync", lambda e, o=o, i=i, tok0=tok0: e.dma_start(out=gates_d[i, :, tok0:tok0 + 512], in_=o),
                      reads=[ro], semres=ro)

    def load_halo(dst, rr, dram, C, t, hw):
        tok0 = t * 512
        seg_start = (t % 4 == 0)
        seg_end = (t % 4 == 3)
        lo = tok0 - hw
        hi = tok0 + 512 + hw
        d0 = 0
        if seg_start and t != 4:
            lo = tok0
            d0 = hw
        if seg_end and t != 3:
            hi = tok0 + 512
        if lo > tok0 - hw:
            P.op("gpsimd", lambda e: e.memset(dst[:, :, 0:hw], 0.0), writes=[rr])
        if hi < tok0 + 512 + hw:
            P.op("gpsimd", lambda e: e.memset(dst[:, :, 512 + hw:512 + 2 * hw], 0.0), writes=[rr])
        P.dma("sync", lambda e: e.dma_start(out=dst[:, :, d0:d0 + (hi - lo)], in_=fm_tile(dram, 0, C, lo, hi - lo)),
              writes=[rr], semres=rr)
        if t == 4:
            P.op("gpsimd", lambda e: e.tensor_scalar(out=dst[:, :, 0:hw], in0=dst[:, :, 0:hw], scalar1=flag[:, 0:1],
                                                     scalar2=None, op0=ALU.mult), reads=[rr, r_const], writes=[rr])
        if t == 3:
            P.op("gpsimd", lambda e: e.tensor_scalar(out=dst[:, :, 512 + hw:512 + 2 * hw],
                                                     in0=dst[:, :, 512 + hw:512 + 2 * hw], scalar1=flag[:, 0:1],
                                                     scalar2=None, op0=ALU.mult), reads=[rr, r_const], writes=[rr])

    def conv_a_gen(l):
        caw = AFa.take(4, 31)
        cab = AFa.take(4)
        lng = AFa.take(4)
        lnb = AFa.take(4)
        r_par = P.res("cpar")
        for dst, src in ((caw, caw_d[l]), (cab, cab_d[l]), (lng, lng_d[l]), (lnb, lnb_d[l])):
            P.dma("sync", lambda e, dst=dst, src=src: e.dma_start(out=dst, in_=src), writes=[r_par], semres=r_par)
        dg = AB.take(4 * 31, 128)
        r_dg = P.res("dg")
        identf = AFa.take(128)
        r_idf = P.res("identf")
        P.op("vector", lambda e: e.tensor_copy(out=identf, in_=ident[:]), reads=[r_const], writes=[r_idf])
        for i in range(4):
            P.op("vector", lambda e, i=i: e.tensor_tensor(
                out=dg[:, i * 31:(i + 1) * 31, :], in0=bc(identf.unsqueeze(1), [128, 31, 128]),
                in1=bc(caw[:, i, :].unsqueeze(2), [128, 31, 128]), op=ALU.mult), reads=[r_idf, r_par], writes=[r_dg])
        ain = [AB.take(4, 542) for _ in range(2)]
        r_ain = [P.res("ain0"), P.res("ain1")]
        acc = AFa.take(4, 512)
        r_acc = [P.res(f"acc{i}") for i in range(4)]
        sq = [AFa.take(512) for _ in range(2)]
        r_sq = [P.res("sq0"), P.res("sq1")]
        mean = AFa.take(512)
        r_mean = P.res("mean")
        rs = AFa.take(512)
        r_rs = P.res("rs")
        xc = [AFa.take(512) for _ in range(2)]
        r_xc = [P.res("xc0"), P.res("xc1")]
        ob = [AB.take(512) for _ in range(8)]
        r_ob = [P.res(f"ob{i}") for i in range(8)]
        ones = masks[:, M_ONE, :]
        nb = 0
        load_halo(ain[0], r_ain[0], aglu_d, 4, 0, 15)
        for t in range(NT):
            tok0 = t * 512
            a, ra = ain[t % 2], r_ain[t % 2]
            if t + 1 < NT:
                load_halo(ain[(t + 1) % 2], r_ain[(t + 1) % 2], aglu_d, 4, t + 1, 15)
            for i in range(4):
                b = 2 + nb % 6
                nb += 1
                for k in range(31):
                    P.op("tensor", lambda e, i=i, k=k, a=a, b=b: e.matmul(psum[b][:, :], lhsT=dg[:, i * 31 + k, :],
                                                                           rhs=a[:, i, k:k + 512], start=(k == 0),
                                                                           stop=(k == 30)),
                         reads=[r_dg, ra], writes=[r_ps[b]])
                P.op("scalar", lambda e, i=i, b=b: e.activation(out=acc[:, i, :], in_=psum[b][:, :], func=AF.Identity,
                                                                bias=cab[:, i:i + 1]),
                     reads=[r_ps[b], r_par], writes=[r_acc[i]])
            for i in range(4):
                P.op("tensor", lambda e, i=i: e.matmul(psum[0][:, :], lhsT=ones, rhs=acc[:, i, :], start=(i == 0),
                                                       stop=(i == 3)), reads=[r_acc[i], r_const], writes=[r_ps[0]])
            for i in range(4):
                P.op("gpsimd", lambda e, i=i: e.tensor_tensor(out=sq[i % 2], in0=acc[:, i, :], in1=acc[:, i, :],
                                                              op=ALU.mult), reads=[r_acc[i]], writes=[r_sq[i % 2]])
                P.op("tensor", lambda e, i=i: e.matmul(psum[1][:, :], lhsT=ones, rhs=sq[i % 2], start=(i == 0),
                                                       stop=(i == 3)), reads=[r_sq[i % 2], r_const], writes=[r_ps[1]])
            P.op("vector", lambda e: e.tensor_scalar(out=mean, in0=psum[0][:, :], scalar1=1.0 / 512, scalar2=None,
                                                     op0=ALU.mult), reads=[r_ps[0]], writes=[r_mean])
            P.op("vector", lambda e: e.tensor_tensor(out=rs, in0=mean, in1=mean, op=ALU.mult), reads=[r_mean], writes=[r_rs])
            P.op("vector", lambda e: e.scalar_tensor_tensor(out=rs, in0=psum[1][:, :], scalar=1.0 / 512, in1=rs,
                                                            op0=ALU.mult, op1=ALU.subtract),
                 reads=[r_ps[1], r_rs], writes=[r_rs])
            P.op("scalar", lambda e: e.activation(out=rs, in_=rs, func=AF.Ln, bias=EPS), reads=[r_rs], writes=[r_rs])
            P.op("scalar", lambda e: e.activation(out=rs, in_=rs, func=AF.Exp, scale=-0.5), reads=[r_rs], writes=[r_rs])
            for i in range(4):
                o, ro = ob[(t * 4 + i) % 8], r_ob[(t * 4 + i) % 8]
                x_, rx_ = xc[i % 2], r_xc[i % 2]
                P.op("vector", lambda e, i=i, x_=x_: e.tensor_tensor(out=x_, in0=acc[:, i, :], in1=mean, op=ALU.subtract),
                     reads=[r_acc[i], r_mean], writes=[rx_])
                P.op("gpsimd", lambda e, x_=x_: e.tensor_tensor(out=x_, in0=x_, in1=rs, op=ALU.mult), reads=[rx_, r_rs],
                     writes=[rx_])
                P.op("scalar", lambda e, i=i, o=o, x_=x_: e.activation(out=o, in_=x_, func=AF.Silu, scale=lng[:, i:i + 1],
                                                                       bias=lnb[:, i:i + 1]),
                     reads=[rx_, r_par], writes=[ro])
                P.dma("sync", lambda e, i=i, o=o, tok0=tok0: e.dma_start(out=acv_d[i, :, tok0:tok0 + 512], in_=o),
                      reads=[ro], semres=ro)
            yield

    def conv_s_gen(l):
        csw = AFa.take(12, 5)
        csb = AFa.take(12)
        r_par = P.res("spar")
        P.dma("sync", lambda e: e.dma_start(out=csw, in_=csw_d[l]), writes=[r_par], semres=r_par)
        P.dma("sync", lambda e: e.dma_start(out=csb, in_=csb_d[l]), writes=[r_par], semres=r_par)
        dgs = AB.take(60, 128)
        r_dgs = P.res("dgs")
        identf = AFa.take(128)
        r_idf = P.res("sidentf")
        P.op("vector", lambda e: e.tensor_copy(out=identf, in_=ident[:]), reads=[r_const], writes=[r_idf])
        for i in range(12):
            P.op("vector", lambda e, i=i: e.tensor_tensor(
                out=dgs[:, i * 5:(i + 1) * 5, :], in0=bc(identf.unsqueeze(1), [128, 5, 128]),
                in1=bc(csw[:, i, :].unsqueeze(2), [128, 5, 128]), op=ALU.mult), reads=[r_idf, r_par], writes=[r_dgs])
        xi = [AB.take(12, 516) for _ in range(2)]
        r_xi = [P.res("xi0"), P.res("xi1")]
        xo = [AB.take(12, 512) for _ in range(2)]
        r_xo = [[P.res(f"xo{b}_{i}") for i in range(12)] for b in range(2)]
        xs = [AB.take(D) for _ in range(2)]
        r_xs = [P.res("xs0"), P.res("xs1")]
        bt = [AB.take(256) for _ in range(2)]
        r_bt = [P.res("bt0"), P.res("bt1")]
        nb = 0
        load_halo(xi[0], r_xi[0], xbc_d, 12, 0, 2)
        for t in range(NT):
            tok0 = t * 512
            a, ra = xi[t % 2], r_xi[t % 2]
            o, ro = xo[t % 2], r_xo[t % 2]
            if t + 1 < NT:
                load_halo(xi[(t + 1) % 2], r_xi[(t + 1) % 2], xbc_d, 12, t + 1, 2)
            for i in range(12):
                b = 4 + nb % 4
                nb += 1
                for k in range(5):
                    P.op("tensor", lambda e, i=i, k=k, a=a, b=b: e.matmul(psum[b][:, :], lhsT=dgs[:, i * 5 + k, :],
                                                                           rhs=a[:, i, k:k + 512], start=(k == 0),
                                                                           stop=(k == 4)),
                         reads=[r_dgs, ra], writes=[r_ps[b]])
                P.op("scalar", lambda e, i=i, o=o, b=b: e.activation(out=o[:, i, :], in_=psum[b][:, :], func=AF.Silu,
                                                                     bias=csb[:, i:i + 1]),
                     reads=[r_ps[b], r_par], writes=[ro[i]])
            P.dma("sync", lambda e, o=o, tok0=tok0: e.dma_start(out=fm_tile(bc_d, 0, 4, tok0, 512), in_=o[:, 8:12, :]),
                  reads=ro[8:12], semres=ro[8])
            for j in range(4):
                pT = psum[j % 2][:, :].bitcast(BF16)
                x_, rx = xs[j % 2], r_xs[j % 2]
                for c in range(8):
                    P.op("tensor", lambda e, c=c, j=j, o=o, pT=pT: e.transpose(out=pT[:, c * 128:(c + 1) * 128],
                                                                               in_=o[:, c, j * 128:(j + 1) * 128],
                                                                               identity=ident[:]),
                         reads=[ro[c], r_const], writes=[r_ps[j % 2]])
                P.op("vector", lambda e, x_=x_, pT=pT: e.tensor_copy(out=x_, in_=pT), reads=[r_ps[j % 2]], writes=[rx])
                P.dma("sync", lambda e, x_=x_, j=j, tok0=tok0: e.dma_start(
                    out=xs_d[tok0 + j * 128: tok0 + (j + 1) * 128, :], in_=x_), reads=[rx], semres=rx)
                pB = psum[2 + j % 2][:, :].bitcast(BF16)
                b_, rb = bt[j % 2], r_bt[j % 2]
                for c in range(2):
                    P.op("tensor", lambda e, c=c, j=j, o=o, pB=pB: e.transpose(out=pB[:, c * 128:(c + 1) * 128],
                                                                               in_=o[:, 8 + c, j * 128:(j + 1) * 128],
                                                                               identity=ident[:]),
                         reads=[ro[8 + c], r_const], writes=[r_ps[2 + j % 2]])
                P.op("scalar", lambda e, b_=b_, pB=pB: e.copy(out=b_, in_=pB[:, 0:256]), reads=[r_ps[2 + j % 2]],
                     writes=[rb])
                P.dma("sync", lambda e, b_=b_, j=j, tok0=tok0: e.dma_start(
                    out=bt_d[tok0 + j * 128: tok0 + (j + 1) * 128, :], in_=b_), reads=[rb], semres=rb)
            yield

    def phase_convs(l):
        new_phase()
        gens = [conv_a_gen(l), conv_s_gen(l)]
        while gens:
            alive = []
            for g_ in gens:
                try:
                    next(g_)
                    alive.append(g_)
                except StopIteration:
                    pass
            gens = alive

    def phase_ssd(l):
        new_phase()
        dtb = AFa.take(32)
        arow = AFa.take(32)
        dsk = AFa.take(16)
        r_par = P.res("dpar")
        P.dma("sync", lambda e: e.dma_start(out=dtb, in_=dtb_d[l:l + 1, :].partition_broadcast(128)), writes=[r_par],
              semres=r_par)
        P.dma("sync", lambda e: e.dma_start(out=arow, in_=alog_d[l:l + 1, :].partition_broadcast(128)), writes=[r_par],
              semres=r_par)
        P.dma("sync", lambda e: e.dma_start(out=dsk, in_=dsk_d[l:l + 1, :].partition_broadcast(128)), writes=[r_par],
              semres=r_par)
        P.op("scalar", lambda e: e.activation(out=arow, in_=arow, func=AF.Exp), reads=[r_par], writes=[r_par])
        P.op("vector", lambda e: e.tensor_scalar(out=arow, in0=arow, scalar1=-1.0, scalar2=None, op0=ALU.mult),
             reads=[r_par], writes=[r_par])
        gsr = AFa.take(D)
        r_gsr = P.res("gsr")
        P.dma("sync", lambda e: e.dma_start(out=gsr, in_=gvec["g_ssd"][l:l + 1, :].partition_broadcast(128)),
              writes=[r_gsr], semres=r_gsr)
        r_yfd = [P.res(f"yfd{i}") for i in range(NSUB)]

        class S:
            pass

        def mk(d):
            b = S()
            n = f"d{d}"
            b.H = AFa.take(2, 512); b.r_H = P.res(n + "H")
            b.R = AFa.take(16, 128); b.r_R = P.res(n + "R")
            b.ytmp = AFa.take(D); b.r_ytmp = P.res(n + "ytmp")
            b.yfl = AFa.take(D); b.r_yfl = P.res(n + "yfl")
            b.small = [AFa.take(8, 16) for _ in range(2)]
            b.r_sm = [[P.res(n + f"sm{q}_{i}") for i in range(8)] for q in range(2)]
            b.cst = AFa.take(32); b.r_cst = P.res(n + "cst")
            b.dtr = [AFa.take(32) for _ in range(3)]; b.r_dtr = [P.res(n + f"dtr{i}") for i in range(3)]
            b.ss1 = AFa.take(1); b.r_ss1 = P.res(n + "ss1")
            b.Hb = AB.take(2, 512); b.r_Hb = P.res(n + "Hb")
            b.xs = [AB.take(D) for _ in range(3)]; b.r_xs = [P.res(n + f"xs{i}") for i in range(3)]
            b.bt = [AB.take(256) for _ in range(3)]; b.r_bt = [P.res(n + f"bt{i}") for i in range(3)]
            b.bcf = [AB.take(4, 128) for _ in range(3)]; b.r_bcf = [P.res(n + f"bcf{i}") for i in range(3)]
            b.zt = [AB.take(D) for _ in range(3)]; b.r_zt = [P.res(n + f"zt{i}") for i in range(3)]
            b.xdt = AB.take(D); b.r_xdt = P.res(n + "xdt")
            b.xw = AB.take(D); b.r_xw = P.res(n + "xw")
            b.Lm = AB.take(16, 128); b.r_Lm = P.res(n + "Lm")
            b.Mm = AB.take(16, 128); b.r_Mm = P.res(n + "Mm")
            b.smk = AB.take(2, 128); b.r_smk = P.res(n + "smk")
            b.stb = [AB.take(D) for _ in range(2)]; b.r_stb = [P.res(n + "stb0"), P.res(n + "stb1")]
            b.s16 = [AB.take(2, 16) for _ in range(2)]
            b.r_s16 = [[P.res(n + f"s16_{q}_{i}") for i in range(2)] for q in range(2)]
            b.ydg = [AB.take(D) for _ in range(2)]; b.r_ydg = [P.res(n + "ydg0"), P.res(n + "ydg1")]
            b.yn = AB.take(D); b.r_yn = P.res(n + "yn")
            b.yfm = [AB.take(8, 128) for _ in range(2)]; b.r_yfm = [P.res(n + "yfm0"), P.res(n + "yfm1")]
            return b

        BUF = [mk(0), mk(1)]
        HALF = NSUB // 2

        def chunk_of(d, n):
            return n if d == 0 else NSUB - 1 - n

        def loads(d, n):
            B = BUF[d]
            ci = chunk_of(d, n)
            tok0 = ci * 128
            q = n % 3
            P.dma("sync", lambda e: e.dma_start(out=B.dtr[q], in_=dt_d[tok0:tok0 + 128, :]), writes=[B.r_dtr[q]],
                  semres=B.r_dtr[q])
            P.dma("sync", lambda e: e.dma_start(out=B.bcf[q], in_=fm_tile(bc_d, 0, 4, tok0, 128)), writes=[B.r_bcf[q]],
                  semres=B.r_bcf[q])
            P.dma("sync", lambda e: e.dma_start(out=B.xs[q], in_=xs_d[tok0:tok0 + 128, :]), writes=[B.r_xs[q]],
                  semres=B.r_xs[q])
            P.dma("sync", lambda e: e.dma_start(out=B.bt[q], in_=bt_d[tok0:tok0 + 128, :]), writes=[B.r_bt[q]],
                  semres=B.r_bt[q])
            if n >= HALF:
                P.dma("sync", lambda e: e.dma_start(out=B.zt[q], in_=zs_d[tok0:tok0 + 128, :]), writes=[B.r_zt[q]],
                      semres=B.r_zt[q])

        def names(d, n):
            B = BUF[d]
            v = S()
            q = n % 3
            p2 = n % 2
            v.B = B
            v.ci = chunk_of(d, n)
            v.tok0 = v.ci * 128
            pb = 4 * d
            v.L0, v.L1, v.X, v.Y = pb, pb + 1, pb + 2, pb + 3
            v.x_, v.rx = B.xs[q], B.r_xs[q]
            v.b_, v.rb = B.bt[q], B.r_bt[q]
            v.f_, v.rf = B.bcf[q], B.r_bcf[q]
            v.dr, v.rdr = B.dtr[q], B.r_dtr[q]
            v.z_, v.rz = B.zt[q], B.r_zt[q]
            small = B.small[p2]
            rs_ = B.r_sm[p2]
            v.dt_, v.a_, v.d1_, v.dtw_ = small[:, 0, :], small[:, 1, :], small[:, 2, :], small[:, 6, :]
            v.E3 = small[:, 3:6, :]
            v.r_dt, v.r_a, v.r_d1, v.r_E, v.r_dtw = rs_[0], rs_[1], rs_[2], rs_[3], rs_[6]
            v.w_out_ = v.E3[:, 0, :] if d == 0 else v.E3[:, 1, :]
            v.w_st = v.E3[:, 1, :] if d == 0 else v.E3[:, 0, :]
            v.dec = v.E3[:, 2, :]
            v.stb, v.r_stb = B.stb[p2], B.r_stb[p2]
            v.dt_b, v.dtw_b = B.s16[p2][:, 0, :], B.s16[p2][:, 1, :]
            v.r_dt_b, v.r_dtw_b = B.r_s16[p2][0], B.r_s16[p2][1]
            v.ydg, v.r_ydg = B.ydg[p2], B.r_ydg[p2]
            v.x3 = v.x_.rearrange("p (k q) -> p k q", k=16)
            return v

        def local(d, n):
            v = names(d, n)
            B = v.B
            dc = slice(d * 16, d * 16 + 16)
            dt_, a_, d1_, dtw_, E3 = v.dt_, v.a_, v.d1_, v.dtw_, v.E3
            L0, L1, X, Y = v.L0, v.L1, v.X, v.Y
            f_, rf = v.f_, v.rf
            P.op("vector", lambda e: e.tensor_tensor(out=dt_, in0=v.dr[:, dc], in1=dtb[:, dc], op=ALU.add),
                 reads=[v.rdr, r_par], writes=[v.r_dt])
            P.op("scalar", lambda e: e.activation(out=dt_, in_=dt_, func=AF.Exp), reads=[v.r_dt], writes=[v.r_dt])
            P.op("scalar", lambda e: e.activation(out=dt_, in_=dt_, func=AF.Ln, bias=1.0), reads=[v.r_dt],
                 writes=[v.r_dt])
            P.op("vector", lambda e: e.tensor_tensor(out=a_, in0=dt_, in1=arow[:, dc], op=ALU.mult),
                 reads=[v.r_dt, r_par], writes=[v.r_a])
            yield
            tri = masks[:, M_LE, :] if d == 0 else masks[:, M_LT, :]
            P.op("tensor", lambda e: e.matmul(psum[X][:, 0:16], lhsT=tri, rhs=a_, start=True, stop=True),
                 reads=[v.r_a, r_const], writes=[r_ps[X]])
            P.op("tensor", lambda e: e.matmul(psum[X][:, 16:32], lhsT=masks[:, M_ONE, :], rhs=a_, start=True, stop=True),
                 reads=[v.r_a, r_const], writes=[r_ps[X]])
            for g in range(2):
                P.op("tensor", lambda e, g=g: e.matmul(psum[X][:, 64 + g * 128: 64 + (g + 1) * 128], lhsT=f_[:, g, :],
                                                       rhs=f_[:, 2 + g, :], start=True, stop=True),
                     reads=[rf], writes=[r_ps[X]])
            P.op("vector", lambda e: e.tensor_copy(out=B.cst, in_=psum[X][:, 0:32]), reads=[r_ps[X]], writes=[B.r_cst])
            m1 = masks[:, M_LE, :] if d == 0 else masks[:, M_GE, :]
            m2 = masks[:, M_GT, :] if d == 0 else masks[:, M_LT, :]
            P.op("vector", lambda e: e.tensor_tensor(out=B.smk, in0=psum[X][:, 64:320].rearrange("p (g l) -> p g l", g=2),
                                                     in1=bc(m1.unsqueeze(1), [128, 2, 128]), op=ALU.mult),
                 reads=[r_ps[X], r_const], writes=[B.r_smk])
            yield
            cst = B.cst
            P.op("vector", lambda e: e.tensor_tensor(out=d1_, in0=cst[:, 16:32], in1=cst[:, 0:16], op=ALU.subtract),
                 reads=[B.r_cst], writes=[v.r_d1])
            P.op("scalar", lambda e: e.activation(out=E3[:, 0, :], in_=cst[:, 0:16], func=AF.Exp), reads=[B.r_cst],
                 writes=[v.r_E])
            P.op("scalar", lambda e: e.activation(out=E3[:, 1, :], in_=d1_, func=AF.Exp), reads=[v.r_d1], writes=[v.r_E])
            P.op("scalar", lambda e: e.activation(out=E3[:, 2, :], in_=cst[:, 16:32], func=AF.Exp), reads=[B.r_cst],
                 writes=[v.r_E])
            P.op("vector", lambda e: e.tensor_tensor(out=v.dtw_b, in0=dt_, in1=v.w_st, op=ALU.mult), reads=[v.r_dt, v.r_E],
                 writes=[v.r_dtw_b])
            P.op("gpsimd", lambda e: e.tensor_copy(out=v.dt_b, in_=dt_), reads=[v.r_dt], writes=[v.r_dt_b])
            yield
            for k in range(16):
                P.op("scalar", lambda e, k=k: e.activation(out=B.R[:, k, :], in_=m1, func=AF.Identity,
                                                           scale=a_[:, k:k + 1]),
                     reads=[v.r_a, r_const], writes=[B.r_R])
            P.op("vector", lambda e: e.tensor_tensor(out=B.xw.rearrange("p (k q) -> p k q", k=16), in0=v.x3,
                                                     in1=bc(v.dtw_b.unsqueeze(2), [128, 16, 64]), op=ALU.mult),
                 reads=[v.rx, v.r_dtw_b], writes=[B.r_xw])
            P.op("vector", lambda e: e.tensor_tensor(out=B.xdt.rearrange("p (k q) -> p k q", k=16), in0=v.x3,
                                                     in1=bc(v.dt_b.unsqueeze(2), [128, 16, 64]), op=ALU.mult),
                 reads=[v.rx, v.r_dt_b], writes=[B.r_xdt])
            yield
            for q in range(4):
                bq = L0 + q % 2
                P.op("tensor", lambda e, q=q, bq=bq: e.matmul(psum[bq][:, :], lhsT=m2,
                                                              rhs=B.R[:, q * 4:(q + 1) * 4, :].rearrange("p k l -> p (k l)"),
                                                              start=True, stop=True), reads=[B.r_R, r_const],
                     writes=[r_ps[bq]])
                P.op("scalar", lambda e, q=q, bq=bq: e.activation(
                    out=B.Lm[:, q * 4:(q + 1) * 4, :].rearrange("p k l -> p (k l)"), in_=psum[bq][:, :], func=AF.Exp),
                    reads=[r_ps[bq]], writes=[B.r_Lm])
                if q == 1:
                    yield
            yield
            for g in range(2):
                P.op("tensor", lambda e, g=g: e.matmul(psum[X + g][:, :], lhsT=v.b_[:, g * 128:(g + 1) * 128],
                                                       rhs=B.xw[:, g * 512:(g + 1) * 512], start=True, stop=True),
                     reads=[v.rb, B.r_xw], writes=[r_ps[X + g]])
                P.op("scalar", lambda e, g=g: e.copy(out=v.stb[:, g * 512:(g + 1) * 512], in_=psum[X + g][:, :]),
                     reads=[r_ps[X + g]], writes=[v.r_stb])
            P.op("vector", lambda e: e.tensor_tensor(out=B.Mm.rearrange("p (g k) l -> p g k l", g=2),
                                                     in0=B.Lm.rearrange("p (g k) l -> p g k l", g=2),
                                                     in1=bc(B.smk.unsqueeze(2), [128, 2, 8, 128]), op=ALU.mult),
                 reads=[B.r_Lm, B.r_smk], writes=[B.r_Mm])
            yield
            for k in range(16):
                bk = L0 + k // 8
                P.op("tensor", lambda e, k=k, bk=bk: e.matmul(psum[bk][:, (k % 8) * 64:(k % 8 + 1) * 64],
                                                              lhsT=B.Mm[:, k, :], rhs=B.xdt[:, k * 64:(k + 1) * 64],
                                                              start=True, stop=True),
                     reads=[B.r_Mm, B.r_xdt], writes=[r_ps[bk]])
            for g in range(2):
                P.op("scalar", lambda e, g=g: e.copy(out=v.ydg[:, g * 512:(g + 1) * 512], in_=psum[L0 + g][:, :]),
                     reads=[r_ps[L0 + g]], writes=[v.r_ydg])
            yield

        def recur(d, n):
            v = names(d, n)
            B = v.B
            ci, tok0 = v.ci, v.tok0
            fin = n >= HALF
            L0, L1, X, Y = v.L0, v.L1, v.X, v.Y
            H, r_H, Hb, r_Hb = B.H, B.r_H, B.Hb, B.r_Hb
            ytmp, r_ytmp, yfl, r_yfl = B.ytmp, B.r_ytmp, B.yfl, B.r_yfl
            for g in range(2):
                P.op("tensor", lambda e, g=g: e.matmul(psum[X + g][:, :], lhsT=v.f_[:, 2 + g, :], rhs=Hb[:, g, :],
                                                       start=True, stop=True), reads=[v.rf, r_Hb], writes=[r_ps[X + g]])
            for g in range(2):
                P.op("gpsimd", lambda e, g=g: e.tensor_tensor(out=H[:, g, :].rearrange("p (k q) -> p k q", k=8),
                                                              in0=H[:, g, :].rearrange("p (k q) -> p k q", k=8),
                                                              in1=bc(v.dec[:, g * 8:(g + 1) * 8].unsqueeze(2), [128, 8, 64]),
                                                              op=ALU.mult), reads=[r_H, v.r_E], writes=[r_H])
            P.op("gpsimd", lambda e: e.tensor_tensor(out=H.rearrange("p g q -> p (g q)"),
                                                     in0=H.rearrange("p g q -> p (g q)"), in1=v.stb, op=ALU.add),
                 reads=[r_H, v.r_stb], writes=[r_H])
            nxt = ci + 1 if d == 0 else ci - 1
            at_edge = (nxt % 16 == 0) if d == 0 else (ci % 16 == 0)
            if at_edge:
                coupled = (d == 0 and nxt == 16) or (d == 1 and ci == 16)
                if coupled:
                    P.op("vector", lambda e: e.tensor_scalar(out=H, in0=H, scalar1=flag[:, 0:1], scalar2=None,
                                                             op0=ALU.mult), reads=[r_H, r_const], writes=[r_H])
                else:
                    P.op("vector", lambda e: e.memset(H, 0.0), writes=[r_H])
            P.op("scalar", lambda e: e.copy(out=Hb, in_=H), reads=[r_H], writes=[r_Hb])
            for g in range(2):
                P.op("vector", lambda e, g=g: e.tensor_tensor(
                    out=ytmp[:, g * 512:(g + 1) * 512].rearrange("p (k q) -> p k q", k=8),
                    in0=psum[X + g][:, :].rearrange("p (k q) -> p k q", k=8),
                    in1=bc(v.w_out_[:, g * 8:(g + 1) * 8].unsqueeze(2), [128, 8, 64]), op=ALU.mult),
                    reads=[r_ps[X + g], v.r_E], writes=[r_ytmp])
            yield
            P.op("gpsimd", lambda e: e.tensor_tensor(out=ytmp, in0=ytmp, in1=v.ydg, op=ALU.add),
                 reads=[r_ytmp, v.r_ydg], writes=[r_ytmp])
            yield
            if not fin:
                P.dma("sync", lambda e: e.dma_start(out=yf_d[tok0:tok0 + 128, :], in_=ytmp), reads=[r_ytmp],
                      writes=[r_yfd[ci]], semres=r_ytmp)
                return
            ss1, r_ss1, yn, r_yn = B.ss1, B.r_ss1, B.yn, B.r_yn
            P.dma("sync", lambda e: e.dma_start(out=yfl, in_=yf_d[tok0:tok0 + 128, :]), reads=[r_yfd[ci]],
                  writes=[r_yfl], semres=r_yfl)
            P.op("gpsimd", lambda e: e.tensor_tensor(out=ytmp, in0=ytmp, in1=yfl, op=ALU.add),
                 reads=[r_ytmp, r_yfl], writes=[r_ytmp])
            P.op("gpsimd", lambda e: e.tensor_tensor(out=yfl.rearrange("p (k q) -> p k q", k=16), in0=v.x3,
                                                     in1=bc(dsk.unsqueeze(2), [128, 16, 64]), op=ALU.mult),
                 reads=[v.rx, r_par, r_yfl], writes=[r_yfl])
            yield
            P.op("vector", lambda e: e.tensor_tensor(out=ytmp, in0=ytmp, in1=yfl, op=ALU.add),
                 reads=[r_ytmp, r_yfl], writes=[r_ytmp])
            P.op("vector", lambda e: e.tensor_tensor(out=ytmp, in0=ytmp, in1=v.z_, op=ALU.mult),
                 reads=[r_ytmp, v.rz], writes=[r_ytmp])
            P.op("scalar", lambda e: e.activation(out=yfl, in_=ytmp, func=AF.Square, accum_out=ss1),
                 reads=[r_ytmp], writes=[r_yfl, r_ss1])
            rstd_from_ss(ss1, ss1, r_ss1, r_ss1, D)
            P.op("vector", lambda e: e.scalar_tensor_tensor(out=yn, in0=ytmp, scalar=ss1[:, 0:1], in1=gsr,
                                                            op0=ALU.mult, op1=ALU.mult),
                 reads=[r_ytmp, r_ss1, r_gsr], writes=[r_yn])
            yield
            pT = psum[L0][:, :].bitcast(BF16)
            for c in range(8):
                P.op("tensor", lambda e, c=c: e.transpose(out=pT[:, c * 128:(c + 1) * 128],
                                                          in_=yn[:, c * 128:(c + 1) * 128], identity=ident[:]),
                     reads=[r_yn, r_const], writes=[r_ps[L0]])
            yo, ryo = B.yfm[n % 2], B.r_yfm[n % 2]
            P.op("scalar", lambda e: e.copy(out=yo, in_=pT.rearrange("p (c t) -> p c t", c=8)), reads=[r_ps[L0]],
                 writes=[ryo])
            P.dma("sync", lambda e: e.dma_start(out=fm_tile(yfm_d, 0, 8, tok0, 128), in_=yo), reads=[ryo], semres=ryo)
            yield

        for d in range(2):
            B = BUF[d]
            P.op("vector", lambda e, B=B: e.memset(B.H, 0.0), writes=[B.r_H])
            P.op("scalar", lambda e, B=B: e.copy(out=B.Hb, in_=B.H), reads=[B.r_H], writes=[B.r_Hb])
            loads(d, 0)
        for m in range(NSUB + 1):
            for d in range(2):
                if m + 1 < NSUB:
                    loads(d, m + 1)
            gens = []
            for d in range(2):
                if m < NSUB:
                    gens.append(local(d, m))
            for d in range(2):
                if m >= 1:
                    gens.append(recur(d, m - 1))
            while gens:
                alive = []
                for g_ in gens:
                    try:
                        next(g_)
                        alive.append(g_)
                    except StopIteration:
                        pass
                gens = alive

    def phase_fnet(l):
        new_phase()
        csc = AB.take(256)
        r_csc = P.res("csc")
        P.dma("sync", lambda e: e.dma_start(out=csc, in_=csc_d), writes=[r_csc], semres=r_csc)
        uv = AB.take(NSUB, D)
        r_uv = P.res("uv")
        ut = [AB.take(4, 512) for _ in range(2)]
        r_ut = [P.res("ut0"), P.res("ut1")]
        tab = [AB.take(16, 512) for _ in range(2)]
        r_tab = [P.res("tab0"), P.res("tab1")]
        ft = [AB.take(4, 512) for _ in range(2)]
        r_ft = [P.res("ft0"), P.res("ft1")]
        def ld_u(tt):
            P.dma("sync", lambda e: e.dma_start(out=ut[tt % 2], in_=fm_tile(u_d, 0, 4, tt * 512, 512)),
                  writes=[r_ut[tt % 2]], semres=r_ut[tt % 2])

        ld_u(0)
        for t in range(NT):
            tok0 = t * 512
            u, ru = ut[t % 2], r_ut[t % 2]
            if t + 1 < NT:
                ld_u(t + 1)
            for j in range(4):
                for half in range(2):
                    b = (j * 2 + half) % 8
                    for gg in range(2):
                        g = half * 2 + gg
                        P.op("tensor", lambda e, g=g, gg=gg, j=j, b=b, u=u: e.matmul(
                            psum[b][:, gg * 256:(gg + 1) * 256], lhsT=u[:, g, j * 128:(j + 1) * 128], rhs=csc,
                            start=True, stop=True), reads=[ru, r_csc], writes=[r_ps[b]])
                    eng = "vector" if half == 0 else "scalar"
                    dst = uv[:, t * 4 + j, half * 512:(half + 1) * 512]
                    if eng == "vector":
                        P.op("vector", lambda e, b=b, dst=dst: e.tensor_copy(out=dst, in_=psum[b][:, :]),
                             reads=[r_ps[b]], writes=[r_uv])
                    else:
                        P.op("scalar", lambda e, b=b, dst=dst: e.copy(out=dst, in_=psum[b][:, :]),
                             reads=[r_ps[b]], writes=[r_uv])
        blocks = {0: [(0, 0), (1, 1)], 1: [(2, 0), (3, 1)], 2: [(4, 2)]}
        allsteps = []
        for oseg in range(3):
            for kt in range(4):
                steps = [(bi, iseg, cs) for (bi, iseg) in blocks[oseg] for cs in range(2)]
                for si, (bi, iseg, cs) in enumerate(steps):
                    allsteps.append((oseg, kt, bi, iseg, cs, si, len(steps)))

        def ld_tab(n):
            oseg, kt, bi, iseg, cs, si, ns = allsteps[n]
            P.dma("sync", lambda e: e.dma_start(out=tab[n % 2], in_=tab_d[bi, kt, cs]), writes=[r_tab[n % 2]],
                  semres=r_tab[n % 2])

        ld_tab(0)
        for n, (oseg, kt, bi, iseg, cs, si, ns) in enumerate(allsteps):
            if n + 1 < len(allsteps):
                ld_tab(n + 1)
            tb, rt = tab[n % 2], r_tab[n % 2]
            tot = ns * 16
            for ncn in range(16):
                cnt = si * 16 + ncn
                for g in range(4):
                    P.op("tensor", lambda e, g=g, ncn=ncn, cnt=cnt: e.matmul(
                        psum[g + 4 * ((oseg * 4 + kt) % 2)][:, :],
                        lhsT=uv[:, iseg * 16 + ncn, g * 256 + cs * 128: g * 256 + (cs + 1) * 128],
                        rhs=tb[:, ncn, :], start=(cnt == 0), stop=(cnt == tot - 1)),
                        reads=[r_uv, rt], writes=[r_ps[g + 4 * ((oseg * 4 + kt) % 2)]])
            if si == ns - 1:
                pb0 = 4 * ((oseg * 4 + kt) % 2)
                fo, rfo = ft[(oseg * 4 + kt) % 2], r_ft[(oseg * 4 + kt) % 2]
                for g in range(4):
                    if g % 2 == 0:
                        P.op("vector", lambda e, g=g: e.tensor_copy(out=fo[:, g, :], in_=psum[pb0 + g][:, :]),
                             reads=[r_ps[pb0 + g]], writes=[rfo])
                    else:
                        P.op("scalar", lambda e, g=g: e.copy(out=fo[:, g, :], in_=psum[pb0 + g][:, :]),
                             reads=[r_ps[pb0 + g]], writes=[rfo])
                tok0 = oseg * SEG + kt * 512
                P.dma("sync", lambda e: e.dma_start(out=fm_tile(f_d, 0, 4, tok0, 512), in_=fo), reads=[rfo], semres=rfo)

    def phase_merge(l, xsrc):
        new_phase()
        wa = AB.take(4, D)
        wb = AB.take(8, D)
        wc = AB.take(4, D)
        wo = AB.take(8, D)
        r_w1 = [P.res("wE0"), P.res("wE1"), P.res("wE2"), P.res("wE3")]
        r_wo = [P.res("wo0"), P.res("wo1")]
        for hq in range(4):
            cq = slice(hq * 256, (hq + 1) * 256)
            load_w(wa[:, :, cq], w_a_out[l][:, cq].rearrange("(c p) n -> p c n", p=128), r_w1[hq])
            load_w(wb[:, :, cq], w_b_out[l][:, cq].rearrange("(c p) n -> p c n", p=128), r_w1[hq])
            load_w(wc[:, :, cq], w_c_out[l][:, cq].rearrange("(c p) n -> p c n", p=128), r_w1[hq])
        for hq in range(2):
            cq = slice(hq * 512, (hq + 1) * 512)
            load_w(wo[:, :, cq], w_out[l][:, cq].rearrange("(c p) n -> p c n", p=128), r_wo[hq])
        ia = [AB.take(4, 512) for _ in range(2)]
        iy = [AB.take(8, 512) for _ in range(2)]
        if_ = [AB.take(4, 512) for _ in range(2)]
        ig = [AB.take(24, 512) for _ in range(2)]
        r_in = [P.res("Ein0"), P.res("Ein1")]
        mb = [AB.take(8, 512) for _ in range(2)]
        r_mb = [P.res("mb0"), P.res("mb1")]
        mf = [AFa.take(512) for _ in range(2)]
        r_mf = [P.res("mf0"), P.res("mf1")]
        tp = [AFa.take(512) for _ in range(2)]
        r_tp = [P.res("tp0"), P.res("tp1")]
        tq = [AFa.take(512) for _ in range(2)]
        r_tq = [P.res("tq0"), P.res("tq1")]
        xt = [AFa.take(D) for _ in range(2)]
        r_xt = [P.res("ext0"), P.res("ext1")]
        gg = [AFa.take(D) for _ in range(2)]
        r_gg = [P.res("gg0"), P.res("gg1")]
        gtmp = AFa.take(D)
        r_gtmp = P.res("egtmp")
        tmp = [AFa.take(D) for _ in range(2)]
        r_tmp = [P.res("etmp0"), P.res("etmp1")]
        xo = [AFa.take(D) for _ in range(2)]
        r_xo = [P.res("exo0"), P.res("exo1")]
        ss = [AFa.take(2) for _ in range(2)]
        r_ss = [P.res("ess0"), P.res("ess1")]
        kk = 0
        for t in range(NT):
            tok0 = t * 512
            slot = t // 4
            if t % 4 == 0:
                load_rows(l, slot, None, [(gg[slot % 2], r_gg[slot % 2], "gate", "g_post_mix", 2, gtmp, r_gtmp)])
            ia_, iy_, if__, ig_, rin = ia[t % 2], iy[t % 2], if_[t % 2], ig[t % 2], r_in[t % 2]
            mb_, rmb = mb[t % 2], r_mb[t % 2]

            def ld_in(tt):
                for dst, src, C in ((ia[tt % 2], acv_d, 4), (iy[tt % 2], yfm_d, 8), (if_[tt % 2], f_d, 4),
                                    (ig[tt % 2], gates_d, 24)):
                    P.dma("sync", lambda e, dst=dst, src=src, C=C: e.dma_start(out=dst,
                                                                               in_=fm_tile(src, 0, C, tt * 512, 512)),
                          writes=[r_in[tt % 2]], semres=r_in[tt % 2])
            if t == 0:
                ld_in(0)
            if t + 1 < NT:
                ld_in(t + 1)
            for i in range(8):
                cs_ = slice(i * 128, (i + 1) * 128)
                b0 = 3 * (kk % 2)
                mf_, rmf = mf[kk % 2], r_mf[kk % 2]
                tp_, rtp = tp[kk % 2], r_tp[kk % 2]
                kk += 1
                for c in range(4):
                    P.op("tensor", lambda e, c=c, cs_=cs_, b0=b0: e.matmul(psum[b0][:, :], lhsT=wa[:, c, cs_],
                                                                            rhs=ia_[:, c, :], start=(c == 0), stop=(c == 3)),
                         reads=[r_w1[i // 2], rin], writes=[r_ps[b0]])
                for c in range(8):
                    P.op("tensor", lambda e, c=c, cs_=cs_, b0=b0: e.matmul(psum[b0 + 1][:, :], lhsT=wb[:, c, cs_],
                                                                            rhs=iy_[:, c, :], start=(c == 0), stop=(c == 7)),
                         reads=[r_w1[i // 2], rin], writes=[r_ps[b0 + 1]])
                for c in range(4):
                    P.op("tensor", lambda e, c=c, cs_=cs_, b0=b0: e.matmul(psum[b0 + 2][:, :], lhsT=wc[:, c, cs_],
                                                                            rhs=if__[:, c, :], start=(c == 0), stop=(c == 3)),
                         reads=[r_w1[i // 2], rin], writes=[r_ps[b0 + 2]])
                P.op("vector", lambda e, i=i, b0=b0: e.tensor_tensor(out=mf_, in0=psum[b0][:, :], in1=ig_[:, i, :],
                                                                     op=ALU.mult),
                     reads=[r_ps[b0], rin], writes=[rmf])
                P.op("vector", lambda e, i=i, b0=b0: e.tensor_tensor(out=tp_, in0=psum[b0 + 1][:, :], in1=ig_[:, 8 + i, :],
                                                                     op=ALU.mult),
                     reads=[r_ps[b0 + 1], rin], writes=[rtp])
                tq_, rtq = tq[(kk - 1) % 2], r_tq[(kk - 1) % 2]
                P.op("vector", lambda e, i=i, b0=b0: e.tensor_tensor(out=tq_, in0=psum[b0 + 2][:, :], in1=ig_[:, 16 + i, :],
                                                                     op=ALU.mult),
                     reads=[r_ps[b0 + 2], rin], writes=[rtq])
                P.op("gpsimd", lambda e: e.tensor_tensor(out=mf_, in0=mf_, in1=tp_, op=ALU.add), reads=[rmf, rtp],
                     writes=[rmf])
                P.op("gpsimd", lambda e, i=i: e.tensor_tensor(out=mb_[:, i, :], in0=mf_, in1=tq_, op=ALU.add),
                     reads=[rmf, rtq], writes=[rmb])
                if t >= 1 and i % 2 == 1:
                    tp_t = t - 1
                    out_proj_residual(tp_t, mb[tp_t % 2], r_mb[tp_t % 2], wo, r_wo, 8, xsrc, xt, r_xt,
                                      gg[(tp_t // 4) % 2], r_gg[(tp_t // 4) % 2], tmp, r_tmp, xo, r_xo, ss, r_ss,
                                      ((6, 7),), js=(i // 2,))
        tp_t = NT - 1
        out_proj_residual(tp_t, mb[tp_t % 2], r_mb[tp_t % 2], wo, r_wo, 8, xsrc, xt, r_xt, gg[(tp_t // 4) % 2],
                          r_gg[(tp_t // 4) % 2], tmp, r_tmp, xo, r_xo, ss, r_ss, ((6, 7), (4, 5)))

    def out_proj_residual(t, act, r_act, w, r_wh, nk, xsrc, xt, r_xt, gg, r_gg, tmps, r_tmps, xo, r_xo, sss, r_sss, banks,
                          js=(0, 1, 2, 3)):
        tok0 = t * 512
        for j in js:
            x_, rx = xt[j % 2], r_xt[j % 2]
            o_, ro = xo[j % 2], r_xo[j % 2]
            tmp, r_tmp = tmps[j % 2], r_tmps[j % 2]
            ss, r_ss = sss[j % 2], r_sss[j % 2]
            r0 = tok0 + j * 128
            P.dma("sync", lambda e, x_=x_, r0=r0: e.dma_start(out=x_, in_=xsrc[r0:r0 + 128, :]), writes=[rx], semres=rx)
            bk = banks[j % len(banks)]
            for half in range(2):
                b = bk[half]
                for c in range(nk):
                    P.op("tensor", lambda e, c=c, b=b, j=j, half=half: e.matmul(
                        psum[b][:, :], lhsT=act[:, c, j * 128:(j + 1) * 128], rhs=w[:, c, half * 512:(half + 1) * 512],
                        start=(c == 0), stop=(c == nk - 1)), reads=[r_act, r_wh[half]], writes=[r_ps[b]])
            P.op("scalar", lambda e, o_=o_, bk=bk, ss=ss: e.activation(out=o_[:, 0:512], in_=psum[bk[0]][:, :],
                                                                       func=AF.Square, accum_out=ss[:, 0:1]),
                 reads=[r_ps[bk[0]]], writes=[ro, r_ss])
            P.op("scalar", lambda e, o_=o_, bk=bk, ss=ss: e.activation(out=o_[:, 512:1024], in_=psum[bk[1]][:, :],
                                                                       func=AF.Square, accum_out=ss[:, 1:2]),
                 reads=[r_ps[bk[1]]], writes=[ro, r_ss])
            P.op("vector", lambda e, ss=ss: e.tensor_tensor(out=ss[:, 0:1], in0=ss[:, 0:1], in1=ss[:, 1:2], op=ALU.add),
                 reads=[r_ss], writes=[r_ss])
            rstd_from_ss(ss[:, 0:1], ss[:, 0:1], r_ss, r_ss, D)
            for half in range(2):
                hs = slice(half * 512, (half + 1) * 512)
                P.op("vector", lambda e, half=half, hs=hs, bk=bk, ss=ss, tmp=tmp: e.scalar_tensor_tensor(
                    out=tmp[:, hs], in0=psum[bk[half]][:, :], scalar=ss[:, 0:1], in1=gg[:, hs], op0=ALU.mult,
                    op1=ALU.mult), reads=[r_ps[bk[half]], r_ss, r_gg], writes=[r_tmp])
            P.op("gpsimd", lambda e, o_=o_, x_=x_, tmp=tmp: e.tensor_tensor(out=o_, in0=tmp, in1=x_, op=ALU.add),
                 reads=[r_tmp, rx], writes=[ro])
            P.dma("sync", lambda e, o_=o_, r0=r0: e.dma_start(out=yout[r0:r0 + 128, :], in_=o_), reads=[ro], semres=ro)

    def phase_ffn1(l):
        new_phase()
        NCOL = 2 * D_FF
        wF = AB.take(8, NCOL)
        r_wFp = [P.res(f"wF{i}") for i in range(6)]
        for pi in (0, 2, 3, 1, 4, 5):
            c0, c1 = pi * 1024, min(NCOL, (pi + 1) * 1024)
            load_w(wF[:, :, c0:c1], w_ffn_in[l, :, c0:c1].rearrange("(c p) n -> p c n", p=128), r_wFp[pi])
        hfm = [AB.take(8, 512) for _ in range(2)]
        r_hfm = [P.res("fh0"), P.res("fh1")]
        xh = [AB.take(D) for _ in range(2)]
        r_xh = [P.res("fxh0"), P.res("fxh1")]
        sgt = [AB.take(512) for _ in range(2)]
        r_sgt = [P.res("sgt0"), P.res("sgt1")]
        oc = [AB.take(512) for _ in range(6)]
        r_oc = [P.res(f"foc{i}") for i in range(6)]
        xt = [AFa.take(D) for _ in range(8)]
        r_xt = [P.res(f"fxt{i}") for i in range(8)]
        gs = AFa.take(D)
        r_gs = P.res("fgs")
        sh = AFa.take(D)
        r_sh = P.res("fsh")
        tmp = AFa.take(D)
        r_tmp = P.res("ftmp")
        gtmp = AFa.take(D)
        r_gtmp = P.res("fgtmp")
        ss = AFa.take(4)
        r_ss = P.res("fss")
        rstd = AFa.take(4)
        r_rstd = P.res("frstd")
        r_junk = P.res("fjunk")
        k = 0

        def prep(tt):
            if tt % 4 == 0:
                load_rows(l, tt // 4, None, [(gs, r_gs, "scale", "g_pre_ffn", 4, gtmp, r_gtmp),
                                             (sh, r_sh, "shift", None, 3, None, None)])
            yield from norm_transpose_tile(yout, tt, xt, r_xt, tmp, r_tmp, ss, r_ss, rstd, r_rstd, gs, r_gs, sh, r_sh, tmp,
                                           r_tmp, xh, r_xh, hfm[tt % 2], r_hfm[tt % 2], 0)

        load_x_tile(yout, 0, xt, r_xt)
        load_x_tile(yout, 1, xt, r_xt)
        for _ in prep(0):
            pass
        for t in range(NT):
            slot = t // 4
            tok0 = t * 512
            if t + 2 < NT:
                load_x_tile(yout, t + 2, xt, r_xt)
            h, rh = hfm[t % 2], r_hfm[t % 2]
            gen = prep(t + 1) if t + 1 < NT else iter(())
            for i in range(22):
                if i in (2, 5, 8, 11, 14, 17, 20):
                    next(gen, None)
                b1 = 1 + (2 * k) % 6
                b2 = 1 + (2 * k + 1) % 6
                sg_, rsg = sgt[k % 2], r_sgt[k % 2]
                o, ro = oc[k % 6], r_oc[k % 6]
                k += 1
                for c in range(8):
                    P.op("tensor", lambda e, c=c, b1=b1, i=i, h=h: e.matmul(psum[b1][:, :], lhsT=wF[:, c, i * 128:(i + 1) * 128],
                                                                             rhs=h[:, c, :], start=(c == 0), stop=(c == 7)),
                         reads=[r_wFp[(i * 128) // 1024], rh], writes=[r_ps[b1]])
                for c in range(8):
                    P.op("tensor", lambda e, c=c, b2=b2, i=i, h=h: e.matmul(
                        psum[b2][:, :], lhsT=wF[:, c, D_FF + i * 128: D_FF + (i + 1) * 128], rhs=h[:, c, :],
                        start=(c == 0), stop=(c == 7)), reads=[r_wFp[(D_FF + i * 128) // 1024], rh], writes=[r_ps[b2]])
                P.op("scalar", lambda e, b1=b1, sg_=sg_: e.activation(out=sg_, in_=psum[b1][:, :], func=AF.Silu),
                     reads=[r_ps[b1]], writes=[rsg])
                P.op("vector", lambda e, b2=b2, sg_=sg_, o=o: e.tensor_tensor(out=o, in0=psum[b2][:, :], in1=sg_,
                                                                              op=ALU.mult),
                     reads=[r_ps[b2], rsg], writes=[ro])
                P.dma("sync", lambda e, o=o, i=i, tok0=tok0: e.dma_start(out=act_d[i, :, tok0:tok0 + 512], in_=o),
                      reads=[ro], semres=ro)
            for _ in gen:
                pass

    def phase_ffn2(l):
        new_phase()
        wD = AB.take(22, D)
        r_wD = [P.res("wD0"), P.res("wD1")]
        for hq in range(2):
            cq = slice(hq * 512, (hq + 1) * 512)
            for c0 in range(0, 22, 6):
                c1 = min(22, c0 + 6)
                load_w(wD[:, c0:c1, cq], w_ffn_out[l, c0 * 128:c1 * 128, cq].rearrange("(c p) n -> p c n", p=128),
                       r_wD[hq])
        act = [AB.take(22, 512) for _ in range(2)]
        r_act = [P.res("act0"), P.res("act1")]
        xt = [AFa.take(D) for _ in range(2)]
        r_xt = [P.res("gxt0"), P.res("gxt1")]
        gg = AFa.take(D)
        r_gg = P.res("ggg")
        gtmp = AFa.take(D)
        r_gtmp = P.res("ggtmp")
        tmp = [AFa.take(D) for _ in range(2)]
        r_tmp = [P.res("gtmp2a"), P.res("gtmp2b")]
        xo = [AFa.take(D) for _ in range(2)]
        r_xo = [P.res("gxo0"), P.res("gxo1")]
        ss = [AFa.take(2) for _ in range(2)]
        r_ss = [P.res("gss0"), P.res("gss1")]
        for t in range(NT):
            tok0 = t * 512
            slot = t // 4
            if t % 4 == 0:
                load_rows(l, slot, None, [(gg, r_gg, "gate", "g_post_ffn", 5, gtmp, r_gtmp)])
            a, ra = act[t % 2], r_act[t % 2]

            def ld_act(tt):
                P.dma("sync", lambda e: e.dma_start(out=act[tt % 2], in_=fm_tile(act_d, 0, 22, tt * 512, 512)),
                      writes=[r_act[tt % 2]], semres=r_act[tt % 2])
            if t == 0:
                ld_act(0)
            if t + 1 < NT:
                ld_act(t + 1)
            out_proj_residual(t, a, ra, wD, r_wD, 22, yout, xt, r_xt, gg, r_gg, tmp, r_tmp, xo, r_xo, ss, r_ss,
                              ((0, 1), (2, 3), (4, 5), (6, 7)))

    phases = []
    phase_mod()
    done = (stop_after == "mod")
    for l in range(nl):
        if done:
            break
        xsrc = xin if l == 0 else yout
        for name, fn in (("a", lambda: phase_a(l, xsrc)), ("gates", lambda: phase_gates(l)),
                         ("conv_a", lambda: None), ("conv_s", lambda: phase_convs(l)),
                         ("ssd", lambda: phase_ssd(l)), ("fnet", lambda: phase_fnet(l)),
                         ("merge", lambda: phase_merge(l, xsrc)), ("ffn1", lambda: phase_ffn1(l)),
                         ("ffn2", lambda: phase_ffn2(l))):
            fn()
            if stop_after == name:
                done = True
                break
    P.finalize()
    return nc, P


def _dft_tables(coupled):
    bf = ml_dtypes.bfloat16
    tab = np.zeros((5, 4, 2, 128, 16, 512), dtype=bf)
    n = np.arange(SEG, dtype=np.float64)
    k = np.arange(SEG, dtype=np.float64)

    def fill(bi, ang, scale):
        for cs, m in ((0, np.cos(ang) * scale), (1, -np.sin(ang) * scale)):
            m = m.reshape(16, 128, 4, 512).transpose(2, 1, 0, 3)
            tab[bi, :, cs] = m.astype(bf)

    if coupled:
        L = 2 * SEG
        sc = 1.0 / np.sqrt(L * 128.0)
        for bi, (os_, is_) in enumerate(((0, 0), (0, 1), (1, 0), (1, 1))):
            prod = np.outer(n + is_ * SEG, k + os_ * SEG) % L
            fill(bi, 2 * np.pi * prod / L, sc)
    else:
        L = SEG
        sc = 1.0 / np.sqrt(L * 128.0)
        ang = 2 * np.pi * (np.outer(n, k) % L) / L
        fill(0, ang, sc)
        tab[3] = tab[0]
    sc = 1.0 / np.sqrt(SEG * 128.0)
    ang = 2 * np.pi * (np.outer(n, k) % SEG) / SEG
    fill(4, ang, sc)
    return tab


def _host_inputs(inp):
    f32 = np.float32
    bf = ml_dtypes.bfloat16
    shared = {}
    for k in ("w_ada", "b_ada", "g_pre_mix", "g_post_mix", "g_pre_ffn", "g_post_ffn", "g_ssd", "w_in", "w_a_out",
              "w_b_out", "w_c_out", "w_out", "w_ffn_in", "w_ffn_out"):
        shared[k] = np.ascontiguousarray(np.asarray(inp[k], dtype=f32))
    shared["caw"] = np.ascontiguousarray(np.asarray(inp["conv_a_w"], f32).reshape(DEPTH, 31, 4, 128).transpose(0, 3, 2, 1))
    for nm, src in (("cab", "conv_a_b"), ("lng", "ln_a_g"), ("lnb", "ln_a_b")):
        shared[nm] = np.ascontiguousarray(np.asarray(inp[src], f32).reshape(DEPTH, 4, 128).transpose(0, 2, 1))
    shared["csw"] = np.ascontiguousarray(np.asarray(inp["conv_s_w"], f32).reshape(DEPTH, 5, 12, 128).transpose(0, 3, 2, 1))
    shared["csb"] = np.ascontiguousarray(np.asarray(inp["conv_s_b"], f32).reshape(DEPTH, 12, 128).transpose(0, 2, 1))
    shared["dtb"] = np.ascontiguousarray(np.concatenate([np.asarray(inp["dt_bias_f"], f32),
                                                         np.asarray(inp["dt_bias_b"], f32)], axis=1))
    shared["alog"] = np.ascontiguousarray(np.concatenate([np.asarray(inp["a_log_f"], f32),
                                                          np.asarray(inp["a_log_b"], f32)], axis=1))
    shared["dsk"] = np.ascontiguousarray(np.asarray(inp["d_skip"], f32))
    shared["ident"] = np.eye(128, dtype=f32).astype(bf)
    j = np.arange(128)[:, None]
    s = np.arange(128)[None, :]
    masks = np.stack([(j > s), (j < s), (j <= s), (j >= s), np.ones((128, 128), bool)], axis=1).astype(f32)
    shared["masks"] = np.ascontiguousarray(masks)
    c = np.arange(128, dtype=np.float64)
    ang = 2 * np.pi * np.outer(c, c) / 128.0
    shared["csc"] = np.concatenate([np.cos(ang), np.sin(ang)], axis=1).astype(bf)
    tabs = {True: _dft_tables(True), False: _dft_tables(False)}
    xp = np.asarray(inp["x_prompt"], f32)
    xs = np.asarray(inp["x_sample"], f32)
    cp = np.asarray(inp["c_prompt"], f32)
    csm = np.asarray(inp["c_sample"], f32)
    maps = []
    for core in range(8):
        if core < 4:
            xin = np.concatenate([xp[core], xs[core]], axis=0)
            cc = np.stack([cp[core], cp[core], csm[core]], axis=0)
            coupled = True
        else:
            ids = [4 + 3 * (core - 4) + q for q in range(3)]
            xin = np.concatenate([xs[q] for q in ids], axis=0)
            cc = np.stack([csm[q] for q in ids], axis=0)
            coupled = False
        m = dict(shared)
        m["xin"] = np.ascontiguousarray(xin)
        m["cT"] = np.ascontiguousarray(cc.reshape(3, 8, 128).transpose(2, 1, 0))
        m["flag"] = np.full((128, 1), 1.0 if coupled else 0.0, f32)
        m["tab"] = tabs[coupled]
        maps.append(m)
    return maps


_CACHE = {}


def kernel(**inputs):
    maps = _host_inputs(inputs)
    if "nc" not in _CACHE:
        _CACHE["nc"] = build_program()[0]
    nc = _CACHE["nc"]
    res = run_bass_kernel_spmd(nc, maps, core_ids=list(range(8)))
    outs = [np.asarray(r["yout"], dtype=np.float32) for r in res.results]
    y_prompt = np.stack([outs[c][:2 * SEG] for c in range(4)], axis=0)
    ys = [None] * 16
    for c in range(4):
        ys[c] = outs[c][2 * SEG:]
    for c in range(4, 8):
        for q in range(3):
            ys[4 + 3 * (c - 4) + q] = outs[c][q * SEG:(q + 1) * SEG]
    y_sample = np.stack(ys, axis=0)
    return (y_prompt, y_sample)
```

```python
import contextlib
import numpy as np
import ml_dtypes
import concourse.bass as bass
import concourse.mybir as mybir
from concourse.bass_utils import run_bass_kernel_spmd

F32 = mybir.dt.float32
BF16 = mybir.dt.bfloat16
AF = mybir.ActivationFunctionType
ALU = mybir.AluOpType

ENGINES = ("tensor", "scalar", "vector", "gpsimd", "sync")
COMPUTE = ("tensor", "scalar", "vector", "gpsimd")

D = 1024
T = 6144
SEG = 2048
NT = 12
NSUB = 48
DEPTH = 4
D_FF = 2816
IN_COLS = 7200
EPS = 1e-6
O_AV, O_AG, O_Z, O_XBC, O_DT, O_UC, O_GATE = 0, 512, 1024, 2048, 3584, 3616, 4128


class Res:
    __slots__ = ("name", "last_writer", "readers", "sem", "issued")

    def __init__(self, name):
        self.name = name
        self.last_writer = None
        self.readers = []
        self.sem = None
        self.issued = 0


class Op:
    __slots__ = ("eng", "fn", "reads", "writes", "dma", "semres", "deps", "signal", "token", "waits", "bar")

    def __init__(self, eng, fn, reads, writes, dma, semres, bar=False):
        self.eng = eng
        self.fn = fn
        self.reads = reads
        self.writes = writes
        self.dma = dma
        self.semres = semres
        self.deps = ()
        self.signal = False
        self.token = None
        self.waits = ()
        self.bar = bar


class _Rec:
    __slots__ = ("call",)

    def __init__(self):
        self.call = None

    def __getattr__(self, name):
        def f(*a, **k):
            self.call = (name, a, k)
            return None
        return f


class Prog:
    def __init__(self, nc):
        self.nc = nc
        self.ops = []
        self.stack = contextlib.ExitStack()
        self.all_res = []

    def sb(self, name, shape, dtype):
        return self.stack.enter_context(self.nc.sbuf_tensor(name, list(shape), dtype))

    def ps(self, name, shape, dtype=F32):
        return self.stack.enter_context(self.nc.psum_tensor(name, list(shape), dtype))

    def res(self, name=None):
        r = Res(name or f"r{len(self.all_res)}")
        self.all_res.append(r)
        return r

    def op(self, eng, fn, reads=(), writes=()):
        rec = _Rec()
        fn(rec)
        assert rec.call is not None
        self.ops.append(Op(eng, rec.call, tuple(reads), tuple(writes), False, None))

    def dma(self, eng, fn, reads=(), writes=(), semres=None):
        assert semres is not None
        rec = _Rec()
        fn(rec)
        assert rec.call is not None
        self.ops.append(Op(eng, rec.call, tuple(reads), tuple(writes), True, semres))

    def barrier(self):
        for e in ENGINES:
            self.ops.append(Op(e, None, (), (), False, None, bar=True))

    def finalize(self):
        nc = self.nc
        ops = self.ops
        last_on = {e: None for e in COMPUTE}
        i = 0
        n = len(ops)
        while i < n:
            o = ops[i]
            if o.bar:
                for e in COMPUTE:
                    if last_on[e] is not None:
                        ops[last_on[e]].signal = True
                for r in self.all_res:
                    r.last_writer = None
                    r.readers = []
                while i < n and ops[i].bar:
                    i += 1
                continue
            deps = set()
            for r in o.reads:
                if r.last_writer is not None:
                    deps.add(r.last_writer)
            for w in o.writes:
                if w.last_writer is not None:
                    deps.add(w.last_writer)
                for rd in w.readers:
                    deps.add(rd)
            deps.discard(i)
            dl = []
            for j in deps:
                oj = ops[j]
                if oj.eng == o.eng and not oj.dma and not o.dma:
                    if o.eng == "tensor":
                        continue
                    if not any((r.last_writer == j) for r in o.reads):
                        continue
                dl.append(j)
            o.deps = dl
            for j in dl:
                ops[j].signal = True
            for r in o.reads:
                r.readers.append(i)
            for w in o.writes:
                w.last_writer = i
                w.readers = []
            if not o.dma and o.eng in last_on:
                last_on[o.eng] = i
            i += 1
        for e in COMPUTE:
            if last_on[e] is not None:
                ops[last_on[e]].signal = True
        sems = {e: None for e in COMPUTE}
        counts = {e: 0 for e in COMPUTE}
        active = []
        free_sems = []
        sem_final = {}
        bar_snap = {}
        for idx, o in enumerate(ops):
            if o.bar:
                if idx not in bar_snap:
                    snap = [("e", e, sems[e], counts[e]) for e in COMPUTE if counts[e] > 0]
                    snap += [("d", id(r.sem), r.sem, r.issued) for r in active]
                    for r in active:
                        free_sems.append((r.sem, r.issued))
                        r.sem = None
                    active = []
                    j = idx
                    while j < len(ops) and ops[j].bar:
                        bar_snap[j] = snap
                        j += 1
                continue
            if o.dma:
                r = o.semres
                if r.sem is None:
                    if free_sems:
                        r.sem, r.issued = free_sems.pop()
                    else:
                        r.sem = nc.alloc_semaphore(name=f"d_{r.name}")
                        r.issued = 0
                    active.append(r)
                r.issued += 16
                o.token = ("d", r.sem, r.issued)
                sem_final[id(r.sem)] = (r.sem, r.issued)
            elif o.signal:
                if sems[o.eng] is None:
                    sems[o.eng] = nc.alloc_semaphore(name=f"e_{o.eng}")
                counts[o.eng] += 1
                o.token = ("e", o.eng, counts[o.eng])
        waited = {e: {} for e in ENGINES}
        issued_sofar = {}
        for idx, o in enumerate(ops):
            need = {}
            if o.bar:
                for kind, key, semh, val in bar_snap[idx]:
                    need[(kind, key)] = (semh if kind == "d" else sems[key], val)
            else:
                for j in o.deps:
                    kind, key, val = ops[j].token
                    if kind == "d":
                        val = max(val, issued_sofar.get(id(key), 0))
                        k = ("d", id(key))
                        semh = key
                    else:
                        k = ("e", key)
                        semh = sems[key]
                    if need.get(k, (None, 0))[1] < val:
                        need[k] = (semh, val)
            w = []
            wd = waited[o.eng]
            for k, (semh, val) in need.items():
                if wd.get(k, 0) >= val:
                    continue
                wd[k] = val
                w.append((semh, val))
            o.waits = w
            if o.dma:
                issued_sofar[id(o.token[1])] = o.token[2]
        per_eng = {e: [o for o in ops if o.eng == e] for e in ENGINES}
        final = list(sem_final.values())
        final += [(sems[e], counts[e]) for e in COMPUTE if counts[e] > 0]
        self.n_ops = len(ops)
        with nc.Block() as block:
            def make(ename):
                def body(eng):
                    for o in per_eng[ename]:
                        for semh, val in o.waits:
                            eng.wait_ge(semh, val)
                        if o.fn is None:
                            continue
                        name, a, k = o.fn
                        ins = getattr(eng, name)(*a, **k)
                        if o.dma:
                            ins.then_inc(o.token[1], 16)
                        elif o.signal:
                            ins.then_inc(sems[o.eng], 1)
                    if ename == "sync":
                        for semh, val in final:
                            eng.wait_ge(semh, val)
                return body
            block.tensor(make("tensor"))
            block.scalar(make("scalar"))
            block.vector(make("vector"))
            block.gpsimd(make("gpsimd"))
            block.sync(make("sync"))
        self.stack.close()


class Arena:
    def __init__(self, ap2d, n):
        self.ap = ap2d
        self.n = n
        self.off = 0

    def reset(self):
        self.off = 0

    def take(self, *shape):
        size = int(np.prod(shape))
        assert self.off + size <= self.n, (self.off, size, self.n)
        v = self.ap[:, self.off:self.off + size]
        self.off += size
        if len(shape) == 2:
            v = v.rearrange("p (a b) -> p a b", a=shape[0])
        elif len(shape) == 3:
            v = v.rearrange("p (a b c) -> p a b c", a=shape[0], b=shape[1])
        return v


def bc(ap, shape):
    return ap.to_broadcast(list(shape))


def build_program(nl=DEPTH, debug=False, stop_after=None):
    nc = bass.Bass("TRN2", target_bir_lowering=False)

    def din(name, shape, dt=F32):
        return nc.dram_tensor(name, list(shape), dt, kind="ExternalInput").ap()

    skind = "ExternalOutput" if debug else "Internal"

    def dscr(name, shape, dt):
        return nc.dram_tensor(name, list(shape), dt, kind=skind).ap()

    xin = din("xin", [T, D])
    cT_d = din("cT", [128, 8, 3])
    flag_d = din("flag", [128, 1])
    w_ada = din("w_ada", [DEPTH, D, 6 * D])
    b_ada = din("b_ada", [DEPTH, 6 * D])
    gvec = {k: din(k, [DEPTH, D]) for k in ("g_pre_mix", "g_post_mix", "g_pre_ffn", "g_post_ffn", "g_ssd")}
    w_in = din("w_in", [DEPTH, D, IN_COLS])
    caw_d = din("caw", [DEPTH, 128, 4, 31])
    cab_d = din("cab", [DEPTH, 128, 4])
    lng_d = din("lng", [DEPTH, 128, 4])
    lnb_d = din("lnb", [DEPTH, 128, 4])
    w_a_out = din("w_a_out", [DEPTH, 512, D])
    csw_d = din("csw", [DEPTH, 128, 12, 5])
    csb_d = din("csb", [DEPTH, 128, 12])
    dtb_d = din("dtb", [DEPTH, 32])
    alog_d = din("alog", [DEPTH, 32])
    dsk_d = din("dsk", [DEPTH, 16])
    w_b_out = din("w_b_out", [DEPTH, D, D])
    w_c_out = din("w_c_out", [DEPTH, 512, D])
    w_out = din("w_out", [DEPTH, D, D])
    w_ffn_in = din("w_ffn_in", [DEPTH, D, 2 * D_FF])
    w_ffn_out = din("w_ffn_out", [DEPTH, D_FF, D])
    ident_d = din("ident", [128, 128], BF16)
    masks_d = din("masks", [128, 5, 128])
    csc_d = din("csc", [128, 256], BF16)
    tab_d = din("tab", [5, 4, 2, 128, 16, 512], BF16)
    yout = nc.dram_tensor("yout", [T, D], F32, kind="ExternalOutput").ap()

    mod_d = dscr("mod_d", [DEPTH, 3, 6 * D], F32)
    hfm_d = dscr("hfm_d", [8, 128, T], BF16)
    aglu_d = dscr("aglu_d", [4, 128, T], BF16)
    xbc_d = dscr("xbc_d", [12, 128, T], BF16)
    u_d = dscr("u_d", [4, 128, T], BF16)
    zs_d = dscr("zs_d", [T, D], BF16)
    dt_d = dscr("dt_d", [T, 32], F32)
    gates_d = dscr("gates_d", [24, 128, T], BF16)
    acv_d = dscr("acv_d", [4, 128, T], BF16)
    xs_d = dscr("xs_d", [T, D], BF16)
    bt_d = dscr("bt_d", [T, 256], BF16)
    bc_d = dscr("bc_d", [4, 128, T], BF16)
    yf_d = dscr("yf_d", [T, D], F32)
    yfm_d = dscr("yfm_d", [8, 128, T], BF16)
    f_d = dscr("f_d", [4, 128, T], BF16)
    act_d = dscr("act_d", [22, 128, T], BF16)

    P = Prog(nc)
    ABF = 73 * 1024
    AFP = 13 * 1024
    arena_bf_t = P.sb("arena_bf", [128, ABF], BF16)
    arena_f_t = P.sb("arena_f", [128, AFP], F32)
    AB = Arena(arena_bf_t[:], ABF)
    AFa = Arena(arena_f_t[:], AFP)
    ident = P.sb("ident_sb", [128, 128], BF16)
    masks = P.sb("masks_sb", [128, 5, 128], F32)
    flag = P.sb("flag_sb", [128, 1], F32)
    r_const = P.res("const")
    psum = [P.ps(f"psb{i}", [128, 512], F32) for i in range(8)]
    r_ps = [P.res(f"ps{i}") for i in range(8)]

    def fm_tile(dram, c0, c1, t0, n):
        return dram[c0:c1, :, t0:t0 + n].rearrange("c p t -> p c t")

    P.dma("sync", lambda e: e.dma_start(out=ident[:], in_=ident_d), writes=[r_const], semres=r_const)
    P.dma("sync", lambda e: e.dma_start(out=masks[:], in_=masks_d), writes=[r_const], semres=r_const)
    P.dma("sync", lambda e: e.dma_start(out=flag[:], in_=flag_d), writes=[r_const], semres=r_const)
    M_GT, M_LT, M_LE, M_GE, M_ONE = range(5)

    def new_phase():
        P.barrier()
        AB.reset()
        AFa.reset()

    def phase_mod():
        new_phase()
        cT = AFa.take(8, 3)
        r_cT = P.res("cT")
        sil = AFa.take(8, 3)
        r_sil = P.res("sil")
        P.dma("sync", lambda e: e.dma_start(out=cT, in_=cT_d), writes=[r_cT], semres=r_cT)
        P.op("scalar", lambda e: e.activation(out=sil, in_=cT, func=AF.Silu), reads=[r_cT], writes=[r_sil])
        NB = 4
        wbuf = [AFa.take(8, 256) for _ in range(NB)]
        r_w = [P.res(f"wada{i}") for i in range(NB)]
        brow = [AFa.take(256) for _ in range(2)]
        mrow = [AFa.take(256) for _ in range(2)]
        r_b = [P.res("brow0"), P.res("brow1")]
        r_m = [P.res("mrow0"), P.res("mrow1")]
        k = 0
        for l in range(nl):
            for ct in range(24):
                wb, rw = wbuf[k % NB], r_w[k % NB]
                mr, rm = mrow[k % 2], r_m[k % 2]
                br, rb = brow[k % 2], r_b[k % 2]
                pb = k % 2
                q = "sync" if k % 2 == 0 else "gpsimd"
                k += 1
                src = w_ada[l, :, ct * 256:(ct + 1) * 256].rearrange("(c p) n -> p c n", p=128)
                P.dma(q, lambda e, wb=wb, src=src: e.dma_start(out=wb, in_=src), writes=[rw], semres=rw)
                bsrc = b_ada[l:l + 1, ct * 256:(ct + 1) * 256].partition_broadcast(3)
                P.dma("sync", lambda e, bsrc=bsrc, br=br: e.dma_start(out=br[0:3, :], in_=bsrc), writes=[rb], semres=rb)
                for c in range(8):
                    P.op("tensor", lambda e, c=c, wb=wb, pb=pb: e.matmul(psum[pb][0:3, 0:256], lhsT=sil[:, c, :],
                                                                          rhs=wb[:, c, :], start=(c == 0), stop=(c == 7)),
                         reads=[r_sil, rw], writes=[r_ps[pb]])
                P.op("vector", lambda e, mr=mr, br=br, pb=pb: e.tensor_tensor(out=mr[0:3, :], in0=psum[pb][0:3, 0:256],
                                                                               in1=br[0:3, :], op=ALU.add),
                     reads=[r_ps[pb], rb], writes=[rm])
                dst = mod_d[l, :, ct * 256:(ct + 1) * 256]
                P.dma("sync", lambda e, mr=mr, dst=dst: e.dma_start(out=dst, in_=mr[0:3, :]), reads=[rm], semres=rm)

    def load_rows(l, slot, arena_tiles, spec):
        for dst, rr, kind, gname, part, tmp, rtmp in spec:
            msrc = mod_d[l, slot:slot + 1, part * D:(part + 1) * D].partition_broadcast(128)
            if kind == "shift":
                P.dma("sync", lambda e, dst=dst, msrc=msrc: e.dma_start(out=dst, in_=msrc), writes=[rr], semres=rr)
                continue
            gsrc = gvec[gname][l:l + 1, :].partition_broadcast(128)
            P.dma("sync", lambda e, dst=dst, msrc=msrc: e.dma_start(out=dst, in_=msrc), writes=[rr], semres=rr)
            P.dma("sync", lambda e, tmp=tmp, gsrc=gsrc: e.dma_start(out=tmp, in_=gsrc), writes=[rtmp], semres=rtmp)
            if kind == "scale":
                P.op("vector", lambda e, dst=dst, tmp=tmp: e.scalar_tensor_tensor(out=dst, in0=dst, scalar=1.0, in1=tmp,
                                                                                  op0=ALU.add, op1=ALU.mult),
                     reads=[rr, rtmp], writes=[rr])
            else:
                P.op("gpsimd", lambda e, dst=dst, tmp=tmp: e.tensor_tensor(out=dst, in0=dst, in1=tmp, op=ALU.mult),
                     reads=[rr, rtmp], writes=[rr])

    def load_w(dst, src, rr):
        P.dma("gpsimd", lambda e: e.dma_start(out=dst, in_=src), writes=[rr], semres=rr)

    def rstd_from_ss(ss, rstd, r_ss, r_rstd, n_feat):
        P.op("scalar", lambda e: e.activation(out=rstd, in_=ss, func=AF.Ln, scale=1.0 / n_feat, bias=EPS),
             reads=[r_ss], writes=[r_rstd])
        P.op("scalar", lambda e: e.activation(out=rstd, in_=rstd, func=AF.Exp, scale=-0.5),
             reads=[r_rstd], writes=[r_rstd])

    def load_x_tile(xsrc, t, xt, r_xt):
        tok0 = t * 512
        for j in range(4):
            q = (t % 2) * 4 + j
            P.dma("sync", lambda e, j=j, q=q: e.dma_start(out=xt[q], in_=xsrc[tok0 + j * 128: tok0 + (j + 1) * 128, :]),
                  writes=[r_xt[q]], semres=r_xt[q])

    def norm_transpose_tile(xsrc, t, xt, r_xt, junk, r_junk, ss, r_ss, rstd, r_rstd, gs, r_gs, sh, r_sh, tmp, r_tmp,
                            xh, r_xh, hfm, r_hfm, pbank):
        for j in range(4):
            q = (t % 2) * 4 + j
            P.op("scalar", lambda e, j=j, q=q: e.activation(out=junk, in_=xt[q], func=AF.Square, accum_out=ss[:, j:j + 1]),
                 reads=[r_xt[q]], writes=[r_junk, r_ss])
        rstd_from_ss(ss, rstd, r_ss, r_rstd, D)
        yield

        def part_a(j):
            q = (t % 2) * 4 + j
            P.op("vector", lambda e: e.scalar_tensor_tensor(out=tmp, in0=xt[q], scalar=rstd[:, j:j + 1], in1=gs,
                                                            op0=ALU.mult, op1=ALU.mult),
                 reads=[r_xt[q], r_rstd, r_gs], writes=[r_tmp])
            P.op("vector", lambda e: e.tensor_tensor(out=xh[j % 2], in0=tmp, in1=sh, op=ALU.add),
                 reads=[r_tmp, r_sh], writes=[r_xh[j % 2]])

        def part_b(j):
            pT = psum[pbank][:, :].bitcast(BF16)
            for c in range(8):
                P.op("tensor", lambda e, c=c: e.transpose(out=pT[:, c * 128:(c + 1) * 128],
                                                          in_=xh[j % 2][:, c * 128:(c + 1) * 128], identity=ident[:]),
                     reads=[r_xh[j % 2], r_const], writes=[r_ps[pbank]])
            P.op("scalar", lambda e: e.copy(out=hfm[:, :, j * 128:(j + 1) * 128],
                                            in_=pT.rearrange("p (c t) -> p c t", c=8)),
                 reads=[r_ps[pbank]], writes=[r_hfm])

        part_a(0)
        yield
        for j in range(4):
            part_b(j)
            if j < 3:
                part_a(j + 1)
            yield

    def phase_a(l, xsrc):
        new_phase()
        NCOL = O_GATE
        wA = AB.take(8, NCOL)
        bounds = [0, 1024, 2048, 3072, NCOL]
        r_wAp = [P.res(f"wA{i}") for i in range(4)]
        for pi in (0, 2, 3, 1):
            c0, c1 = bounds[pi], bounds[pi + 1]
            load_w(wA[:, :, c0:c1], w_in[l, :, c0:c1].rearrange("(c p) n -> p c n", p=128), r_wAp[pi])

        def rwA(col):
            return r_wAp[min(col // 1024, 3)]
        hfm = [AB.take(8, 512) for _ in range(2)]
        r_hfm = [P.res("hfm0"), P.res("hfm1")]
        xh = [AB.take(D) for _ in range(2)]
        r_xh = [P.res("xh0"), P.res("xh1")]
        sg = AB.take(4, 512)
        r_sg = P.res("sg")
        oc = [AB.take(512) for _ in range(8)]
        r_oc = [P.res(f"oc{i}") for i in range(8)]
        zs = [AB.take(D) for _ in range(2)]
        r_zs = [P.res("zs0"), P.res("zs1")]
        xt = [AFa.take(D) for _ in range(8)]
        r_xt = [P.res(f"xt{i}") for i in range(8)]
        gs = AFa.take(D)
        r_gs = P.res("gs")
        sh = AFa.take(D)
        r_sh = P.res("sh")
        tmp = AFa.take(D)
        r_tmp = P.res("tmp")
        gtmp = AFa.take(D)
        r_gtmp = P.res("gtmp")
        ss = AFa.take(4)
        r_ss = P.res("ss")
        rstd = AFa.take(4)
        r_rstd = P.res("rstd")
        dtr = [AFa.take(32) for _ in range(2)]
        r_dtr = [P.res("dtr0"), P.res("dtr1")]
        r_junk = P.res("junk")
        ocn = [0]
        pb = [0]

        def next_oc():
            i = ocn[0] % 8
            ocn[0] += 1
            return oc[i], r_oc[i]

        def next_pb():
            i = 1 + (pb[0] % 6)
            pb[0] += 1
            return i

        def prep(tt):
            if tt % 4 == 0:
                load_rows(l, tt // 4, None, [(gs, r_gs, "scale", "g_pre_mix", 1, gtmp, r_gtmp),
                                             (sh, r_sh, "shift", None, 0, None, None)])
            hh, rhh = hfm[tt % 2], r_hfm[tt % 2]
            yield from norm_transpose_tile(xsrc, tt, xt, r_xt, tmp, r_tmp, ss, r_ss, rstd, r_rstd, gs, r_gs, sh, r_sh, tmp,
                                           r_tmp, xh, r_xh, hh, rhh, 0)
            P.dma("sync", lambda e: e.dma_start(out=fm_tile(hfm_d, 0, 8, tt * 512, 512), in_=hh), reads=[rhh], semres=rhh)

        load_x_tile(xsrc, 0, xt, r_xt)
        load_x_tile(xsrc, 1, xt, r_xt)
        for _ in prep(0):
            pass
        for t in range(NT):
            slot = t // 4
            tok0 = t * 512
            if t + 2 < NT:
                load_x_tile(xsrc, t + 2, xt, r_xt)
            h, rh = hfm[t % 2], r_hfm[t % 2]
            nchunk = [0]
            gen = prep(t + 1) if t + 1 < NT else iter(())

            def fm_chunk(col0, epi, gen=gen):
                nchunk[0] += 1
                if nchunk[0] in (4, 7, 10, 13, 16, 19, 22):
                    next(gen, None)
                b = next_pb()
                for c in range(8):
                    P.op("tensor", lambda e, c=c, b=b: e.matmul(psum[b][:, :], lhsT=wA[:, c, col0:col0 + 128],
                                                                 rhs=h[:, c, :], start=(c == 0), stop=(c == 7)),
                         reads=[rwA(col0), rh], writes=[r_ps[b]])
                epi(b)

            for i in range(4):
                fm_chunk(O_AG + i * 128, lambda b, i=i: P.op(
                    "scalar", lambda e: e.activation(out=sg[:, i, :], in_=psum[b][:, :], func=AF.Sigmoid),
                    reads=[r_ps[b]], writes=[r_sg]))
            for i in range(4):
                def epi(b, i=i):
                    o, ro = next_oc()
                    P.op("vector", lambda e: e.tensor_tensor(out=o, in0=psum[b][:, :], in1=sg[:, i, :], op=ALU.mult),
                         reads=[r_ps[b], r_sg], writes=[ro])
                    P.dma("sync", lambda e: e.dma_start(out=aglu_d[i, :, tok0:tok0 + 512], in_=o), reads=[ro], semres=ro)
                fm_chunk(O_AV + i * 128, epi)
            for i in range(12):
                def epi(b, i=i):
                    o, ro = next_oc()
                    P.op("scalar", lambda e: e.copy(out=o, in_=psum[b][:, :]), reads=[r_ps[b]], writes=[ro])
                    P.dma("sync", lambda e: e.dma_start(out=xbc_d[i, :, tok0:tok0 + 512], in_=o), reads=[ro], semres=ro)
                fm_chunk(O_XBC + i * 128, epi)
            for i in range(4):
                def epi(b, i=i):
                    o, ro = next_oc()
                    P.op("vector", lambda e: e.tensor_copy(out=o, in_=psum[b][:, :]), reads=[r_ps[b]], writes=[ro])
                    P.dma("sync", lambda e: e.dma_start(out=u_d[i, :, tok0:tok0 + 512], in_=o), reads=[ro], semres=ro)
                fm_chunk(O_UC + i * 128, epi)
            for j in range(4):
                z, rz = zs[j % 2], r_zs[j % 2]
                for half in range(2):
                    b = next_pb()
                    for c in range(8):
                        P.op("tensor", lambda e, c=c, b=b, j=j, half=half: e.matmul(
                            psum[b][:, :], lhsT=h[:, c, j * 128:(j + 1) * 128],
                            rhs=wA[:, c, O_Z + half * 512: O_Z + (half + 1) * 512], start=(c == 0), stop=(c == 7)),
                            reads=[rwA(O_Z), rh], writes=[r_ps[b]])
                    P.op("scalar", lambda e, b=b, z=z, half=half: e.activation(out=z[:, half * 512:(half + 1) * 512],
                                                                               in_=psum[b][:, :], func=AF.Silu),
                         reads=[r_ps[b]], writes=[rz])
                P.dma("sync", lambda e, z=z, j=j: e.dma_start(out=zs_d[tok0 + j * 128: tok0 + (j + 1) * 128, :], in_=z),
                      reads=[rz], semres=rz)
                b = next_pb()
                dd, rd = dtr[j % 2], r_dtr[j % 2]
                for c in range(8):
                    P.op("tensor", lambda e, c=c, b=b, j=j: e.matmul(
                        psum[b][:, 0:32], lhsT=h[:, c, j * 128:(j + 1) * 128], rhs=wA[:, c, O_DT:O_DT + 32],
                        start=(c == 0), stop=(c == 7)), reads=[rwA(O_DT), rh], writes=[r_ps[b]])
                P.op("vector", lambda e, b=b, dd=dd: e.tensor_copy(out=dd, in_=psum[b][:, 0:32]),
                     reads=[r_ps[b]], writes=[rd])
                P.dma("sync", lambda e, dd=dd, j=j: e.dma_start(out=dt_d[tok0 + j * 128: tok0 + (j + 1) * 128, :], in_=dd),
                      reads=[rd], semres=rd)
            for _ in gen:
                pass

    def phase_gates(l):
        new_phase()
        wG = AB.take(8, 3072)
        r_wGp = [P.res(f"wG{i}") for i in range(6)]
        for pi in range(6):
            c0 = pi * 512
            load_w(wG[:, :, c0:c0 + 512], w_in[l, :, O_GATE + c0:O_GATE + c0 + 512].rearrange("(c p) n -> p c n", p=128),
                   r_wGp[pi])
        hfm = [AB.take(8, 512) for _ in range(2)]
        r_hfm = [P.res("ghfm0"), P.res("ghfm1")]
        oc = [AB.take(512) for _ in range(8)]
        r_oc = [P.res(f"goc{i}") for i in range(8)]
        k = 0

        def ldh(t):
            h, rh = hfm[t % 2], r_hfm[t % 2]
            P.dma("sync", lambda e: e.dma_start(out=h, in_=fm_tile(hfm_d, 0, 8, t * 512, 512)), writes=[rh], semres=rh)

        ldh(0)
        for t in range(NT):
            tok0 = t * 512
            h, rh = hfm[t % 2], r_hfm[t % 2]
            if t + 1 < NT:
                ldh(t + 1)
            for i in range(24):
                b = k % 8
                o, ro = oc[k % 8], r_oc[k % 8]
                k += 1
                for c in range(8):
                    P.op("tensor", lambda e, c=c, b=b, i=i, h=h: e.matmul(psum[b][:, :], lhsT=wG[:, c, i * 128:(i + 1) * 128],
                                                                           rhs=h[:, c, :], start=(c == 0), stop=(c == 7)),
                         reads=[r_wGp[i // 4], rh], writes=[r_ps[b]])
                P.op("scalar", lambda e, b=b, o=o: e.activation(out=o, in_=psum[b][:, :], func=AF.Sigmoid),
                     reads=[r_ps[b]], writes=[ro])
                P.dma("sync", lambda e, o=o, i=i, tok0=tok0: e.dma_start(out=gates_d[i, :, tok0:tok0 + 512], in_=o),
                      reads=[ro], semres=ro)

    def load_halo(dst, rr, dram, C, t, hw):
        tok0 = t * 512
        seg_start = (t % 4 == 0)
        seg_end = (t % 4 == 3)
        lo = tok0 - hw
        hi = tok0 + 512 + hw
        d0 = 0
        if seg_start and t != 4:
            lo = tok0
            d0 = hw
        if seg_end and t != 3:
            hi = tok0 + 512
        if lo > tok0 - hw:
            P.op("gpsimd", lambda e: e.memset(dst[:, :, 0:hw], 0.0), writes=[rr])
        if hi < tok0 + 512 + hw:
            P.op("gpsimd", lambda e: e.memset(dst[:, :, 512 + hw:512 + 2 * hw], 0.0), writes=[rr])
        P.dma("sync", lambda e: e.dma_start(out=dst[:, :, d0:d0 + (hi - lo)], in_=fm_tile(dram, 0, C, lo, hi - lo)),
              writes=[rr], semres=rr)
        if t == 4:
            P.op("gpsimd", lambda e: e.tensor_scalar(out=dst[:, :, 0:hw], in0=dst[:, :, 0:hw], scalar1=flag[:, 0:1],
                                                     scalar2=None, op0=ALU.mult), reads=[rr, r_const], writes=[rr])
        if t == 3:
            P.op("gpsimd", lambda e: e.tensor_scalar(out=dst[:, :, 512 + hw:512 + 2 * hw],
                                                     in0=dst[:, :, 512 + hw:512 + 2 * hw], scalar1=flag[:, 0:1],
                                                     scalar2=None, op0=ALU.mult), reads=[rr, r_const], writes=[rr])

    def phase_conv_a(l):
        new_phase()
        caw = AFa.take(4, 31)
        cab = AFa.take(4)
        lng = AFa.take(4)
        lnb = AFa.take(4)
        r_par = P.res("cpar")
        for dst, src in ((caw, caw_d[l]), (cab, cab_d[l]), (lng, lng_d[l]), (lnb, lnb_d[l])):
            P.dma("sync", lambda e, dst=dst, src=src: e.dma_start(out=dst, in_=src), writes=[r_par], semres=r_par)
        dg = AB.take(4 * 31, 128)
        r_dg = P.res("dg")
        identf = AFa.take(128)
        r_idf = P.res("identf")
        P.op("vector", lambda e: e.tensor_copy(out=identf, in_=ident[:]), reads=[r_const], writes=[r_idf])
        for i in range(4):
            P.op("vector", lambda e, i=i: e.tensor_tensor(
                out=dg[:, i * 31:(i + 1) * 31, :], in0=bc(identf.unsqueeze(1), [128, 31, 128]),
                in1=bc(caw[:, i, :].unsqueeze(2), [128, 31, 128]), op=ALU.mult), reads=[r_idf, r_par], writes=[r_dg])
        ain = [AB.take(4, 542) for _ in range(2)]
        r_ain = [P.res("ain0"), P.res("ain1")]
        acc = AFa.take(4, 512)
        r_acc = [P.res(f"acc{i}") for i in range(4)]
        sq = [AFa.take(512) for _ in range(2)]
        r_sq = [P.res("sq0"), P.res("sq1")]
        mean = AFa.take(512)
        r_mean = P.res("mean")
        rs = AFa.take(512)
        r_rs = P.res("rs")
        xc = [AFa.take(512) for _ in range(2)]
        r_xc = [P.res("xc0"), P.res("xc1")]
        ob = [AB.take(512) for _ in range(8)]
        r_ob = [P.res(f"ob{i}") for i in range(8)]
        ones = masks[:, M_ONE, :]
        nb = 0
        load_halo(ain[0], r_ain[0], aglu_d, 4, 0, 15)
        for t in range(NT):
            tok0 = t * 512
            a, ra = ain[t % 2], r_ain[t % 2]
            if t + 1 < NT:
                load_halo(ain[(t + 1) % 2], r_ain[(t + 1) % 2], aglu_d, 4, t + 1, 15)
            for i in range(4):
                b = 2 + nb % 6
                nb += 1
                for k in range(31):
                    P.op("tensor", lambda e, i=i, k=k, a=a, b=b: e.matmul(psum[b][:, :], lhsT=dg[:, i * 31 + k, :],
                                                                           rhs=a[:, i, k:k + 512], start=(k == 0),
                                                                           stop=(k == 30)),
                         reads=[r_dg, ra], writes=[r_ps[b]])
                P.op("scalar", lambda e, i=i, b=b: e.activation(out=acc[:, i, :], in_=psum[b][:, :], func=AF.Identity,
                                                                bias=cab[:, i:i + 1]),
                     reads=[r_ps[b], r_par], writes=[r_acc[i]])
            for i in range(4):
                P.op("tensor", lambda e, i=i: e.matmul(psum[0][:, :], lhsT=ones, rhs=acc[:, i, :], start=(i == 0),
                                                       stop=(i == 3)), reads=[r_acc[i], r_const], writes=[r_ps[0]])
            for i in range(4):
                P.op("gpsimd", lambda e, i=i: e.tensor_tensor(out=sq[i % 2], in0=acc[:, i, :], in1=acc[:, i, :],
                                                              op=ALU.mult), reads=[r_acc[i]], writes=[r_sq[i % 2]])
                P.op("tensor", lambda e, i=i: e.matmul(psum[1][:, :], lhsT=ones, rhs=sq[i % 2], start=(i == 0),
                                                       stop=(i == 3)), reads=[r_sq[i % 2], r_const], writes=[r_ps[1]])
            P.op("vector", lambda e: e.tensor_scalar(out=mean, in0=psum[0][:, :], scalar1=1.0 / 512, scalar2=None,
                                                     op0=ALU.mult), reads=[r_ps[0]], writes=[r_mean])
            P.op("vector", lambda e: e.tensor_tensor(out=rs, in0=mean, in1=mean, op=ALU.mult), reads=[r_mean], writes=[r_rs])
            P.op("vector", lambda e: e.scalar_tensor_tensor(out=rs, in0=psum[1][:, :], scalar=1.0 / 512, in1=rs,
                                                            op0=ALU.mult, op1=ALU.subtract),
                 reads=[r_ps[1], r_rs], writes=[r_rs])
            P.op("scalar", lambda e: e.activation(out=rs, in_=rs, func=AF.Ln, bias=EPS), reads=[r_rs], writes=[r_rs])
            P.op("scalar", lambda e: e.activation(out=rs, in_=rs, func=AF.Exp, scale=-0.5), reads=[r_rs], writes=[r_rs])
            for i in range(4):
                o, ro = ob[(t * 4 + i) % 8], r_ob[(t * 4 + i) % 8]
                x_, rx_ = xc[i % 2], r_xc[i % 2]
                P.op("vector", lambda e, i=i, x_=x_: e.tensor_tensor(out=x_, in0=acc[:, i, :], in1=mean, op=ALU.subtract),
                     reads=[r_acc[i], r_mean], writes=[rx_])
                P.op("gpsimd", lambda e, x_=x_: e.tensor_tensor(out=x_, in0=x_, in1=rs, op=ALU.mult), reads=[rx_, r_rs],
                     writes=[rx_])
                P.op("scalar", lambda e, i=i, o=o, x_=x_: e.activation(out=o, in_=x_, func=AF.Silu, scale=lng[:, i:i + 1],
                                                                       bias=lnb[:, i:i + 1]),
                     reads=[rx_, r_par], writes=[ro])
                P.dma("sync", lambda e, i=i, o=o, tok0=tok0: e.dma_start(out=acv_d[i, :, tok0:tok0 + 512], in_=o),
                      reads=[ro], semres=ro)

    def phase_conv_s(l):
        new_phase()
        csw = AFa.take(12, 5)
        csb = AFa.take(12)
        r_par = P.res("spar")
        P.dma("sync", lambda e: e.dma_start(out=csw, in_=csw_d[l]), writes=[r_par], semres=r_par)
        P.dma("sync", lambda e: e.dma_start(out=csb, in_=csb_d[l]), writes=[r_par], semres=r_par)
        dgs = AB.take(60, 128)
        r_dgs = P.res("dgs")
        identf = AFa.take(128)
        r_idf = P.res("sidentf")
        P.op("vector", lambda e: e.tensor_copy(out=identf, in_=ident[:]), reads=[r_const], writes=[r_idf])
        for i in range(12):
            P.op("vector", lambda e, i=i: e.tensor_tensor(
                out=dgs[:, i * 5:(i + 1) * 5, :], in0=bc(identf.unsqueeze(1), [128, 5, 128]),
                in1=bc(csw[:, i, :].unsqueeze(2), [128, 5, 128]), op=ALU.mult), reads=[r_idf, r_par], writes=[r_dgs])
        xi = [AB.take(12, 516) for _ in range(2)]
        r_xi = [P.res("xi0"), P.res("xi1")]
        xo = [AB.take(12, 512) for _ in range(2)]
        r_xo = [[P.res(f"xo{b}_{i}") for i in range(12)] for b in range(2)]
        xs = [AB.take(D) for _ in range(2)]
        r_xs = [P.res("xs0"), P.res("xs1")]
        bt = [AB.take(256) for _ in range(2)]
        r_bt = [P.res("bt0"), P.res("bt1")]
        nb = 0
        load_halo(xi[0], r_xi[0], xbc_d, 12, 0, 2)
        for t in range(NT):
            tok0 = t * 512
            a, ra = xi[t % 2], r_xi[t % 2]
            o, ro = xo[t % 2], r_xo[t % 2]
            if t + 1 < NT:
                load_halo(xi[(t + 1) % 2], r_xi[(t + 1) % 2], xbc_d, 12, t + 1, 2)
            for i in range(12):
                b = 4 + nb % 4
                nb += 1
                for k in range(5):
                    P.op("tensor", lambda e, i=i, k=k, a=a, b=b: e.matmul(psum[b][:, :], lhsT=dgs[:, i * 5 + k, :],
                                                                           rhs=a[:, i, k:k + 512], start=(k == 0),
                                                                           stop=(k == 4)),
                         reads=[r_dgs, ra], writes=[r_ps[b]])
                P.op("scalar", lambda e, i=i, o=o, b=b: e.activation(out=o[:, i, :], in_=psum[b][:, :], func=AF.Silu,
                                                                     bias=csb[:, i:i + 1]),
                     reads=[r_ps[b], r_par], writes=[ro[i]])
            P.dma("sync", lambda e, o=o, tok0=tok0: e.dma_start(out=fm_tile(bc_d, 0, 4, tok0, 512), in_=o[:, 8:12, :]),
                  reads=ro[8:12], semres=ro[8])
            for j in range(4):
                pT = psum[j % 2][:, :].bitcast(BF16)
                x_, rx = xs[j % 2], r_xs[j % 2]
                for c in range(8):
                    P.op("tensor", lambda e, c=c, j=j, o=o, pT=pT: e.transpose(out=pT[:, c * 128:(c + 1) * 128],
                                                                               in_=o[:, c, j * 128:(j + 1) * 128],
                                                                               identity=ident[:]),
                         reads=[ro[c], r_const], writes=[r_ps[j % 2]])
                P.op("vector", lambda e, x_=x_, pT=pT: e.tensor_copy(out=x_, in_=pT), reads=[r_ps[j % 2]], writes=[rx])
                P.dma("sync", lambda e, x_=x_, j=j, tok0=tok0: e.dma_start(
                    out=xs_d[tok0 + j * 128: tok0 + (j + 1) * 128, :], in_=x_), reads=[rx], semres=rx)
                pB = psum[2 + j % 2][:, :].bitcast(BF16)
                b_, rb = bt[j % 2], r_bt[j % 2]
                for c in range(2):
                    P.op("tensor", lambda e, c=c, j=j, o=o, pB=pB: e.transpose(out=pB[:, c * 128:(c + 1) * 128],
                                                                               in_=o[:, 8 + c, j * 128:(j + 1) * 128],
                                                                               identity=ident[:]),
                         reads=[ro[8 + c], r_const], writes=[r_ps[2 + j % 2]])
                P.op("scalar", lambda e, b_=b_, pB=pB: e.copy(out=b_, in_=pB[:, 0:256]), reads=[r_ps[2 + j % 2]],
                     writes=[rb])
                P.dma("sync", lambda e, b_=b_, j=j, tok0=tok0: e.dma_start(
                    out=bt_d[tok0 + j * 128: tok0 + (j + 1) * 128, :], in_=b_), reads=[rb], semres=rb)

    def phase_ssd(l):
        new_phase()
        dtb = AFa.take(32)
        arow = AFa.take(32)
        dsk = AFa.take(16)
        r_par = P.res("dpar")
        P.dma("sync", lambda e: e.dma_start(out=dtb, in_=dtb_d[l:l + 1, :].partition_broadcast(128)), writes=[r_par],
              semres=r_par)
        P.dma("sync", lambda e: e.dma_start(out=arow, in_=alog_d[l:l + 1, :].partition_broadcast(128)), writes=[r_par],
              semres=r_par)
        P.dma("sync", lambda e: e.dma_start(out=dsk, in_=dsk_d[l:l + 1, :].partition_broadcast(128)), writes=[r_par],
              semres=r_par)
        P.op("scalar", lambda e: e.activation(out=arow, in_=arow, func=AF.Exp), reads=[r_par], writes=[r_par])
        P.op("vector", lambda e: e.tensor_scalar(out=arow, in0=arow, scalar1=-1.0, scalar2=None, op0=ALU.mult),
             reads=[r_par], writes=[r_par])
        gsr = AFa.take(D)
        r_gsr = P.res("gsr")
        P.dma("sync", lambda e: e.dma_start(out=gsr, in_=gvec["g_ssd"][l:l + 1, :].partition_broadcast(128)),
              writes=[r_gsr], semres=r_gsr)
        r_yfd = [P.res(f"yfd{i}") for i in range(NSUB)]

        class S:
            pass

        def mk(d):
            b = S()
            n = f"d{d}"
            b.H = AFa.take(2, 512); b.r_H = P.res(n + "H")
            b.R = AFa.take(16, 128); b.r_R = P.res(n + "R")
            b.ytmp = AFa.take(D); b.r_ytmp = P.res(n + "ytmp")
            b.yfl = AFa.take(D); b.r_yfl = P.res(n + "yfl")
            b.small = [AFa.take(8, 16) for _ in range(2)]
            b.r_sm = [[P.res(n + f"sm{q}_{i}") for i in range(8)] for q in range(2)]
            b.cst = AFa.take(32); b.r_cst = P.res(n + "cst")
            b.dtr = [AFa.take(32) for _ in range(3)]; b.r_dtr = [P.res(n + f"dtr{i}") for i in range(3)]
            b.ss1 = AFa.take(1); b.r_ss1 = P.res(n + "ss1")
            b.Hb = AB.take(2, 512); b.r_Hb = P.res(n + "Hb")
            b.xs = [AB.take(D) for _ in range(3)]; b.r_xs = [P.res(n + f"xs{i}") for i in range(3)]
            b.bt = [AB.take(256) for _ in range(3)]; b.r_bt = [P.res(n + f"bt{i}") for i in range(3)]
            b.bcf = [AB.take(4, 128) for _ in range(3)]; b.r_bcf = [P.res(n + f"bcf{i}") for i in range(3)]
            b.zt = [AB.take(D) for _ in range(3)]; b.r_zt = [P.res(n + f"zt{i}") for i in range(3)]
            b.xdt = AB.take(D); b.r_xdt = P.res(n + "xdt")
            b.xw = AB.take(D); b.r_xw = P.res(n + "xw")
            b.Lm = AB.take(16, 128); b.r_Lm = P.res(n + "Lm")
            b.Mm = AB.take(16, 128); b.r_Mm = P.res(n + "Mm")
            b.smk = AB.take(2, 128); b.r_smk = P.res(n + "smk")
            b.stb = [AB.take(D) for _ in range(2)]; b.r_stb = [P.res(n + "stb0"), P.res(n + "stb1")]
            b.s16 = [AB.take(2, 16) for _ in range(2)]
            b.r_s16 = [[P.res(n + f"s16_{q}_{i}") for i in range(2)] for q in range(2)]
            b.ydg = [AB.take(D) for _ in range(2)]; b.r_ydg = [P.res(n + "ydg0"), P.res(n + "ydg1")]
            b.yn = AB.take(D); b.r_yn = P.res(n + "yn")
            b.yfm = [AB.take(8, 128) for _ in range(2)]; b.r_yfm = [P.res(n + "yfm0"), P.res(n + "yfm1")]
            return b

        BUF = [mk(0), mk(1)]
        HALF = NSUB // 2

        def chunk_of(d, n):
            return n if d == 0 else NSUB - 1 - n

        def loads(d, n):
            B = BUF[d]
            ci = chunk_of(d, n)
            tok0 = ci * 128
            q = n % 3
            P.dma("sync", lambda e: e.dma_start(out=B.dtr[q], in_=dt_d[tok0:tok0 + 128, :]), writes=[B.r_dtr[q]],
                  semres=B.r_dtr[q])
            P.dma("sync", lambda e: e.dma_start(out=B.bcf[q], in_=fm_tile(bc_d, 0, 4, tok0, 128)), writes=[B.r_bcf[q]],
                  semres=B.r_bcf[q])
            P.dma("sync", lambda e: e.dma_start(out=B.xs[q], in_=xs_d[tok0:tok0 + 128, :]), writes=[B.r_xs[q]],
                  semres=B.r_xs[q])
            P.dma("sync", lambda e: e.dma_start(out=B.bt[q], in_=bt_d[tok0:tok0 + 128, :]), writes=[B.r_bt[q]],
                  semres=B.r_bt[q])
            if n >= HALF:
                P.dma("sync", lambda e: e.dma_start(out=B.zt[q], in_=zs_d[tok0:tok0 + 128, :]), writes=[B.r_zt[q]],
                      semres=B.r_zt[q])

        def names(d, n):
            B = BUF[d]
            v = S()
            q = n % 3
            p2 = n % 2
            v.B = B
            v.ci = chunk_of(d, n)
            v.tok0 = v.ci * 128
            pb = 4 * d
            v.L0, v.L1, v.X, v.Y = pb, pb + 1, pb + 2, pb + 3
            v.x_, v.rx = B.xs[q], B.r_xs[q]
            v.b_, v.rb = B.bt[q], B.r_bt[q]
            v.f_, v.rf = B.bcf[q], B.r_bcf[q]
            v.dr, v.rdr = B.dtr[q], B.r_dtr[q]
            v.z_, v.rz = B.zt[q], B.r_zt[q]
            small = B.small[p2]
            rs_ = B.r_sm[p2]
            v.dt_, v.a_, v.d1_, v.dtw_ = small[:, 0, :], small[:, 1, :], small[:, 2, :], small[:, 6, :]
            v.E3 = small[:, 3:6, :]
            v.r_dt, v.r_a, v.r_d1, v.r_E, v.r_dtw = rs_[0], rs_[1], rs_[2], rs_[3], rs_[6]
            v.w_out_ = v.E3[:, 0, :] if d == 0 else v.E3[:, 1, :]
            v.w_st = v.E3[:, 1, :] if d == 0 else v.E3[:, 0, :]
            v.dec = v.E3[:, 2, :]
            v.stb, v.r_stb = B.stb[p2], B.r_stb[p2]
            v.dt_b, v.dtw_b = B.s16[p2][:, 0, :], B.s16[p2][:, 1, :]
            v.r_dt_b, v.r_dtw_b = B.r_s16[p2][0], B.r_s16[p2][1]
            v.ydg, v.r_ydg = B.ydg[p2], B.r_ydg[p2]
            v.x3 = v.x_.rearrange("p (k q) -> p k q", k=16)
            return v

        def local(d, n):
            v = names(d, n)
            B = v.B
            dc = slice(d * 16, d * 16 + 16)
            dt_, a_, d1_, dtw_, E3 = v.dt_, v.a_, v.d1_, v.dtw_, v.E3
            L0, L1, X, Y = v.L0, v.L1, v.X, v.Y
            f_, rf = v.f_, v.rf
            P.op("vector", lambda e: e.tensor_tensor(out=dt_, in0=v.dr[:, dc], in1=dtb[:, dc], op=ALU.add),
                 reads=[v.rdr, r_par], writes=[v.r_dt])
            P.op("scalar", lambda e: e.activation(out=dt_, in_=dt_, func=AF.Exp), reads=[v.r_dt], writes=[v.r_dt])
            P.op("scalar", lambda e: e.activation(out=dt_, in_=dt_, func=AF.Ln, bias=1.0), reads=[v.r_dt],
                 writes=[v.r_dt])
            P.op("vector", lambda e: e.tensor_tensor(out=a_, in0=dt_, in1=arow[:, dc], op=ALU.mult),
                 reads=[v.r_dt, r_par], writes=[v.r_a])
            yield
            tri = masks[:, M_LE, :] if d == 0 else masks[:, M_LT, :]
            P.op("tensor", lambda e: e.matmul(psum[X][:, 0:16], lhsT=tri, rhs=a_, start=True, stop=True),
                 reads=[v.r_a, r_const], writes=[r_ps[X]])
            P.op("tensor", lambda e: e.matmul(psum[X][:, 16:32], lhsT=masks[:, M_ONE, :], rhs=a_, start=True, stop=True),
                 reads=[v.r_a, r_const], writes=[r_ps[X]])
            for g in range(2):
                P.op("tensor", lambda e, g=g: e.matmul(psum[X][:, 64 + g * 128: 64 + (g + 1) * 128], lhsT=f_[:, g, :],
                                                       rhs=f_[:, 2 + g, :], start=True, stop=True),
                     reads=[rf], writes=[r_ps[X]])
            P.op("vector", lambda e: e.tensor_copy(out=B.cst, in_=psum[X][:, 0:32]), reads=[r_ps[X]], writes=[B.r_cst])
            m1 = masks[:, M_LE, :] if d == 0 else masks[:, M_GE, :]
            m2 = masks[:, M_GT, :] if d == 0 else masks[:, M_LT, :]
            P.op("vector", lambda e: e.tensor_tensor(out=B.smk, in0=psum[X][:, 64:320].rearrange("p (g l) -> p g l", g=2),
                                                     in1=bc(m1.unsqueeze(1), [128, 2, 128]), op=ALU.mult),
                 reads=[r_ps[X], r_const], writes=[B.r_smk])
            yield
            cst = B.cst
            P.op("vector", lambda e: e.tensor_tensor(out=d1_, in0=cst[:, 16:32], in1=cst[:, 0:16], op=ALU.subtract),
                 reads=[B.r_cst], writes=[v.r_d1])
            P.op("scalar", lambda e: e.activation(out=E3[:, 0, :], in_=cst[:, 0:16], func=AF.Exp), reads=[B.r_cst],
                 writes=[v.r_E])
            P.op("scalar", lambda e: e.activation(out=E3[:, 1, :], in_=d1_, func=AF.Exp), reads=[v.r_d1], writes=[v.r_E])
            P.op("scalar", lambda e: e.activation(out=E3[:, 2, :], in_=cst[:, 16:32], func=AF.Exp), reads=[B.r_cst],
                 writes=[v.r_E])
            P.op("vector", lambda e: e.tensor_tensor(out=v.dtw_b, in0=dt_, in1=v.w_st, op=ALU.mult), reads=[v.r_dt, v.r_E],
                 writes=[v.r_dtw_b])
            P.op("gpsimd", lambda e: e.tensor_copy(out=v.dt_b, in_=dt_), reads=[v.r_dt], writes=[v.r_dt_b])
            yield
            for k in range(8):
                P.op("scalar", lambda e, k=k: e.activation(out=B.R[:, k, :], in_=m1, func=AF.Identity,
                                                           scale=a_[:, k:k + 1]),
                     reads=[v.r_a, r_const], writes=[B.r_R])
            P.op("vector", lambda e: e.tensor_tensor(out=B.R[:, 8:16, :], in0=bc(a_[:, 8:16].unsqueeze(2), [128, 8, 128]),
                                                     in1=bc(m1.unsqueeze(1), [128, 8, 128]), op=ALU.mult),
                 reads=[v.r_a, r_const], writes=[B.r_R])
            P.op("vector", lambda e: e.tensor_tensor(out=B.xw.rearrange("p (k q) -> p k q", k=16), in0=v.x3,
                                                     in1=bc(v.dtw_b.unsqueeze(2), [128, 16, 64]), op=ALU.mult),
                 reads=[v.rx, v.r_dtw_b], writes=[B.r_xw])
            P.op("vector", lambda e: e.tensor_tensor(out=B.xdt.rearrange("p (k q) -> p k q", k=16), in0=v.x3,
                                                     in1=bc(v.dt_b.unsqueeze(2), [128, 16, 64]), op=ALU.mult),
                 reads=[v.rx, v.r_dt_b], writes=[B.r_xdt])
            yield
            for q in range(4):
                bq = L0 + q % 2
                P.op("tensor", lambda e, q=q, bq=bq: e.matmul(psum[bq][:, :], lhsT=m2,
                                                              rhs=B.R[:, q * 4:(q + 1) * 4, :].rearrange("p k l -> p (k l)"),
                                                              start=True, stop=True), reads=[B.r_R, r_const],
                     writes=[r_ps[bq]])
                P.op("scalar", lambda e, q=q, bq=bq: e.activation(
                    out=B.Lm[:, q * 4:(q + 1) * 4, :].rearrange("p k l -> p (k l)"), in_=psum[bq][:, :], func=AF.Exp),
                    reads=[r_ps[bq]], writes=[B.r_Lm])
                if q == 1:
                    yield
            yield
            for g in range(2):
                P.op("tensor", lambda e, g=g: e.matmul(psum[X + g][:, :], lhsT=v.b_[:, g * 128:(g + 1) * 128],
                                                       rhs=B.xw[:, g * 512:(g + 1) * 512], start=True, stop=True),
                     reads=[v.rb, B.r_xw], writes=[r_ps[X + g]])
                P.op("scalar", lambda e, g=g: e.copy(out=v.stb[:, g * 512:(g + 1) * 512], in_=psum[X + g][:, :]),
                     reads=[r_ps[X + g]], writes=[v.r_stb])
            P.op("vector", lambda e: e.tensor_tensor(out=B.Mm.rearrange("p (g k) l -> p g k l", g=2),
                                                     in0=B.Lm.rearrange("p (g k) l -> p g k l", g=2),
                                                     in1=bc(B.smk.unsqueeze(2), [128, 2, 8, 128]), op=ALU.mult),
                 reads=[B.r_Lm, B.r_smk], writes=[B.r_Mm])
            yield
            for k in range(16):
                bk = L0 + k // 8
                P.op("tensor", lambda e, k=k, bk=bk: e.matmul(psum[bk][:, (k % 8) * 64:(k % 8 + 1) * 64],
                                                              lhsT=B.Mm[:, k, :], rhs=B.xdt[:, k * 64:(k + 1) * 64],
                                                              start=True, stop=True),
                     reads=[B.r_Mm, B.r_xdt], writes=[r_ps[bk]])
            for g in range(2):
                P.op("scalar", lambda e, g=g: e.copy(out=v.ydg[:, g * 512:(g + 1) * 512], in_=psum[L0 + g][:, :]),
                     reads=[r_ps[L0 + g]], writes=[v.r_ydg])
            yield

        def recur(d, n):
            v = names(d, n)
            B = v.B
            ci, tok0 = v.ci, v.tok0
            fin = n >= HALF
            L0, L1, X, Y = v.L0, v.L1, v.X, v.Y
            H, r_H, Hb, r_Hb = B.H, B.r_H, B.Hb, B.r_Hb
            ytmp, r_ytmp, yfl, r_yfl = B.ytmp, B.r_ytmp, B.yfl, B.r_yfl
            for g in range(2):
                P.op("tensor", lambda e, g=g: e.matmul(psum[X + g][:, :], lhsT=v.f_[:, 2 + g, :], rhs=Hb[:, g, :],
                                                       start=True, stop=True), reads=[v.rf, r_Hb], writes=[r_ps[X + g]])
            for g in range(2):
                P.op("gpsimd", lambda e, g=g: e.tensor_tensor(out=H[:, g, :].rearrange("p (k q) -> p k q", k=8),
                                                              in0=H[:, g, :].rearrange("p (k q) -> p k q", k=8),
                                                              in1=bc(v.dec[:, g * 8:(g + 1) * 8].unsqueeze(2), [128, 8, 64]),
                                                              op=ALU.mult), reads=[r_H, v.r_E], writes=[r_H])
            P.op("gpsimd", lambda e: e.tensor_tensor(out=H.rearrange("p g q -> p (g q)"),
                                                     in0=H.rearrange("p g q -> p (g q)"), in1=v.stb, op=ALU.add),
                 reads=[r_H, v.r_stb], writes=[r_H])
            nxt = ci + 1 if d == 0 else ci - 1
            at_edge = (nxt % 16 == 0) if d == 0 else (ci % 16 == 0)
            if at_edge:
                coupled = (d == 0 and nxt == 16) or (d == 1 and ci == 16)
                if coupled:
                    P.op("vector", lambda e: e.tensor_scalar(out=H, in0=H, scalar1=flag[:, 0:1], scalar2=None,
                                                             op0=ALU.mult), reads=[r_H, r_const], writes=[r_H])
                else:
                    P.op("vector", lambda e: e.memset(H, 0.0), writes=[r_H])
            P.op("scalar", lambda e: e.copy(out=Hb, in_=H), reads=[r_H], writes=[r_Hb])
            for g in range(2):
                P.op("vector", lambda e, g=g: e.tensor_tensor(
                    out=ytmp[:, g * 512:(g + 1) * 512].rearrange("p (k q) -> p k q", k=8),
                    in0=psum[X + g][:, :].rearrange("p (k q) -> p k q", k=8),
                    in1=bc(v.w_out_[:, g * 8:(g + 1) * 8].unsqueeze(2), [128, 8, 64]), op=ALU.mult),
                    reads=[r_ps[X + g], v.r_E], writes=[r_ytmp])
            yield
            P.op("gpsimd", lambda e: e.tensor_tensor(out=ytmp, in0=ytmp, in1=v.ydg, op=ALU.add),
                 reads=[r_ytmp, v.r_ydg], writes=[r_ytmp])
            yield
            if not fin:
                P.dma("sync", lambda e: e.dma_start(out=yf_d[tok0:tok0 + 128, :], in_=ytmp), reads=[r_ytmp],
                      writes=[r_yfd[ci]], semres=r_ytmp)
                return
            ss1, r_ss1, yn, r_yn = B.ss1, B.r_ss1, B.yn, B.r_yn
            P.dma("sync", lambda e: e.dma_start(out=yfl, in_=yf_d[tok0:tok0 + 128, :]), reads=[r_yfd[ci]],
                  writes=[r_yfl], semres=r_yfl)
            P.op("gpsimd", lambda e: e.tensor_tensor(out=ytmp, in0=ytmp, in1=yfl, op=ALU.add),
                 reads=[r_ytmp, r_yfl], writes=[r_ytmp])
            P.op("gpsimd", lambda e: e.tensor_tensor(out=yfl.rearrange("p (k q) -> p k q", k=16), in0=v.x3,
                                                     in1=bc(dsk.unsqueeze(2), [128, 16, 64]), op=ALU.mult),
                 reads=[v.rx, r_par, r_yfl], writes=[r_yfl])
            yield
            P.op("vector", lambda e: e.tensor_tensor(out=ytmp, in0=ytmp, in1=yfl, op=ALU.add),
                 reads=[r_ytmp, r_yfl], writes=[r_ytmp])
            P.op("gpsimd", lambda e: e.tensor_tensor(out=ytmp, in0=ytmp, in1=v.z_, op=ALU.mult),
                 reads=[r_ytmp, v.rz], writes=[r_ytmp])
            P.op("scalar", lambda e: e.activation(out=yfl, in_=ytmp, func=AF.Square, accum_out=ss1),
                 reads=[r_ytmp], writes=[r_yfl, r_ss1])
            rstd_from_ss(ss1, ss1, r_ss1, r_ss1, D)
            P.op("vector", lambda e: e.scalar_tensor_tensor(out=yn, in0=ytmp, scalar=ss1[:, 0:1], in1=gsr,
                                                            op0=ALU.mult, op1=ALU.mult),
                 reads=[r_ytmp, r_ss1, r_gsr], writes=[r_yn])
            yield
            pT = psum[L0][:, :].bitcast(BF16)
            for c in range(8):
                P.op("tensor", lambda e, c=c: e.transpose(out=pT[:, c * 128:(c + 1) * 128],
                                                          in_=yn[:, c * 128:(c + 1) * 128], identity=ident[:]),
                     reads=[r_yn, r_const], writes=[r_ps[L0]])
            yo, ryo = B.yfm[n % 2], B.r_yfm[n % 2]
            P.op("scalar", lambda e: e.copy(out=yo, in_=pT.rearrange("p (c t) -> p c t", c=8)), reads=[r_ps[L0]],
                 writes=[ryo])
            P.dma("sync", lambda e: e.dma_start(out=fm_tile(yfm_d, 0, 8, tok0, 128), in_=yo), reads=[ryo], semres=ryo)
            yield

        for d in range(2):
            B = BUF[d]
            P.op("vector", lambda e, B=B: e.memset(B.H, 0.0), writes=[B.r_H])
            P.op("scalar", lambda e, B=B: e.copy(out=B.Hb, in_=B.H), reads=[B.r_H], writes=[B.r_Hb])
            loads(d, 0)
        for m in range(NSUB + 1):
            for d in range(2):
                if m + 1 < NSUB:
                    loads(d, m + 1)
            gens = []
            for d in range(2):
                if m < NSUB:
                    gens.append(local(d, m))
            for d in range(2):
                if m >= 1:
                    gens.append(recur(d, m - 1))
            while gens:
                alive = []
                for g_ in gens:
                    try:
                        next(g_)
                        alive.append(g_)
                    except StopIteration:
                        pass
                gens = alive

    def phase_fnet(l):
        new_phase()
        csc = AB.take(256)
        r_csc = P.res("csc")
        P.dma("sync", lambda e: e.dma_start(out=csc, in_=csc_d), writes=[r_csc], semres=r_csc)
        uv = AB.take(NSUB, D)
        r_uv = P.res("uv")
        ut = [AB.take(4, 512) for _ in range(2)]
        r_ut = [P.res("ut0"), P.res("ut1")]
        tab = [AB.take(16, 512) for _ in range(2)]
        r_tab = [P.res("tab0"), P.res("tab1")]
        ft = [AB.take(4, 512) for _ in range(2)]
        r_ft = [P.res("ft0"), P.res("ft1")]
        def ld_u(tt):
            P.dma("sync", lambda e: e.dma_start(out=ut[tt % 2], in_=fm_tile(u_d, 0, 4, tt * 512, 512)),
                  writes=[r_ut[tt % 2]], semres=r_ut[tt % 2])

        ld_u(0)
        for t in range(NT):
            tok0 = t * 512
            u, ru = ut[t % 2], r_ut[t % 2]
            if t + 1 < NT:
                ld_u(t + 1)
            for j in range(4):
                for half in range(2):
                    b = (j * 2 + half) % 8
                    for gg in range(2):
                        g = half * 2 + gg
                        P.op("tensor", lambda e, g=g, gg=gg, j=j, b=b, u=u: e.matmul(
                            psum[b][:, gg * 256:(gg + 1) * 256], lhsT=u[:, g, j * 128:(j + 1) * 128], rhs=csc,
                            start=True, stop=True), reads=[ru, r_csc], writes=[r_ps[b]])
                    eng = "vector" if half == 0 else "scalar"
                    dst = uv[:, t * 4 + j, half * 512:(half + 1) * 512]
                    if eng == "vector":
                        P.op("vector", lambda e, b=b, dst=dst: e.tensor_copy(out=dst, in_=psum[b][:, :]),
                             reads=[r_ps[b]], writes=[r_uv])
                    else:
                        P.op("scalar", lambda e, b=b, dst=dst: e.copy(out=dst, in_=psum[b][:, :]),
                             reads=[r_ps[b]], writes=[r_uv])
        blocks = {0: [(0, 0), (1, 1)], 1: [(2, 0), (3, 1)], 2: [(4, 2)]}
        allsteps = []
        for oseg in range(3):
            for kt in range(4):
                steps = [(bi, iseg, cs) for (bi, iseg) in blocks[oseg] for cs in range(2)]
                for si, (bi, iseg, cs) in enumerate(steps):
                    allsteps.append((oseg, kt, bi, iseg, cs, si, len(steps)))

        def ld_tab(n):
            oseg, kt, bi, iseg, cs, si, ns = allsteps[n]
            P.dma("sync", lambda e: e.dma_start(out=tab[n % 2], in_=tab_d[bi, kt, cs]), writes=[r_tab[n % 2]],
                  semres=r_tab[n % 2])

        ld_tab(0)
        for n, (oseg, kt, bi, iseg, cs, si, ns) in enumerate(allsteps):
            if n + 1 < len(allsteps):
                ld_tab(n + 1)
            tb, rt = tab[n % 2], r_tab[n % 2]
            tot = ns * 16
            for ncn in range(16):
                cnt = si * 16 + ncn
                for g in range(4):
                    P.op("tensor", lambda e, g=g, ncn=ncn, cnt=cnt: e.matmul(
                        psum[g + 4 * ((oseg * 4 + kt) % 2)][:, :],
                        lhsT=uv[:, iseg * 16 + ncn, g * 256 + cs * 128: g * 256 + (cs + 1) * 128],
                        rhs=tb[:, ncn, :], start=(cnt == 0), stop=(cnt == tot - 1)),
                        reads=[r_uv, rt], writes=[r_ps[g + 4 * ((oseg * 4 + kt) % 2)]])
            if si == ns - 1:
                pb0 = 4 * ((oseg * 4 + kt) % 2)
                fo, rfo = ft[(oseg * 4 + kt) % 2], r_ft[(oseg * 4 + kt) % 2]
                for g in range(4):
                    if g % 2 == 0:
                        P.op("vector", lambda e, g=g: e.tensor_copy(out=fo[:, g, :], in_=psum[pb0 + g][:, :]),
                             reads=[r_ps[pb0 + g]], writes=[rfo])
                    else:
                        P.op("scalar", lambda e, g=g: e.copy(out=fo[:, g, :], in_=psum[pb0 + g][:, :]),
                             reads=[r_ps[pb0 + g]], writes=[rfo])
                tok0 = oseg * SEG + kt * 512
                P.dma("sync", lambda e: e.dma_start(out=fm_tile(f_d, 0, 4, tok0, 512), in_=fo), reads=[rfo], semres=rfo)

    def phase_merge(l, xsrc):
        new_phase()
        wa = AB.take(4, D)
        wb = AB.take(8, D)
        wc = AB.take(4, D)
        wo = AB.take(8, D)
        r_w1 = [P.res("wE0"), P.res("wE1"), P.res("wE2"), P.res("wE3")]
        r_wo = [P.res("wo0"), P.res("wo1")]
        for hq in range(4):
            cq = slice(hq * 256, (hq + 1) * 256)
            load_w(wa[:, :, cq], w_a_out[l][:, cq].rearrange("(c p) n -> p c n", p=128), r_w1[hq])
            load_w(wb[:, :, cq], w_b_out[l][:, cq].rearrange("(c p) n -> p c n", p=128), r_w1[hq])
            load_w(wc[:, :, cq], w_c_out[l][:, cq].rearrange("(c p) n -> p c n", p=128), r_w1[hq])
        for hq in range(2):
            cq = slice(hq * 512, (hq + 1) * 512)
            load_w(wo[:, :, cq], w_out[l][:, cq].rearrange("(c p) n -> p c n", p=128), r_wo[hq])
        ia = [AB.take(4, 512) for _ in range(2)]
        iy = [AB.take(8, 512) for _ in range(2)]
        if_ = [AB.take(4, 512) for _ in range(2)]
        ig = [AB.take(24, 512) for _ in range(2)]
        r_in = [P.res("Ein0"), P.res("Ein1")]
        mb = [AB.take(8, 512) for _ in range(2)]
        r_mb = [P.res("mb0"), P.res("mb1")]
        mf = [AFa.take(512) for _ in range(2)]
        r_mf = [P.res("mf0"), P.res("mf1")]
        tp = [AFa.take(512) for _ in range(2)]
        r_tp = [P.res("tp0"), P.res("tp1")]
        tq = [AFa.take(512) for _ in range(2)]
        r_tq = [P.res("tq0"), P.res("tq1")]
        xt = [AFa.take(D) for _ in range(2)]
        r_xt = [P.res("ext0"), P.res("ext1")]
        gg = [AFa.take(D) for _ in range(2)]
        r_gg = [P.res("gg0"), P.res("gg1")]
        gtmp = AFa.take(D)
        r_gtmp = P.res("egtmp")
        tmp = [AFa.take(D) for _ in range(2)]
        r_tmp = [P.res("etmp0"), P.res("etmp1")]
        xo = [AFa.take(D) for _ in range(2)]
        r_xo = [P.res("exo0"), P.res("exo1")]
        ss = [AFa.take(2) for _ in range(2)]
        r_ss = [P.res("ess0"), P.res("ess1")]
        kk = 0
        for t in range(NT):
            tok0 = t * 512
            slot = t // 4
            if t % 4 == 0:
                load_rows(l, slot, None, [(gg[slot % 2], r_gg[slot % 2], "gate", "g_post_mix", 2, gtmp, r_gtmp)])
            ia_, iy_, if__, ig_, rin = ia[t % 2], iy[t % 2], if_[t % 2], ig[t % 2], r_in[t % 2]
            mb_, rmb = mb[t % 2], r_mb[t % 2]

            def ld_in(tt):
                for dst, src, C in ((ia[tt % 2], acv_d, 4), (iy[tt % 2], yfm_d, 8), (if_[tt % 2], f_d, 4),
                                    (ig[tt % 2], gates_d, 24)):
                    P.dma("sync", lambda e, dst=dst, src=src, C=C: e.dma_start(out=dst,
                                                                               in_=fm_tile(src, 0, C, tt * 512, 512)),
                          writes=[r_in[tt % 2]], semres=r_in[tt % 2])
            if t == 0:
                ld_in(0)
            if t + 1 < NT:
                ld_in(t + 1)
            for i in range(8):
                cs_ = slice(i * 128, (i + 1) * 128)
                b0 = 3 * (kk % 2)
                mf_, rmf = mf[kk % 2], r_mf[kk % 2]
                tp_, rtp = tp[kk % 2], r_tp[kk % 2]
                kk += 1
                for c in range(4):
                    P.op("tensor", lambda e, c=c, cs_=cs_, b0=b0: e.matmul(psum[b0][:, :], lhsT=wa[:, c, cs_],
                                                                            rhs=ia_[:, c, :], start=(c == 0), stop=(c == 3)),
                         reads=[r_w1[i // 2], rin], writes=[r_ps[b0]])
                for c in range(8):
                    P.op("tensor", lambda e, c=c, cs_=cs_, b0=b0: e.matmul(psum[b0 + 1][:, :], lhsT=wb[:, c, cs_],
                                                                            rhs=iy_[:, c, :], start=(c == 0), stop=(c == 7)),
                         reads=[r_w1[i // 2], rin], writes=[r_ps[b0 + 1]])
                for c in range(4):
                    P.op("tensor", lambda e, c=c, cs_=cs_, b0=b0: e.matmul(psum[b0 + 2][:, :], lhsT=wc[:, c, cs_],
                                                                            rhs=if__[:, c, :], start=(c == 0), stop=(c == 3)),
                         reads=[r_w1[i // 2], rin], writes=[r_ps[b0 + 2]])
                P.op("vector", lambda e, i=i, b0=b0: e.tensor_tensor(out=mf_, in0=psum[b0][:, :], in1=ig_[:, i, :],
                                                                     op=ALU.mult),
                     reads=[r_ps[b0], rin], writes=[rmf])
                P.op("vector", lambda e, i=i, b0=b0: e.tensor_tensor(out=tp_, in0=psum[b0 + 1][:, :], in1=ig_[:, 8 + i, :],
                                                                     op=ALU.mult),
                     reads=[r_ps[b0 + 1], rin], writes=[rtp])
                tq_, rtq = tq[(kk - 1) % 2], r_tq[(kk - 1) % 2]
                P.op("vector", lambda e, i=i, b0=b0: e.tensor_tensor(out=tq_, in0=psum[b0 + 2][:, :], in1=ig_[:, 16 + i, :],
                                                                     op=ALU.mult),
                     reads=[r_ps[b0 + 2], rin], writes=[rtq])
                P.op("gpsimd", lambda e: e.tensor_tensor(out=mf_, in0=mf_, in1=tp_, op=ALU.add), reads=[rmf, rtp],
                     writes=[rmf])
                P.op("gpsimd", lambda e, i=i: e.tensor_tensor(out=mb_[:, i, :], in0=mf_, in1=tq_, op=ALU.add),
                     reads=[rmf, rtq], writes=[rmb])
                if t >= 1 and i % 2 == 1:
                    tp_t = t - 1
                    out_proj_residual(tp_t, mb[tp_t % 2], r_mb[tp_t % 2], wo, r_wo, 8, xsrc, xt, r_xt,
                                      gg[(tp_t // 4) % 2], r_gg[(tp_t // 4) % 2], tmp, r_tmp, xo, r_xo, ss, r_ss,
                                      ((6, 7),), js=(i // 2,))
        tp_t = NT - 1
        out_proj_residual(tp_t, mb[tp_t % 2], r_mb[tp_t % 2], wo, r_wo, 8, xsrc, xt, r_xt, gg[(tp_t // 4) % 2],
                          r_gg[(tp_t // 4) % 2], tmp, r_tmp, xo, r_xo, ss, r_ss, ((6, 7), (4, 5)))

    def out_proj_residual(t, act, r_act, w, r_wh, nk, xsrc, xt, r_xt, gg, r_gg, tmps, r_tmps, xo, r_xo, sss, r_sss, banks,
                          js=(0, 1, 2, 3)):
        tok0 = t * 512
        for j in js:
            x_, rx = xt[j % 2], r_xt[j % 2]
            o_, ro = xo[j % 2], r_xo[j % 2]
            tmp, r_tmp = tmps[j % 2], r_tmps[j % 2]
            ss, r_ss = sss[j % 2], r_sss[j % 2]
            r0 = tok0 + j * 128
            P.dma("sync", lambda e, x_=x_, r0=r0: e.dma_start(out=x_, in_=xsrc[r0:r0 + 128, :]), writes=[rx], semres=rx)
            bk = banks[j % len(banks)]
            for half in range(2):
                b = bk[half]
                for c in range(nk):
                    P.op("tensor", lambda e, c=c, b=b, j=j, half=half: e.matmul(
                        psum[b][:, :], lhsT=act[:, c, j * 128:(j + 1) * 128], rhs=w[:, c, half * 512:(half + 1) * 512],
                        start=(c == 0), stop=(c == nk - 1)), reads=[r_act, r_wh[half]], writes=[r_ps[b]])
            P.op("scalar", lambda e, o_=o_, bk=bk, ss=ss: e.activation(out=o_[:, 0:512], in_=psum[bk[0]][:, :],
                                                                       func=AF.Square, accum_out=ss[:, 0:1]),
                 reads=[r_ps[bk[0]]], writes=[ro, r_ss])
            P.op("scalar", lambda e, o_=o_, bk=bk, ss=ss: e.activation(out=o_[:, 512:1024], in_=psum[bk[1]][:, :],
                                                                       func=AF.Square, accum_out=ss[:, 1:2]),
                 reads=[r_ps[bk[1]]], writes=[ro, r_ss])
            P.op("vector", lambda e, ss=ss: e.tensor_tensor(out=ss[:, 0:1], in0=ss[:, 0:1], in1=ss[:, 1:2], op=ALU.add),
                 reads=[r_ss], writes=[r_ss])
            rstd_from_ss(ss[:, 0:1], ss[:, 0:1], r_ss, r_ss, D)
            for half in range(2):
                hs = slice(half * 512, (half + 1) * 512)
                P.op("vector", lambda e, half=half, hs=hs, bk=bk, ss=ss, tmp=tmp: e.scalar_tensor_tensor(
                    out=tmp[:, hs], in0=psum[bk[half]][:, :], scalar=ss[:, 0:1], in1=gg[:, hs], op0=ALU.mult,
                    op1=ALU.mult), reads=[r_ps[bk[half]], r_ss, r_gg], writes=[r_tmp])
            P.op("gpsimd", lambda e, o_=o_, x_=x_, tmp=tmp: e.tensor_tensor(out=o_, in0=tmp, in1=x_, op=ALU.add),
                 reads=[r_tmp, rx], writes=[ro])
            P.dma("sync", lambda e, o_=o_, r0=r0: e.dma_start(out=yout[r0:r0 + 128, :], in_=o_), reads=[ro], semres=ro)

    def phase_ffn1(l):
        new_phase()
        NCOL = 2 * D_FF
        wF = AB.take(8, NCOL)
        r_wFp = [P.res(f"wF{i}") for i in range(6)]
        for pi in (0, 2, 3, 1, 4, 5):
            c0, c1 = pi * 1024, min(NCOL, (pi + 1) * 1024)
            load_w(wF[:, :, c0:c1], w_ffn_in[l, :, c0:c1].rearrange("(c p) n -> p c n", p=128), r_wFp[pi])
        hfm = [AB.take(8, 512) for _ in range(2)]
        r_hfm = [P.res("fh0"), P.res("fh1")]
        xh = [AB.take(D) for _ in range(2)]
        r_xh = [P.res("fxh0"), P.res("fxh1")]
        sgt = [AB.take(512) for _ in range(2)]
        r_sgt = [P.res("sgt0"), P.res("sgt1")]
        oc = [AB.take(512) for _ in range(6)]
        r_oc = [P.res(f"foc{i}") for i in range(6)]
        xt = [AFa.take(D) for _ in range(8)]
        r_xt = [P.res(f"fxt{i}") for i in range(8)]
        gs = AFa.take(D)
        r_gs = P.res("fgs")
        sh = AFa.take(D)
        r_sh = P.res("fsh")
        tmp = AFa.take(D)
        r_tmp = P.res("ftmp")
        gtmp = AFa.take(D)
        r_gtmp = P.res("fgtmp")
        ss = AFa.take(4)
        r_ss = P.res("fss")
        rstd = AFa.take(4)
        r_rstd = P.res("frstd")
        r_junk = P.res("fjunk")
        k = 0

        def prep(tt):
            if tt % 4 == 0:
                load_rows(l, tt // 4, None, [(gs, r_gs, "scale", "g_pre_ffn", 4, gtmp, r_gtmp),
                                             (sh, r_sh, "shift", None, 3, None, None)])
            yield from norm_transpose_tile(yout, tt, xt, r_xt, tmp, r_tmp, ss, r_ss, rstd, r_rstd, gs, r_gs, sh, r_sh, tmp,
                                           r_tmp, xh, r_xh, hfm[tt % 2], r_hfm[tt % 2], 0)

        load_x_tile(yout, 0, xt, r_xt)
        load_x_tile(yout, 1, xt, r_xt)
        for _ in prep(0):
            pass
        for t in range(NT):
            slot = t // 4
            tok0 = t * 512
            if t + 2 < NT:
                load_x_tile(yout, t + 2, xt, r_xt)
            h, rh = hfm[t % 2], r_hfm[t % 2]
            gen = prep(t + 1) if t + 1 < NT else iter(())
            for i in range(22):
                if i in (2, 5, 8, 11, 14, 17, 20):
                    next(gen, None)
                b1 = 1 + (2 * k) % 6
                b2 = 1 + (2 * k + 1) % 6
                sg_, rsg = sgt[k % 2], r_sgt[k % 2]
                o, ro = oc[k % 6], r_oc[k % 6]
                k += 1
                for c in range(8):
                    P.op("tensor", lambda e, c=c, b1=b1, i=i, h=h: e.matmul(psum[b1][:, :], lhsT=wF[:, c, i * 128:(i + 1) * 128],
                                                                             rhs=h[:, c, :], start=(c == 0), stop=(c == 7)),
                         reads=[r_wFp[(i * 128) // 1024], rh], writes=[r_ps[b1]])
                for c in range(8):
                    P.op("tensor", lambda e, c=c, b2=b2, i=i, h=h: e.matmul(
                        psum[b2][:, :], lhsT=wF[:, c, D_FF + i * 128: D_FF + (i + 1) * 128], rhs=h[:, c, :],
                        start=(c == 0), stop=(c == 7)), reads=[r_wFp[(D_FF + i * 128) // 1024], rh], writes=[r_ps[b2]])
                P.op("scalar", lambda e, b1=b1, sg_=sg_: e.activation(out=sg_, in_=psum[b1][:, :], func=AF.Silu),
                     reads=[r_ps[b1]], writes=[rsg])
                P.op("vector", lambda e, b2=b2, sg_=sg_, o=o: e.tensor_tensor(out=o, in0=psum[b2][:, :], in1=sg_,
                                                                              op=ALU.mult),
                     reads=[r_ps[b2], rsg], writes=[ro])
                P.dma("sync", lambda e, o=o, i=i, tok0=tok0: e.dma_start(out=act_d[i, :, tok0:tok0 + 512], in_=o),
                      reads=[ro], semres=ro)
            for _ in gen:
                pass

    def phase_ffn2(l):
        new_phase()
        wD = AB.take(22, D)
        r_wD = [P.res("wD0"), P.res("wD1")]
        for hq in range(2):
            cq = slice(hq * 512, (hq + 1) * 512)
            for c0 in range(0, 22, 6):
                c1 = min(22, c0 + 6)
                load_w(wD[:, c0:c1, cq], w_ffn_out[l, c0 * 128:c1 * 128, cq].rearrange("(c p) n -> p c n", p=128),
                       r_wD[hq])
        act = [AB.take(22, 512) for _ in range(2)]
        r_act = [P.res("act0"), P.res("act1")]
        xt = [AFa.take(D) for _ in range(2)]
        r_xt = [P.res("gxt0"), P.res("gxt1")]
        gg = AFa.take(D)
        r_gg = P.res("ggg")
        gtmp = AFa.take(D)
        r_gtmp = P.res("ggtmp")
        tmp = [AFa.take(D) for _ in range(2)]
        r_tmp = [P.res("gtmp2a"), P.res("gtmp2b")]
        xo = [AFa.take(D) for _ in range(2)]
        r_xo = [P.res("gxo0"), P.res("gxo1")]
        ss = [AFa.take(2) for _ in range(2)]
        r_ss = [P.res("gss0"), P.res("gss1")]
        for t in range(NT):
            tok0 = t * 512
            slot = t // 4
            if t % 4 == 0:
                load_rows(l, slot, None, [(gg, r_gg, "gate", "g_post_ffn", 5, gtmp, r_gtmp)])
            a, ra = act[t % 2], r_act[t % 2]

            def ld_act(tt):
                P.dma("sync", lambda e: e.dma_start(out=act[tt % 2], in_=fm_tile(act_d, 0, 22, tt * 512, 512)),
                      writes=[r_act[tt % 2]], semres=r_act[tt % 2])
            if t == 0:
                ld_act(0)
            if t + 1 < NT:
                ld_act(t + 1)
            out_proj_residual(t, a, ra, wD, r_wD, 22, yout, xt, r_xt, gg, r_gg, tmp, r_tmp, xo, r_xo, ss, r_ss,
                              ((0, 1), (2, 3), (4, 5), (6, 7)))

    phases = []
    phase_mod()
    done = (stop_after == "mod")
    for l in range(nl):
        if done:
            break
        xsrc = xin if l == 0 else yout
        for name, fn in (("a", lambda: phase_a(l, xsrc)), ("gates", lambda: phase_gates(l)),
                         ("conv_a", lambda: phase_conv_a(l)), ("conv_s", lambda: phase_conv_s(l)),
                         ("ssd", lambda: phase_ssd(l)), ("fnet", lambda: phase_fnet(l)),
                         ("merge", lambda: phase_merge(l, xsrc)), ("ffn1", lambda: phase_ffn1(l)),
                         ("ffn2", lambda: phase_ffn2(l))):
            fn()
            if stop_after == name:
                done = True
                break
    P.finalize()
    return nc, P


def _dft_tables(coupled):
    bf = ml_dtypes.bfloat16
    tab = np.zeros((5, 4, 2, 128, 16, 512), dtype=bf)
    n = np.arange(SEG, dtype=np.float64)
    k = np.arange(SEG, dtype=np.float64)

    def fill(bi, ang, scale):
        for cs, m in ((0, np.cos(ang) * scale), (1, -np.sin(ang) * scale)):
            m = m.reshape(16, 128, 4, 512).transpose(2, 1, 0, 3)
            tab[bi, :, cs] = m.astype(bf)

    if coupled:
        L = 2 * SEG
        sc = 1.0 / np.sqrt(L * 128.0)
        for bi, (os_, is_) in enumerate(((0, 0), (0, 1), (1, 0), (1, 1))):
            prod = np.outer(n + is_ * SEG, k + os_ * SEG) % L
            fill(bi, 2 * np.pi * prod / L, sc)
    else:
        L = SEG
        sc = 1.0 / np.sqrt(L * 128.0)
        ang = 2 * np.pi * (np.outer(n, k) % L) / L
        fill(0, ang, sc)
        tab[3] = tab[0]
    sc = 1.0 / np.sqrt(SEG * 128.0)
    ang = 2 * np.pi * (np.outer(n, k) % SEG) / SEG
    fill(4, ang, sc)
    return tab


def _host_inputs(inp):
    f32 = np.float32
    bf = ml_dtypes.bfloat16
    shared = {}
    for k in ("w_ada", "b_ada", "g_pre_mix", "g_post_mix", "g_pre_ffn", "g_post_ffn", "g_ssd", "w_in", "w_a_out",
              "w_b_out", "w_c_out", "w_out", "w_ffn_in", "w_ffn_out"):
        shared[k] = np.ascontiguousarray(np.asarray(inp[k], dtype=f32))
    shared["caw"] = np.ascontiguousarray(np.asarray(inp["conv_a_w"], f32).reshape(DEPTH, 31, 4, 128).transpose(0, 3, 2, 1))
    for nm, src in (("cab", "conv_a_b"), ("lng", "ln_a_g"), ("lnb", "ln_a_b")):
        shared[nm] = np.ascontiguousarray(np.asarray(inp[src], f32).reshape(DEPTH, 4, 128).transpose(0, 2, 1))
    shared["csw"] = np.ascontiguousarray(np.asarray(inp["conv_s_w"], f32).reshape(DEPTH, 5, 12, 128).transpose(0, 3, 2, 1))
    shared["csb"] = np.ascontiguousarray(np.asarray(inp["conv_s_b"], f32).reshape(DEPTH, 12, 128).transpose(0, 2, 1))
    shared["dtb"] = np.ascontiguousarray(np.concatenate([np.asarray(inp["dt_bias_f"], f32),
                                                         np.asarray(inp["dt_bias_b"], f32)], axis=1))
    shared["alog"] = np.ascontiguousarray(np.concatenate([np.asarray(inp["a_log_f"], f32),
                                                          np.asarray(inp["a_log_b"], f32)], axis=1))
    shared["dsk"] = np.ascontiguousarray(np.asarray(inp["d_skip"], f32))
    shared["ident"] = np.eye(128, dtype=f32).astype(bf)
    j = np.arange(128)[:, None]
    s = np.arange(128)[None, :]
    masks = np.stack([(j > s), (j < s), (j <= s), (j >= s), np.ones((128, 128), bool)], axis=1).astype(f32)
    shared["masks"] = np.ascontiguousarray(masks)
    c = np.arange(128, dtype=np.float64)
    ang = 2 * np.pi * np.outer(c, c) / 128.0
    shared["csc"] = np.concatenate([np.cos(ang), np.sin(ang)], axis=1).astype(bf)
    tabs = {True: _dft_tables(True), False: _dft_tables(False)}
    xp = np.asarray(inp["x_prompt"], f32)
    xs = np.asarray(inp["x_sample"], f32)
    cp = np.asarray(inp["c_prompt"], f32)
    csm = np.asarray(inp["c_sample"], f32)
    maps = []
    for core in range(8):
        if core < 4:
            xin = np.concatenate([xp[core], xs[core]], axis=0)
            cc = np.stack([cp[core], cp[core], csm[core]], axis=0)
            coupled = True
        else:
            ids = [4 + 3 * (core - 4) + q for q in range(3)]
            xin = np.concatenate([xs[q] for q in ids], axis=0)
            cc = np.stack([csm[q] for q in ids], axis=0)
            coupled = False
        m = dict(shared)
        m["xin"] = np.ascontiguousarray(xin)
        m["cT"] = np.ascontiguousarray(cc.reshape(3, 8, 128).transpose(2, 1, 0))
        m["flag"] = np.full((128, 1), 1.0 if coupled else 0.0, f32)
        m["tab"] = tabs[coupled]
        maps.append(m)
    return maps


_CACHE = {}


def kernel(**inputs):
    maps = _host_inputs(inputs)
    if "nc" not in _CACHE:
        _CACHE["nc"] = build_program()[0]
    nc = _CACHE["nc"]
    res = run_bass_kernel_spmd(nc, maps, core_ids=list(range(8)))
    outs = [np.asarray(r["yout"], dtype=np.float32) for r in res.results]
    y_prompt = np.stack([outs[c][:2 * SEG] for c in range(4)], axis=0)
    ys = [None] * 16
    for c in range(4):
        ys[c] = outs[c][2 * SEG:]
    for c in range(4, 8):
        for q in range(3):
            ys[4 + 3 * (c - 4) + q] = outs[c][q * SEG:(q + 1) * SEG]
    y_sample = np.stack(ys, axis=0)
    return (y_prompt, y_sample)
```
